# Optimizing a Trainium2 kernel written in Bass

```python
import jax, jax.numpy as jnp
from jax import lax
import numpy as np

D_MODEL = 1024
BATCH = 8
SEQ = 8192
DEPTH = 1

D_RNN = 1024
RNN_BLOCKS = 8
RNN_BLOCK_W = D_RNN // RNN_BLOCKS
CONV_W = 4
LRU_C = 8.0
N_HEADS = 8
HEAD_DIM = 128
D_ATTN = N_HEADS * HEAD_DIM
ROT_DIM = HEAD_DIM // 4
ROPE_THETA = 500000.0
DILATED_GROUPS = ((128, 1), (512, 4), (2048, 16))
NEG_INF = -1e30
IN_WIDTHS = (D_RNN, D_RNN, D_ATTN, D_ATTN, D_ATTN, D_ATTN, D_MODEL, D_MODEL)
D_IN = sum(IN_WIDTHS)
IN_SPLITS = [int(s) for s in np.cumsum(IN_WIDTHS)[:-1]]
NORM_EPS = 1e-6

kernel_name = "hybrid_rglru_dilated_attn_block"


def rms_norm(x, g):
    xf = x.astype(jnp.float32)
    y = xf * lax.rsqrt(jnp.mean(xf * xf, axis=-1, keepdims=True) + NORM_EPS)
    return (y * g.astype(jnp.float32)).astype(x.dtype)


def causal_depthwise_conv(x, w, b):
    y = lax.conv_general_dilated(
        x, w[:, None, :], window_strides=(1,), padding=[(CONV_W - 1, 0)],
        dimension_numbers=("NWC", "WIO", "NWC"), feature_group_count=x.shape[-1])
    return y + b


def rg_lru(x, w_a, b_a, w_x, b_x, lam, positions):
    B, S, _ = x.shape
    xf = x.astype(jnp.float32)
    xh = xf.reshape(B, S, RNN_BLOCKS, RNN_BLOCK_W)
    r = jax.nn.sigmoid(jnp.einsum("bshi,hij->bshj", xh, w_a.astype(jnp.float32)) + b_a.astype(jnp.float32))
    i = jax.nn.sigmoid(jnp.einsum("bshi,hij->bshj", xh, w_x.astype(jnp.float32)) + b_x.astype(jnp.float32))
    r = r.reshape(B, S, D_RNN)
    i = i.reshape(B, S, D_RNN)
    log_a = -LRU_C * r * jax.nn.softplus(-lam.astype(jnp.float32))
    reset = (positions == 0)[..., None]
    a = jnp.where(reset, 0.0, jnp.exp(log_a))
    mult = jnp.where(reset, 1.0, jnp.sqrt(-jnp.expm1(2.0 * log_a)))
    bx = mult * i * xf

    def combine(left, right):
        a1, b1 = left
        a2, b2 = right
        return a1 * a2, a2 * b1 + b2

    _, h = lax.associative_scan(combine, (a, bx), axis=1)
    return h


def apply_partial_rope(t, cos, sin):
    tf = t.astype(jnp.float32)
    half = ROT_DIM // 2
    x1 = tf[..., :half]
    x2 = tf[..., half:ROT_DIM]
    return jnp.concatenate([x1 * cos - x2 * sin, x2 * cos + x1 * sin, tf[..., ROT_DIM:]], axis=-1)


def dilated_window_attention(q, k, v, window, dilation):
    B, S, H, Dh = q.shape
    blk = window // dilation
    span = blk * dilation
    s_pad = -(-S // span) * span
    nb = s_pad // span

    def to_blocks(t):
        t = jnp.pad(t, ((0, 0), (0, s_pad - S), (0, 0), (0, 0)))
        return t.reshape(B, nb, blk, dilation, H, Dh)

    qb, kb, vb = to_blocks(q), to_blocks(k), to_blocks(v)
    pad_prev = ((0, 0), (1, 0), (0, 0), (0, 0), (0, 0), (0, 0))
    kk = jnp.concatenate([jnp.pad(kb[:, :-1], pad_prev), kb], axis=2)
    vv = jnp.concatenate([jnp.pad(vb[:, :-1], pad_prev), vb], axis=2)
    s = jnp.einsum("bnqrhd,bnkrhd->bnrhqk", qb, kk) * (HEAD_DIM ** -0.5)
    qi = jnp.arange(blk)[:, None]
    ki = jnp.arange(2 * blk)[None, :]
    dist = blk + qi - ki
    band = (dist >= 0) & (dist <= blk)
    n_idx = jnp.arange(nb)[:, None, None]
    valid = band[None] & ((n_idx > 0) | (ki[None] >= blk))
    s = jnp.where(valid[None, :, None, None], s, NEG_INF)
    lse = jax.nn.logsumexp(s, axis=-1)
    p = jnp.exp(s - lse[..., None])
    o = jnp.einsum("bnrhqk,bnkrhd->bnqrhd", p, vv)
    o = o.reshape(B, s_pad, H, Dh)[:, :S]
    lse = lse.transpose(0, 1, 4, 2, 3).reshape(B, s_pad, H)[:, :S]
    return o, lse


def dilated_attention_mixture(q, k, v):
    outs, lses = [], []
    for window, dilation in DILATED_GROUPS:
        o, l = dilated_window_attention(q, k, v, window, dilation)
        outs.append(o)
        lses.append(l)
    w = jax.nn.softmax(jnp.stack(lses, axis=0), axis=0)
    return jnp.einsum("gbsh,gbshd->bshd", w, jnp.stack(outs, axis=0))


def setup_inputs(seed: int = 0) -> dict:
    key = jax.random.key(seed)
    ks = jax.random.split(key, 20)
    f32 = jnp.float32
    nrm = lambda k, shape, scale: jax.random.normal(k, shape, f32) * scale
    x = jax.random.normal(ks[0], (BATCH, SEQ, D_MODEL), f32)
    c = jax.random.normal(ks[1], (BATCH, D_MODEL), f32)
    positions = jnp.broadcast_to(jnp.arange(SEQ, dtype=jnp.int32)[None, :], (BATCH, SEQ))
    g_norm = 1.0 + nrm(ks[2], (DEPTH, D_MODEL), 0.02)
    w_mod = nrm(ks[3], (DEPTH, D_MODEL, 3 * D_MODEL), 0.5 * D_MODEL ** -0.5)
    b_mod = nrm(ks[4], (DEPTH, 3 * D_MODEL), 0.01)
    w_in = nrm(ks[5], (DEPTH, D_MODEL, D_IN), D_MODEL ** -0.5)
    b_gate = nrm(ks[6], (DEPTH, 2 * D_MODEL), 0.01)
    conv_w = nrm(ks[7], (DEPTH, CONV_W, D_RNN), CONV_W ** -0.5)
    conv_b = nrm(ks[8], (DEPTH, D_RNN), 0.01)
    w_a = nrm(ks[9], (DEPTH, RNN_BLOCKS, RNN_BLOCK_W, RNN_BLOCK_W), RNN_BLOCK_W ** -0.5)
    b_a = nrm(ks[10], (DEPTH, RNN_BLOCKS, RNN_BLOCK_W), 0.01)
    w_x = nrm(ks[11], (DEPTH, RNN_BLOCKS, RNN_BLOCK_W, RNN_BLOCK_W), RNN_BLOCK_W ** -0.5)
    b_x = nrm(ks[12], (DEPTH, RNN_BLOCKS, RNN_BLOCK_W), 0.01)
    a0 = jax.random.uniform(ks[13], (DEPTH, D_RNN), f32, 0.9, 0.999)
    sig = a0 ** (1.0 / LRU_C)
    lam = jnp.log(sig) - jnp.log1p(-sig)
    w_out_rnn = nrm(ks[14], (DEPTH, D_RNN, D_MODEL), D_RNN ** -0.5)
    w_out_attn = nrm(ks[15], (DEPTH, D_ATTN, D_MODEL), D_ATTN ** -0.5)
    w_o = nrm(ks[16], (DEPTH, D_MODEL, D_MODEL), D_MODEL ** -0.5)
    g_final = 1.0 + nrm(ks[17], (D_MODEL,), 0.02)
    return {"x": x, "c": c, "positions": positions, "g_norm": g_norm, "w_mod": w_mod,
            "b_mod": b_mod, "w_in": w_in, "b_gate": b_gate, "conv_w": conv_w, "conv_b": conv_b,
            "w_a": w_a, "b_a": b_a, "w_x": w_x, "b_x": b_x, "lam": lam, "w_out_rnn": w_out_rnn,
            "w_out_attn": w_out_attn, "w_o": w_o, "g_final": g_final}


def reference(x, c, positions, g_norm, w_mod, b_mod, w_in, b_gate, conv_w, conv_b,
              w_a, b_a, w_x, b_x, lam, w_out_rnn, w_out_attn, w_o, g_final):
    B, S, _ = x.shape
    dt = x.dtype
    inv_freq = ROPE_THETA ** (-jnp.arange(0, ROT_DIM, 2, dtype=jnp.float32) / ROT_DIM)
    ang = positions.astype(jnp.float32)[..., None] * inv_freq
    cos = jnp.cos(ang)[:, :, None, :]
    sin = jnp.sin(ang)[:, :, None, :]
    c_act = jax.nn.silu(c)
    for l in range(DEPTH):
        mod = c_act @ w_mod[l] + b_mod[l]
        shift, scale, gate = jnp.split(mod, 3, axis=-1)
        h = rms_norm(x, g_norm[l]) * (1.0 + scale[:, None, :]) + shift[:, None, :]
        proj = h @ w_in[l]
        x_rnn, z_rnn, q, k, v, z_attn, g_r, g_a = jnp.split(proj, IN_SPLITS, axis=-1)
        xc = causal_depthwise_conv(x_rnn, conv_w[l], conv_b[l])
        hr = rg_lru(xc, w_a[l], b_a[l], w_x[l], b_x[l], lam[l], positions)
        y_rnn = (hr * jax.nn.silu(z_rnn.astype(jnp.float32))).astype(dt) @ w_out_rnn[l]
        qh = apply_partial_rope(q.reshape(B, S, N_HEADS, HEAD_DIM), cos, sin)
        kh = apply_partial_rope(k.reshape(B, S, N_HEADS, HEAD_DIM), cos, sin)
        vh = v.reshape(B, S, N_HEADS, HEAD_DIM).astype(jnp.float32)
        o = dilated_attention_mixture(qh, kh, vh).reshape(B, S, D_ATTN)
        y_attn = (o * jax.nn.silu(z_attn.astype(jnp.float32))).astype(dt) @ w_out_attn[l]
        bg_r, bg_a = jnp.split(b_gate[l], 2, axis=-1)
        merged = jax.nn.sigmoid(g_r + bg_r) * y_rnn + jax.nn.sigmoid(g_a + bg_a) * y_attn
        x = x + gate[:, None, :] * (merged @ w_o[l])
    return rms_norm(x, g_final)
```

```python
import numpy as np
import concourse.bass as bass
import concourse.mybir as mybir
from concourse.bass_utils import run_bass_kernel_spmd

F32 = mybir.dt.float32
BF16 = mybir.dt.bfloat16
I32 = mybir.dt.int32
AF = mybir.ActivationFunctionType
ALU = mybir.AluOpType

D = 1024
SEQ = 8192
NB = 8
TWO_PI = 6.283185307179586
C1 = 6.28125
C2 = TWO_PI - C1
ENGS = ("pe", "act", "dve", "pool", "sp")


class DSem:
    def __init__(self, sem, name):
        self.sem = sem
        self.count = 0
        self.name = name


class Sched:
    def __init__(self, nc, esems):
        self.nc = nc
        self.ops = {e: [] for e in ENGS}
        self.cnt = {e: 0 for e in ENGS}
        self.pending = {e: False for e in ENGS}
        self.esem = esems
        self.known = {e: {} for e in ENGS}
        self.lw = {}
        self.rd = {}
        self.dsems = []

    def dsem(self, sem, name):
        d = DSem(sem, name)
        self.dsems.append(d)
        return d

    def _tok_val(self, tok):
        if tok[0] == "eng":
            return ("e", tok[1]), self.esem[tok[1]], tok[2]
        d = tok[1]
        return ("d", d.name), d.sem, d.count

    def _deps(self, e, reads, writes):
        toks = []
        for k in reads:
            t = self.lw.get(k)
            if t is not None:
                toks.append(t)
        for k in writes:
            t = self.lw.get(k)
            if t is not None:
                toks.append(t)
            toks.extend(self.rd.get(k, ()))
        need = {}
        for t in toks:
            if t[0] == "eng" and t[1] == e and e == "pe":
                continue
            key, sem, val = self._tok_val(t)
            if val <= 0:
                continue
            if need.get(key, (None, 0))[1] < val:
                need[key] = (sem, val)
        for key, (sem, val) in need.items():
            if self.known[e].get(key, 0) >= val:
                continue
            self.known[e][key] = val
            if key == ("e", e) and val > self.cnt[e]:
                raise RuntimeError("same-engine dep on un-incremented instruction (%s)" % e)
            self.ops[e].append(lambda eng, sem=sem, val=val: eng.wait_ge(sem, val))

    def _commit(self, tok, reads, writes):
        for k in writes:
            self.lw[k] = tok
            self.rd[k] = []
        for k in reads:
            self.rd.setdefault(k, []).append(tok)

    def op(self, e, fn, reads=(), writes=(), inc=True):
        self._deps(e, reads, writes)
        sem = self.esem[e]
        if inc:
            self.cnt[e] += 1
            self.pending[e] = False
            self.ops[e].append(lambda eng, fn=fn, sem=sem: fn(eng).then_inc(sem, 1))
            tok = ("eng", e, self.cnt[e])
        else:
            self.pending[e] = True
            self.ops[e].append(lambda eng, fn=fn: fn(eng))
            tok = ("eng", e, self.cnt[e] + 1)
        self._commit(tok, reads, writes)

    def dma(self, q, ds, fn, reads=(), writes=()):
        self._deps(q, reads, writes)
        ds.count += 16
        self.ops[q].append(lambda eng, fn=fn, sem=ds.sem: fn(eng).then_inc(sem, 16))
        self._commit(("dma", ds), reads, writes)

    def barrier(self):
        for e in ENGS:
            for e2 in ENGS:
                if e2 == e:
                    continue
                assert not self.pending[e2]
                val = self.cnt[e2]
                key = ("e", e2)
                if val > self.known[e].get(key, 0):
                    self.known[e][key] = val
                    self.ops[e].append(lambda eng, sem=self.esem[e2], val=val: eng.wait_ge(sem, val))
            for d in self.dsems:
                key = ("d", d.name)
                if d.count > self.known[e].get(key, 0):
                    self.known[e][key] = d.count
                    self.ops[e].append(lambda eng, sem=d.sem, val=d.count: eng.wait_ge(sem, val))

    def final_wait(self, e="sp"):
        for d in self.dsems:
            key = ("d", d.name)
            if d.count > self.known[e].get(key, 0):
                self.known[e][key] = d.count
                self.ops[e].append(lambda eng, sem=d.sem, val=d.count: eng.wait_ge(sem, val))

    def emit(self, block):
        ops = self.ops

        @block.tensor
        def _(eng):
            for f in ops["pe"]:
                f(eng)

        @block.scalar
        def _(eng):
            for f in ops["act"]:
                f(eng)

        @block.vector
        def _(eng):
            for f in ops["dve"]:
                f(eng)

        @block.gpsimd
        def _(eng):
            for f in ops["pool"]:
                f(eng)

        @block.sync
        def _(eng):
            for f in ops["sp"]:
                f(eng)


class View:
    def __init__(self, tensor, row, off, shape):
        self.t = tensor
        self.row = row
        self.off = off
        self.shape = tuple(shape)
        st = []
        s = 1
        for d in reversed(self.shape[1:]):
            st.append(s)
            s *= d
        self.strides = tuple(reversed(st))
        self.size = s

    def ap(self, p0=0, pn=None, dims=None, off=0):
        if pn is None:
            pn = self.shape[0] - p0
        if dims is None:
            dims = [(1, self.size)]
        pat = [[self.row, pn]] + [[s, n] for (s, n) in dims]
        return bass.AP(self.t, p0 * self.row + self.off + off, pat)

    def __getitem__(self, key):
        if not isinstance(key, tuple):
            key = (key,)
        key = key + (slice(None),) * (len(self.shape) - len(key))
        ps = key[0]
        if isinstance(ps, int):
            p0, pn = ps, 1
        else:
            p0 = ps.start or 0
            pn = (ps.stop if ps.stop is not None else self.shape[0]) - p0
        off = 0
        dims = []
        for k, d, st in zip(key[1:], self.shape[1:], self.strides):
            if isinstance(k, int):
                off += k * st
            else:
                a = k.start or 0
                b = k.stop if k.stop is not None else d
                step = k.step or 1
                n = (b - a + step - 1) // step
                off += a * st
                dims.append((st * step, n))
        merged = []
        for s, n in dims:
            if merged and merged[-1][0] == s * n:
                merged[-1] = (s, merged[-1][1] * n)
            else:
                merged.append((s, n))
        if not merged:
            merged = [(1, 1)]
        return self.ap(p0, pn, dims=merged, off=off)


class Arena:
    def __init__(self, t32, words):
        self.t32 = t32
        self.words = words
        self.views = {}
        self.pos = 0
        self.hi = 0

    def view(self, dtype):
        if dtype not in self.views:
            if dtype == F32:
                self.views[dtype] = (self.t32, self.words, 1)
            else:
                r = 4 // mybir.dt.size(dtype)
                self.views[dtype] = (self.t32.bitcast(dtype), self.words * r, r)
        return self.views[dtype]

    def reset(self, pos):
        self.pos = pos

    def alloc(self, shape, dtype):
        t, row, r = self.view(dtype)
        n = 1
        for d in shape[1:]:
            n *= d
        words = (n + r - 1) // r
        words = (words + 7) // 8 * 8
        off = self.pos
        self.pos += words
        self.hi = max(self.hi, self.pos)
        assert self.pos <= self.words, "arena overflow %d > %d" % (self.pos, self.words)
        return View(t, row, off * r, shape)


def DA(t, off, pat):
    return bass.AP(t, off, [list(p) for p in pat])


SV_BMOD, SV_GN, SV_CW, SV_CB, SV_BA, SV_BX, SV_LAM, SV_BG, SV_FREQ, SV_SIGN = 0, 24, 32, 64, 72, 80, 88, 96, 112, 113
NSV = 120
NWB = 4608
ARENA_WORDS = 52224


def build(S, debug=False):
    NSC = S // 512
    nc = bass.Bass("TRN2", target_bir_lowering=False)
    kin = "ExternalInput"
    x_d = nc.dram_tensor("x", [S, D], F32, kind=kin)
    pos_d = nc.dram_tensor("pos", [1, S], I32, kind=kin)
    cT_d = nc.dram_tensor("cT", [128, 8], F32, kind=kin)
    wmod_d = nc.dram_tensor("w_mod", [D, 3 * D], F32, kind=kin)
    sv_d = nc.dram_tensor("sv", [128, NSV], F32, kind=kin)
    wa_d = nc.dram_tensor("WA", [D, 2048], F32, kind=kin)
    wb_d = nc.dram_tensor("WB", [D, NWB], F32, kind=kin)
    wg_d = nc.dram_tensor("WG", [D, 2048], F32, kind=kin)
    wor_d = nc.dram_tensor("wor", [D, D], F32, kind=kin)
    woa_d = nc.dram_tensor("woa", [D, D], F32, kind=kin)
    wo_d = nc.dram_tensor("wo", [D, D], F32, kind=kin)
    wax_d = nc.dram_tensor("wax", [128, 16 * 128], F32, kind=kin)
    gfin_d = nc.dram_tensor("gfin", [1, D], F32, kind=kin)
    ident_d = nc.dram_tensor("ident", [128, 128], F32, kind=kin)
    masks_d = nc.dram_tensor("masks", [128, 12 * 512], F32, kind=kin)
    skind = "ExternalOutput" if debug else "Internal"
    mod_d = nc.dram_tensor("MODS", [24, 128], F32, kind=skind)
    ht_d = nc.dram_tensor("HT", [D, S], BF16, kind=skind)
    rt_d = nc.dram_tensor("RT", [D, S], BF16, kind=skind)
    qt_d = nc.dram_tensor("QT", [D, S], BF16, kind=skind)
    kt_d = nc.dram_tensor("KT", [D, S], BF16, kind=skind)
    vt_d = nc.dram_tensor("VT", [D, S], BF16, kind=skind)
    zd_d = nc.dram_tensor("ZD", [D, S], F32, kind=skind)
    at_d = nc.dram_tensor("AT", [D, S], BF16, kind=skind)
    out_d = nc.dram_tensor("out", [S, D], F32, kind="ExternalOutput")

    import contextlib
    with contextlib.ExitStack() as es:
        ar_t = es.enter_context(nc.sbuf_tensor("arena", [128, ARENA_WORDS], F32))
        psb = [es.enter_context(nc.psum_tensor("ps%d" % i, [128, 512], F32)) for i in range(8)]
        esems = {e: es.enter_context(nc.semaphore("s_" + e)) for e in ENGS}
        SCH = Sched(nc, esems)
        _dn = [0]

        def newds(name):
            _dn[0] += 1
            return SCH.dsem(es.enter_context(nc.semaphore("d%d_%s" % (_dn[0], name))), "%d_%s" % (_dn[0], name))

        A = Arena(ar_t, ARENA_WORDS)
        PS = [View(p, 512, 0, [128, 512]) for p in psb]
        PSB = [View(p.bitcast(BF16), 1024, 0, [128, 1024]) for p in psb]
        op, dma = SCH.op, SCH.dma

        sv = A.alloc([128, NSV], F32)
        identf = A.alloc([128, 128], F32)
        identb = A.alloc([128, 128], BF16)
        onesb = A.alloc([128, 128], BF16)
        masks = A.alloc([128, 12, 512], BF16)
        MT = A.alloc([128, 24], F32)
        G3 = A.alloc([128, 24], F32)
        hc = A.alloc([128, 8], F32)
        hba = A.alloc([128, 8], F32)
        hbx = A.alloc([128, 8], F32)
        mhalf = A.alloc([128, 8], F32)
        BC0 = A.alloc([128, D], F32)
        BC1 = A.alloc([128, D], F32)
        base = A.pos
        WAs = A.alloc([128, 8, 2048], BF16)
        WX = A.alloc([128, 16, 128], BF16)
        base1a = A.pos

        ds_c = newds("const")
        ds_c2 = newds("const2")
        ds_w = newds("w")
        ds_w2 = newds("w2")
        dma("sp", ds_c, lambda e: e.dma_start(out=sv[:, :], in_=sv_d.ap()), writes=["sv"])
        dma("sp", ds_c, lambda e: e.dma_start(out=identf[:, :], in_=ident_d.ap()), writes=["identf"])
        dma("pool", ds_c2, lambda e: e.dma_start(out=identb[:, :], in_=ident_d.ap()), writes=["identb"])
        for i in range(12):
            dma("pool", ds_c2, lambda e, i=i: e.dma_start(out=masks[:, i, :], in_=DA(masks_d, i * 512, [[12 * 512, 128], [1, 512]])), writes=["masks"])
        op("pool", lambda e: e.memset(onesb[:, :], 1.0), writes=["onesb"])
        op("pool", lambda e: e.memset(mhalf[:, :], -0.5), writes=["mhalf"])
        def load_w(dst, src_d, W, col0, ncols, ds, key, chunk=2048, keyf=None, dsf=None, extra=()):
            c = 0
            while c < ncols:
                n = min(chunk, ncols - c)
                k_ = keyf(c) if keyf else key
                d_ = dsf(c) if dsf else ds
                dma("pool", d_, lambda e, c=c, n=n: e.dma_start(
                    out=dst[:, :, c:c + n], in_=DA(src_d, col0 + c, [[W, 128], [128 * W, 8], [1, n]])), writes=[k_] + list(extra))
                c += n

        load_w(WAs, wa_d, 2048, 0, 2048, ds_w2, "WAs", chunk=1024)
        dma("pool", ds_w2, lambda e: e.dma_start(out=WX[:, :, :], in_=wax_d.ap()), writes=["WX"])

        A.reset(base1a)
        cT = A.alloc([128, 8], F32)
        cact = A.alloc([128, 8], F32)
        wm = A.alloc([128, 8, 3 * D], F32)
        tmp24 = A.alloc([128, 128], F32)
        dma("sp", ds_c, lambda e: e.dma_start(out=cT[:, :], in_=cT_d.ap()), writes=["cT"])
        for kc in range(8):
            dma("sp", ds_w, lambda e, kc=kc: e.dma_start(out=wm[:, kc, :], in_=DA(wmod_d, kc * 128 * 3 * D, [[3 * D, 128], [1, 3 * D]])), writes=["wm"])
        op("act", lambda e: e.activation(cact[:, :], cT[:, :], AF.Silu), reads=["cT"], writes=["cact"])
        for ft in range(24):
            for kc in range(8):
                op("pe", lambda e, ft=ft, kc=kc: e.matmul(PS[0][:, ft:ft + 1], wm[:, kc, ft * 128:(ft + 1) * 128], cact[:, kc:kc + 1],
                                                      start=(kc == 0), stop=(kc == 7)),
                   reads=["wm", "cact"], writes=["ps0"], inc=(ft == 23 and kc == 7))
        op("dve", lambda e: e.tensor_tensor(MT[:, :], PS[0][:, 0:24], sv[:, SV_BMOD:SV_BMOD + 24], ALU.add), reads=["ps0", "sv"], writes=["MT"])
        op("dve", lambda e: e.scalar_tensor_tensor(G3[:, 0:8], MT[:, 8:16], 1.0, sv[:, SV_GN:SV_GN + 8], ALU.add, ALU.mult), reads=["MT", "sv"], writes=["G3a"])
        op("dve", lambda e: e.tensor_copy(G3[:, 8:16], MT[:, 0:8]), reads=["MT"], writes=["G3b"])
        op("dve", lambda e: e.tensor_copy(G3[:, 16:24], MT[:, 16:24]), reads=["MT"], writes=["G3c"])
        op("pe", lambda e: e.transpose(PS[1][0:24, 0:128], G3[:, 0:24], identf[:, :]), reads=["G3a", "G3b", "G3c", "identf"], writes=["ps1"])
        op("dve", lambda e: e.tensor_copy(tmp24[0:24, :], PS[1][0:24, 0:128]), reads=["ps1"], writes=["tmp24"])
        dma("sp", ds_c, lambda e: e.dma_start(out=mod_d.ap(), in_=tmp24[0:24, :]), reads=["tmp24"], writes=["MODd"])
        dma("sp", ds_c, lambda e: e.dma_start(out=BC0[:, :], in_=DA(mod_d, 0, [[0, 128], [1, D]])), reads=["MODd"], writes=["BC0"])
        dma("sp", ds_c, lambda e: e.dma_start(out=BC1[:, :], in_=DA(mod_d, D, [[0, 128], [1, D]])), reads=["MODd"], writes=["BC1"])
        op("act", lambda e: e.activation(hc[:, :], sv[:, SV_LAM:SV_LAM + 8], AF.Exp, scale=-1.0), reads=["sv"], writes=["hc"])
        op("act", lambda e: e.activation(hc[:, :], hc[:, :], AF.Ln, bias=1.0), reads=["hc"], writes=["hc"])
        op("dve", lambda e: e.tensor_scalar(hc[:, :], hc[:, :], -4.0, None, ALU.mult), reads=["hc"], writes=["hc"])
        op("dve", lambda e: e.tensor_scalar(hba[:, :], sv[:, SV_BA:SV_BA + 8], 0.5, None, ALU.mult), reads=["sv"], writes=["hba"])
        op("dve", lambda e: e.tensor_scalar(hbx[:, :], sv[:, SV_BX:SV_BX + 8], 0.5, None, ALU.mult), reads=["sv"], writes=["hbx"])
        SCH.barrier()

        A.reset(base1a)
        xs = [A.alloc([128, D], F32) for _ in range(4)]
        sqs = A.alloc([128, D], BF16)
        ssq = [A.alloc([128, 4], F32) for _ in range(2)]
        rstd = [A.alloc([128, 4], F32) for _ in range(2)]
        tn = [A.alloc([128, D], F32) for _ in range(2)]
        hb = A.alloc([128, 4, D], BF16)
        hT = [A.alloc([128, 8, 512], BF16) for _ in range(2)]
        XR = A.alloc([128, 8, 515], F32)
        posi = A.alloc([128, 512], I32)
        nm0 = [A.alloc([128, 512], F32) for _ in range(3)]
        hst = A.alloc([128, 8], F32)
        NZU, NXC, NXB, NTR, NHH = 7, 6, 3, 5, 3
        ZU = [A.alloc([128, 512], F32) for _ in range(NZU)]
        XC = [A.alloc([128, 512], F32) for _ in range(NXC)]
        XCb = [A.alloc([128, 512], BF16) for _ in range(NXB)]
        TR = [A.alloc([128, 512], F32) for _ in range(NTR)]
        TI = [A.alloc([128, 512], F32) for _ in range(NTR)]
        NA2 = 4
        A2 = [A.alloc([128, 512], F32) for _ in range(NA2)]
        HH = [A.alloc([128, 512], F32) for _ in range(NHH)]
        RO = [A.alloc([128, 4, 512], BF16) for _ in range(2)]
        ds_x = [newds("x%d" % i) for i in range(4)]
        ds_ht = [newds("ht%d" % i) for i in range(2)]
        ds_ro = [newds("ro%d" % i) for i in range(2)]
        ds_pos = newds("pos")
        op("dve", lambda e: e.memset(XR[:, :, 0:3], 0.0), writes=["XRc%d" % cb for cb in range(8)])
        op("dve", lambda e: e.memset(hst[:, :], 0.0), writes=["hst%d" % cb for cb in range(8)])

        def pbank(lo, n, ctr):
            b = lo + ctr[0] % n
            ctr[0] += 1
            return b
        ctr_p, ctr_t, ctr_g = [0], [0], [0]

        def norm_piece(s, piece):
            t0 = s * 512
            hTs, hkey = hT[s % 2], "hT%d" % (s % 2)
            nm, nmk = nm0[s % 3], "nm0_%d" % (s % 3)
            sq_, rs_, sk = ssq[s % 2], rstd[s % 2], "%d" % (s % 2)
            if piece == 0:
                dma("sp", ds_pos, lambda e: e.dma_start(out=posi[:, :], in_=DA(pos_d, t0, [[0, 128], [1, 512]])), writes=["posi"])
                for tt in range(4):
                    j = s * 4 + tt
                    xsl, xk = xs[j % 4], "x%d" % (j % 4)
                    dma("sp", ds_x[j % 4], lambda e, xsl=xsl, tt=tt: e.dma_start(out=xsl[:, :], in_=DA(x_d, (t0 + 128 * tt) * D, [[D, 128], [1, D]])), writes=[xk])
            elif piece == 1:
                op("dve", lambda e: e.tensor_scalar(nm[:, :], posi[:, :], 0.0, None, ALU.not_equal), reads=["posi"], writes=[nmk])
                for tt in range(4):
                    j = s * 4 + tt
                    xsl, xk = xs[j % 4], "x%d" % (j % 4)
                    op("act", lambda e, xsl=xsl, tt=tt: e.activation(sqs[:, :], xsl[:, :], AF.Square, accum_out=sq_[:, tt:tt + 1]), reads=[xk], writes=["sqs", "ssq" + sk + "_%d" % tt])
            elif piece == 2:
                op("dve", lambda e: e.tensor_scalar(rs_[:, :], sq_[:, :], 1.0 / D, 1e-6, ALU.mult, ALU.add), reads=["ssq" + sk + "_%d" % i for i in range(4)], writes=["rstd" + sk])
                op("pool", lambda e: e.tensor_tensor(rs_[:, :], rs_[:, :], mhalf[:, 0:4], ALU.pow), reads=["rstd" + sk, "mhalf"], writes=["rstd" + sk])
            elif piece == 3:
                for tt in range(4):
                    j = s * 4 + tt
                    xsl, xk = xs[j % 4], "x%d" % (j % 4)
                    tns, tk = tn[tt % 2], "tn%d" % (tt % 2)
                    op("dve", lambda e, xsl=xsl, tns=tns, tt=tt: e.scalar_tensor_tensor(tns[:, :], xsl[:, :], rs_[:, tt:tt + 1], BC0[:, :], ALU.mult, ALU.mult),
                       reads=[xk, "rstd" + sk, "BC0"], writes=[tk])
                    op("pool", lambda e, tns=tns, tt=tt: e.tensor_tensor(hb[:, tt, :], tns[:, :], BC1[:, :], ALU.add), reads=[tk, "BC1"], writes=["hb%d" % tt])
            elif piece in (4, 5, 6):
                if piece in (5, 6):
                    for kc in range(4 * (piece - 5), 4 * (piece - 5) + 4):
                        b = kc % 4 // 2
                        o = (kc % 2) * 512
                        if kc % 4 != 3:
                            op("act", lambda e, b=b, o=o, kc=kc: e.activation(hTs[:, kc, :], PSB[b][:, o:o + 512], AF.Copy), reads=["ps%d" % b], writes=[hkey])
                        else:
                            op("dve", lambda e, b=b, o=o, kc=kc: e.tensor_copy(hTs[:, kc, :], PSB[b][:, o:o + 512]), reads=["ps%d" % b], writes=[hkey])
                if piece in (4, 5):
                    for kc in range(4 * (piece - 4), 4 * (piece - 4) + 4):
                        b = kc % 4 // 2
                        o = (kc % 2) * 512
                        for tt in range(4):
                            op("pe", lambda e, b=b, o=o, kc=kc, tt=tt: e.transpose(PSB[b][:, o + tt * 128:o + (tt + 1) * 128], hb[:, tt, kc * 128:(kc + 1) * 128], identb[:, :]),
                               reads=["hb%d" % tt, "identb"], writes=["ps%d" % b], inc=(tt == 3))
                if piece == 6:
                    dma("sp", ds_ht[s % 2], lambda e: e.dma_start(out=DA(ht_d, t0, [[S, 128], [128 * S, 8], [1, 512]]), in_=hTs[:, :, :]), reads=[hkey], writes=["HTd"])

        NIT = NSC * 8
        bx_of, bz_of, bgr_of, bgi_of = {}, {}, {}, {}

        def names(n):
            s, cb = n // 8, n % 8
            return dict(s=s, cb=cb, zu=ZU[n % NZU], zuk="ZU%d" % (n % NZU), xc=XC[n % NXC], xck="XC%d" % (n % NXC), xb=XCb[n % NXB], xbk="XCb%d" % (n % NXB),
                        tr=TR[n % NTR], trk="TR%d" % (n % NTR), ti=TI[n % NTR], tik="TI%d" % (n % NTR), a2=A2[n % NA2], a2k="A2%d" % (n % NA2),
                        hh=HH[n % NHH], hhk="HH%d" % (n % NHH))

        def pe_proj(n):
            s, cb = n // 8, n % 8
            hTs, hkey = hT[s % 2], "hT%d" % (s % 2)
            for which, store in ((0, bx_of), (1, bz_of)):
                b = pbank(2, 3, ctr_p)
                store[n] = b
                for kc in range(8):
                    op("pe", lambda e, b=b, kc=kc, which=which: e.matmul(PS[b][:, :], WAs[:, kc, which * 1024 + cb * 128:which * 1024 + (cb + 1) * 128], hTs[:, kc, :], start=(kc == 0), stop=(kc == 7)),
                       reads=[hkey, "WAs"], writes=["ps%d" % b], inc=(kc == 7))

        def pe_gates(n):
            d = names(n)
            cb = d["cb"]
            for which, store in ((0, bgr_of), (1, bgi_of)):
                b = pbank(5, 3, ctr_g)
                store[n] = b
                op("pe", lambda e, b=b, which=which: e.matmul(PS[b][:, :], WX[:, which * 8 + cb, :], d["xb"][:, :], start=True, stop=True), reads=[d["xbk"], "WX"], writes=["ps%d" % b])

        NIT = NSC * 8
        for pc in range(8):
            norm_piece(0, pc)
        for t in range(NIT + 7):
            if t < NIT and (t // 8 + 1) < NSC:
                norm_piece(t // 8 + 1, t % 8)
            if 0 <= t - 2 < NIT:
                pe_gates(t - 2)
            if t < NIT:
                pe_proj(t)
            if 0 <= t - 1 < NIT:
                d = names(t - 1); cb = d["cb"]; cw = SV_CW + cb * 4
                op("act", lambda e, d=d, cb=cb, cw=cw: e.activation(d["xc"][:, :], XR[:, cb, 3:515], AF.Identity, scale=sv[:, cw + 3:cw + 4], bias=sv[:, SV_CB + cb:SV_CB + cb + 1]),
                   reads=["XR%d" % cb, "sv"], writes=[d["xck"]])
                for k in range(3):
                    op("dve", lambda e, d=d, cb=cb, cw=cw, k=k: e.scalar_tensor_tensor(d["xc"][:, :], XR[:, cb, k:k + 512], sv[:, cw + k:cw + k + 1], d["xc"][:, :], ALU.mult, ALU.add),
                       reads=["XR%d" % cb, "XRc%d" % cb, d["xck"], "sv"], writes=[d["xck"]])
                op("dve", lambda e, d=d: e.tensor_copy(d["xb"][:, :], d["xc"][:, :]), reads=[d["xck"]], writes=[d["xbk"]])
            if 0 <= t - 3 < NIT:
                d = names(t - 3); cb = d["cb"]
                nm, nmk = nm0[d["s"] % 3], "nm0_%d" % (d["s"] % 3)
                op("act", lambda e, d=d, cb=cb: e.activation(d["tr"][:, :], d["tr"][:, :], AF.Exp, scale=hc[:, cb:cb + 1], bias=hc[:, cb:cb + 1]), reads=[d["trk"], "hc"], writes=[d["trk"]])
                op("pool", lambda e, d=d, nm=nm: e.tensor_tensor(d["tr"][:, :], d["tr"][:, :], nm[:, :], ALU.mult), reads=[d["trk"], nmk], writes=[d["trk"]])
                op("pool", lambda e, d=d: e.tensor_tensor(d["a2"][:, :], d["tr"][:, :], d["tr"][:, :], ALU.mult), reads=[d["trk"]], writes=[d["a2k"]])
            if 0 <= t - 2 < NIT:
                d = names(t - 2); cb = d["cb"]
                br, bi_ = bgr_of.pop(t - 2), bgi_of.pop(t - 2)
                op("act", lambda e, d=d, br=br, cb=cb: e.activation(d["tr"][:, :], PS[br][:, :], AF.Tanh, scale=0.5, bias=hba[:, cb:cb + 1]), reads=["ps%d" % br, "hba"], writes=[d["trk"]])
                op("act", lambda e, d=d, bi_=bi_, cb=cb: e.activation(d["ti"][:, :], PS[bi_][:, :], AF.Tanh, scale=0.5, bias=hbx[:, cb:cb + 1]), reads=["ps%d" % bi_, "hbx"], writes=[d["tik"]])
            if t < NIT:
                d = names(t); cb = d["cb"]
                bx, bz = bx_of.pop(t), bz_of.pop(t)
                op("act", lambda e, bx=bx, cb=cb: e.activation(XR[:, cb, 3:515], PS[bx][:, :], AF.Copy), reads=["ps%d" % bx], writes=["XR%d" % cb])
                op("act", lambda e, d=d, bz=bz: e.activation(d["zu"][:, :], PS[bz][:, :], AF.Tanh, scale=0.5), reads=["ps%d" % bz], writes=[d["zuk"]])
                op("dve", lambda e, d=d, bz=bz: e.scalar_tensor_tensor(d["zu"][:, :], d["zu"][:, :], 1.0, PS[bz][:, :], ALU.add, ALU.mult), reads=[d["zuk"], "ps%d" % bz], writes=[d["zuk"]])
            if 0 <= t - 1 < NIT:
                d = names(t - 1); cb = d["cb"]
                op("pool", lambda e, cb=cb: e.tensor_copy(XR[:, cb, 0:3], XR[:, cb, 512:515]), reads=["XR%d" % cb, "XRc%d" % cb], writes=["XRc%d" % cb])
            m = t - 3
            if 0 <= m < NIT and m % 4 == 3:
                for q_ in range(m - 3, m + 1):
                    d = names(q_)
                    op("act", lambda e, d=d: e.activation(d["a2"][:, :], d["a2"][:, :], AF.Sqrt, scale=-0.0625, bias=0.0625), reads=[d["a2k"]], writes=[d["a2k"]])
            for q_ in ([t - 6] if 0 <= t - 6 < NIT else []):
                if True:
                    d = names(q_); cb = d["cb"]; s_ = d["s"]
                    half, g = cb // 4, cb % 4
                    ro, rok = RO[half], "RO%d" % half
                    op("dve", lambda e, d=d: e.scalar_tensor_tensor(d["ti"][:, :], d["ti"][:, :], 1.0, d["xc"][:, :], ALU.add, ALU.mult), reads=[d["tik"], d["xck"]], writes=[d["tik"]])
                    op("dve", lambda e, d=d: e.tensor_tensor(d["ti"][:, :], d["ti"][:, :], d["a2"][:, :], ALU.mult), reads=[d["tik"], d["a2k"]], writes=[d["tik"]])
                    op("dve", lambda e, d=d, cb=cb: e.tensor_tensor_scan(d["hh"][:, :], d["tr"][:, :], d["ti"][:, :], hst[:, cb:cb + 1], ALU.mult, ALU.add),
                       reads=[d["trk"], d["tik"], "hst%d" % cb], writes=[d["hhk"]])
                    op("dve", lambda e, d=d, cb=cb: e.tensor_copy(hst[:, cb:cb + 1], d["hh"][:, 511:512]), reads=[d["hhk"]], writes=["hst%d" % cb])
                    op("pool", lambda e, d=d, ro=ro, g=g: e.tensor_tensor(ro[:, g, :], d["hh"][:, :], d["zu"][:, :], ALU.mult), reads=[d["hhk"], d["zuk"]], writes=[rok])
                    if g == 3:
                        dma("sp", ds_ro[half], lambda e, ro=ro, half=half, s_=s_: e.dma_start(out=DA(rt_d, half * 4 * 128 * S + s_ * 512, [[S, 128], [128 * S, 4], [1, 512]]), in_=ro[:, :, :]),
                            reads=[rok], writes=["RTd"])
        print('P1A arena hi', A.pos)
        SCH.barrier()

        A.reset(base)
        WBs = A.alloc([128, 8, NWB], BF16)
        hTb = [A.alloc([128, 8, 512], BF16) for _ in range(2)]
        posf = A.alloc([128, 512], F32)
        ang = A.alloc([128, 512], F32)
        kki = A.alloc([128, 512], I32)
        kkf = A.alloc([128, 512], F32)
        rr = A.alloc([128, 512], F32)
        COS = A.alloc([128, 512], F32)
        SIN = A.alloc([128, 512], F32)
        ropef = [A.alloc([128, 512], F32) for _ in range(2)]
        partf = [A.alloc([128, 512], F32) for _ in range(2)]
        QO = [A.alloc([128, 8, 512], BF16) for _ in range(2)]
        ZO = [A.alloc([128, 8, 512], F32) for _ in range(2)]
        VO = [A.alloc([128, 8, 512], BF16) for _ in range(2)]
        ds_h = [newds("hb%d" % i) for i in range(2)]
        ds_q = [newds("qo%d" % i) for i in range(2)]
        ds_z = [newds("zo%d" % i) for i in range(2)]
        ds_v = [newds("vo%d" % i) for i in range(2)]
        WCH = 1536
        ds_wb = [newds("wb%d" % i) for i in range(NWB // WCH)]
        load_w(WBs, wb_d, NWB, 0, NWB, None, None, chunk=WCH, keyf=lambda c: "WBs%d" % (c // WCH), dsf=lambda c: ds_wb[c // WCH])
        ctr_p, ctr_z, ctr_v = [0], [0], [0]

        def proj(b, col0, hTs, hkey, M=128):
            for kc in range(8):
                op("pe", lambda e, b=b, kc=kc, col0=col0, hTs=hTs: e.matmul(PS[b][:, :], WBs[:, kc, col0:col0 + 128], hTs[:, kc, :], start=(kc == 0), stop=(kc == 7)),
                   reads=[hkey, "WBs%d" % (col0 // WCH)], writes=["ps%d" % b], inc=(kc == 7))

        posi2 = [A.alloc([128, 512], I32), A.alloc([128, 512], I32)]
        ds_pos2 = [ds_pos, newds("pos2")]

        def p1b_loads(s):
            hTs_, hkey_ = hTb[s % 2], "hTb%d" % (s % 2)
            dma("sp", ds_h[s % 2], lambda e: e.dma_start(out=hTs_[:, :, :], in_=DA(ht_d, s * 512, [[S, 128], [128 * S, 8], [1, 512]])), reads=["HTd"], writes=[hkey_])
            dma("sp", ds_pos2[s % 2], lambda e: e.dma_start(out=posi2[s % 2][:, :], in_=DA(pos_d, s * 512, [[0, 128], [1, 512]])), writes=["posi%d" % (s % 2)])

        p1b_loads(0)
        for s in range(NSC):
            t0 = s * 512
            hTs, hkey = hTb[s % 2], "hTb%d" % (s % 2)
            if s + 1 < NSC:
                p1b_loads(s + 1)
            op("dve", lambda e, s=s: e.tensor_copy(posf[:, :], posi2[s % 2][:, :]), reads=["posi%d" % (s % 2)], writes=["posf"])
            op("dve", lambda e: e.tensor_scalar(ang[:, :], posf[:, :], sv[:, SV_FREQ:SV_FREQ + 1], None, ALU.mult), reads=["posf", "sv"], writes=["ang"])
            for which in range(2):
                tab, tk = (SIN, "SIN") if which == 0 else (COS, "COS")
                op("dve", lambda e, which=which: e.tensor_scalar(kki[:, :], ang[:, :], 1.0 / TWO_PI, 0.25 * which, ALU.mult, ALU.add), reads=["ang"], writes=["kki"])
                op("dve", lambda e: e.tensor_copy(kkf[:, :], kki[:, :]), reads=["kki"], writes=["kkf"])
                op("dve", lambda e: e.scalar_tensor_tensor(rr[:, :], kkf[:, :], -C1, ang[:, :], ALU.mult, ALU.add), reads=["kkf", "ang"], writes=["rr"])
                op("dve", lambda e: e.scalar_tensor_tensor(rr[:, :], kkf[:, :], -C2, rr[:, :], ALU.mult, ALU.add), reads=["kkf", "rr"], writes=["rr"])
                op("dve", lambda e, which=which: e.tensor_scalar(rr[:, :], rr[:, :], 0.5 * np.pi * which, np.pi, ALU.add, ALU.min), reads=["rr"], writes=["rr"])
                op("dve", lambda e: e.tensor_scalar(rr[:, :], rr[:, :], -np.pi, None, ALU.max), reads=["rr"], writes=["rr"])
                if which == 0:
                    op("act", lambda e, tab=tab: e.activation(tab[:, :], rr[:, :], AF.Sin, scale=sv[:, SV_SIGN:SV_SIGN + 1]), reads=["rr", "sv"], writes=[tk])
                else:
                    op("act", lambda e, tab=tab: e.activation(tab[:, :], rr[:, :], AF.Sin), reads=["rr"], writes=[tk])
            zi = s % 2
            for h in range(8):
                b = pbank(2, 4, ctr_p)
                proj(b, 2560 + h * 128, hTs, hkey)
                op("act", lambda e, b=b, zi=zi, h=h: e.activation(ZO[zi][:, h, :], PS[b][:, :], AF.Silu), reads=["ps%d" % b], writes=["ZO%d" % zi])
            dma("sp", ds_z[zi], lambda e, zi=zi, t0=t0: e.dma_start(out=DA(zd_d, t0, [[S, 128], [128 * S, 8], [1, 512]]), in_=ZO[zi][:, :, :]), reads=["ZO%d" % zi], writes=["ZDd"])
            vi = s % 2
            for h in range(8):
                b = pbank(2, 4, ctr_p)
                proj(b, 3584 + h * 128, hTs, hkey)
                if h % 2 == 0:
                    op("act", lambda e, b=b, vi=vi, h=h: e.activation(VO[vi][:, h, :], PS[b][:, :], AF.Copy), reads=["ps%d" % b], writes=["VO%d" % vi])
                else:
                    op("dve", lambda e, b=b, vi=vi, h=h: e.tensor_copy(VO[vi][:, h, :], PS[b][:, :]), reads=["ps%d" % b], writes=["VO%d" % vi])
            dma("sp", ds_v[vi], lambda e, vi=vi, t0=t0: e.dma_start(out=DA(vt_d, t0, [[S, 128], [128 * S, 8], [1, 512]]), in_=VO[vi][:, :, :]), reads=["VO%d" % vi], writes=["VTd"])
            for qk in range(2):
                cbase = qk * 1280
                qo, qok = QO[qk], "QO%d" % qk
                for g in range(2):
                    b = pbank(2, 4, ctr_p)
                    proj(b, cbase + g * 128, hTs, hkey)
                    b2 = pbank(2, 4, ctr_p)
                    proj(b2, cbase + 256 + g * 128, hTs, hkey)
                    rf, rk = ropef[g], "ropef%d" % g
                    pf, pk = partf[g], "partf%d" % g
                    op("dve", lambda e, b=b, rf=rf: e.tensor_tensor(rf[:, :], PS[b][:, :], COS[:, :], ALU.mult), reads=["ps%d" % b, "COS"], writes=[rk])
                    op("dve", lambda e, b2=b2, pf=pf: e.tensor_tensor(pf[:, :], PS[b2][:, :], SIN[:, :], ALU.mult), reads=["ps%d" % b2, "SIN"], writes=[pk])
                    op("pool", lambda e, rf=rf, pf=pf, qo=qo, g=g: e.tensor_tensor(qo[:, g, :], rf[:, :], pf[:, :], ALU.add), reads=[rk, pk], writes=[qok])
                for t in range(6):
                    b = pbank(2, 4, ctr_p)
                    proj(b, cbase + 512 + t * 128, hTs, hkey)
                    if t % 2 == 0:
                        op("act", lambda e, b=b, qo=qo, t=t: e.activation(qo[:, 2 + t, :], PS[b][:, :], AF.Copy), reads=["ps%d" % b], writes=[qok])
                    else:
                        op("dve", lambda e, b=b, qo=qo, t=t: e.tensor_copy(qo[:, 2 + t, :], PS[b][:, :]), reads=["ps%d" % b], writes=[qok])
                dst = qt_d if qk == 0 else kt_d
                dkey = "QTd" if qk == 0 else "KTd"
                dma("sp", ds_q[qk], lambda e, qo=qo, dst=dst, t0=t0: e.dma_start(out=DA(dst, t0, [[S, 128], [128 * S, 8], [1, 512]]), in_=qo[:, :, :]), reads=[qok], writes=[dkey])
        print('P1B arena hi', A.pos)
        SCH.barrier()

        A.reset(base)
        KTs = [A.alloc([128, S], BF16) for _ in range(2)]
        VTs = [A.alloc([128, S], BF16) for _ in range(2)]
        NV1, NV2, NV3 = 16, 16, 48
        V1r = A.alloc([128, NV1, 128], BF16)
        V2r = A.alloc([128, NV2, 128], BF16)
        V3r = A.alloc([128, NV3, 128], BF16)
        Qs = A.alloc([128, 8, 512], BF16)
        NZ = 4
        Zs = [A.alloc([128, 512], F32) for _ in range(NZ)]
        NPT = 8
        PT = [A.alloc([128, 512], BF16) for _ in range(NPT)]
        RD = [A.alloc([128, 512], F32) for _ in range(2)]
        OT = [A.alloc([128, 512], F32) for _ in range(2)]
        AO = [A.alloc([128, 512], BF16) for _ in range(2)]
        O3buf = A.alloc([128, 2048], F32)
        D3buf = A.alloc([128, 2048], F32)
        p2_hi = A.pos
        W3TOP = ARENA_WORDS - 12288
        assert p2_hi <= W3TOP, (p2_hi, W3TOP)
        A.reset(W3TOP)
        WGs = A.alloc([128, 8, 2048], BF16)
        WRs = A.alloc([128, 8, D], BF16)
        ds_w3 = newds("w3")
        load_w(WGs, wg_d, 2048, 0, 2048, ds_w3, "WGs")
        load_w(WRs, wor_d, D, 0, D, ds_w3, "WRs")
        A.reset(base)
        WAt = A.alloc([128, 8, D], BF16)
        assert A.pos == base + 4096
        A.reset(base + 8192)
        WOs = A.alloc([128, 8, D], BF16)
        A.reset(p2_hi)
        ds_k = [newds("k%d" % i) for i in range(2)]
        ds_qs = [newds("qsp%d" % i) for i in range(2)]
        ds_zs = [newds("zs%d" % i) for i in range(NZ)]
        ds_ao = [newds("ao%d" % i) for i in range(2)]
        ctr_b, ctr_pt, ctr_u = [0], [0], [0]
        inv_sqrt = 1.0 / np.sqrt(128.0)
        NSP = S // 2048
        LAG = 3

        def load_head(h):
            hs = h % 2
            Kh, kk = KTs[hs], "K%d" % hs
            g, j = h // 4, h % 4
            dma("sp", ds_k[hs], lambda e: e.dma_start(out=Kh[0:32, :], in_=DA(kt_d, (g * 128 + 32 * j) * S, [[S, 32], [1, S]])), reads=["KTd"], writes=[kk])
            dma("sp", ds_k[hs], lambda e: e.dma_start(out=Kh[32:128, :], in_=DA(kt_d, (256 + 96 * h) * S, [[S, 96], [1, S]])), reads=["KTd"], writes=[kk])
            dma("sp", ds_k[hs], lambda e: e.dma_start(out=VTs[hs][:, :], in_=DA(vt_d, h * 128 * S, [[S, 128], [1, S]])), reads=["VTd"], writes=[kk])

        def load_qspan(h, n):
            sl = (4 * n) % 8
            qk_ = "Qsp%d" % (n % 2)
            g, j = h // 4, h % 4
            dma("sp", ds_qs[n % 2], lambda e: e.dma_start(out=Qs.ap(p0=0, pn=32, dims=[(1, 2048)], off=sl * 512), in_=DA(qt_d, (g * 128 + 32 * j) * S + 2048 * n, [[S, 32], [1, 2048]])), reads=["QTd"], writes=[qk_])
            dma("sp", ds_qs[n % 2], lambda e: e.dma_start(out=Qs.ap(p0=32, pn=96, dims=[(1, 2048)], off=sl * 512), in_=DA(qt_d, (256 + 96 * h) * S + 2048 * n, [[S, 96], [1, 2048]])), reads=["QTd"], writes=[qk_])

        def load_z(h, s):
            zi = s % NZ
            dma("sp", ds_zs[zi], lambda e: e.dma_start(out=Zs[zi][:, :], in_=DA(zd_d, h * 128 * S + s * 512, [[S, 128], [1, 512]])), reads=["ZDd"], writes=["Z%d" % zi])

        load_head(0)
        load_qspan(0, 0)
        for h in range(8):
            hs = h % 2
            Kh, kk = KTs[hs], "K%d" % hs
            VTh = VTs[hs]

            def vgen(ring, rk, idx0, kps, kst, VTh=VTh, kk=kk):
                for c, kp in enumerate(kps):
                    op("pe", lambda e, c=c, kp=kp: e.transpose(PSB[3][:, c * 128:(c + 1) * 128], VTh.ap(dims=[(kst, 128)], off=kp), identb[:, :]),
                       reads=[kk, "identb"], writes=["ps3"], inc=(c == len(kps) - 1))
                n_ = len(kps)
                op("dve", lambda e: e.tensor_copy(ring[:, idx0:idx0 + n_, :], PSB[3][:, 0:128 * n_]), reads=["ps3"], writes=[rk])

            def vgen_sub(s, which):
                if which == 0:
                    vgen(V1r, "V1r", (4 * s) % NV1, [512 * s + 128 * c for c in range(4)], 1)
                elif which == 1:
                    vgen(V2r, "V2r", (4 * s) % NV2, [512 * s + r for r in range(4)], 4)
                else:
                    n_, qd = s // 4 + 1, s % 4
                    if n_ < NSP:
                        vgen(V3r, "V3r", (16 * n_ + 4 * qd) % NV3, [2048 * n_ + 4 * qd + r for r in range(4)], 16)

            def vgen12(s, VTh=VTh, kk=kk):
                kps = [(512 * s + 128 * c, 1) for c in range(4)] + [(512 * s + r, 4) for r in range(4)]
                for c, (kp, kst) in enumerate(kps):
                    op("pe", lambda e, c=c, kp=kp, kst=kst: e.transpose(PSB[3][:, c * 128:(c + 1) * 128], VTh.ap(dims=[(kst, 128)], off=kp), identb[:, :]),
                       reads=[kk, "identb"], writes=["ps3"], inc=(c == 7))
                i1, i2 = (4 * s) % NV1, (4 * s) % NV2
                op("dve", lambda e: e.tensor_copy(V1r[:, i1:i1 + 4, :], PSB[3][:, 0:512]), reads=["ps3"], writes=["V1r"])
                op("dve", lambda e: e.tensor_copy(V2r[:, i2:i2 + 4, :], PSB[3][:, 512:1024]), reads=["ps3"], writes=["V2r"])

            v1, v2, v3 = (V1r, NV1), (V2r, NV2), (V3r, NV3)
            if h + 1 < 8:
                load_head(h + 1)
            EARLY3 = (S // 2 >= 4096)
            if h == 7 and EARLY3:
                load_w(WAt, woa_d, D, 0, D, ds_w3, "WAt", extra=["K0"])
                load_w(WOs, wo_d, D, 0, D, ds_w3, "WOs", extra=["K0"])
            vgen12(0)
            for qd in range(4):
                vgen(V3r, "V3r", 4 * qd, [4 * qd + r for r in range(4)], 16)
            jobs = []
            for n in range(NSP):
                qsl = ((4 * n) % 8) * 512
                for g in range(4):
                    banks = [(0, 0, [((128 * c, 1, 128), (qsl + 4 * g + c, 16, 128), (2048 * n + 4 * g + c, 16), (v3, 16 * n + 4 * g + c)) for c in range(4)])]
                    if n > 0:
                        banks.append((1, 0, [((128 * c, 1, 128), (qsl + 4 * g + c, 16, 128), (2048 * (n - 1) + 4 * g + c, 16), (v3, 16 * (n - 1) + 4 * g + c)) for c in range(4)]))
                    for bi, bk in enumerate(banks):
                        jobs.append((("g3", n, g), bi, len(banks), bk))
                for sp in range(4):
                    s = 4 * n + sp
                    t0 = s * 512
                    qb = (s % 8) * 512
                    banks = []
                    banks.append((0, 0, [((128 * c, 1, 128), (qb + 128 * c, 1, 128), (t0 + 128 * c, 1), (v1, 4 * s + c)) for c in range(4)]))
                    c0 = 1 if s == 0 else 0
                    banks.append((1, 128 * c0, [((128 * c, 1, 128), (qb + 128 * c, 1, 128), (t0 + 128 * (c - 1), 1), (v1, 4 * s + c - 1)) for c in range(c0, 4)]))
                    banks.append((2, 0, [((r, 4, 128), (qb + r, 4, 128), (t0 + r, 4), (v2, 4 * s + r)) for r in range(4)]))
                    if s > 0:
                        banks.append((3, 0, [((r, 4, 128), (qb + r, 4, 128), (t0 - 512 + r, 4), (v2, 4 * (s - 1) + r)) for r in range(4)]))
                    for bi, bk in enumerate(banks):
                        jobs.append((("sc", s, sp), bi, len(banks), bk))
            load_z(h, 0)
            if NSC > 1:
                load_z(h, 1)
            pend = []
            for i in range(len(jobs) + LAG):
                if i < len(jobs):
                    (unit, bi, nbk, (mi, col0, items)) = jobs[i]
                    if bi == 0:
                        ctr_u[0] += 1
                    upar = ctr_u[0] % 2
                    if unit[0] == "sc":
                        s = unit[1]
                        qk_ = "Qsp%d" % ((s // 4) % 2)
                        if bi == 0 and s + 2 < NSC:
                            load_z(h, s + 2)
                        if bi == 0 and unit[2] == 0:
                            if s // 4 + 1 < NSP:
                                load_qspan(h, s // 4 + 1)
                            elif h + 1 < 8:
                                load_qspan(h + 1, 0)
                        if bi == 0 and s + 1 < NSC:
                            vgen12(s + 1)
                        if bi == 2 and unit[2] % 2 == 0 and s // 4 + 1 < NSP:
                            n_, hf_ = s // 4 + 1, unit[2] // 2
                            vgen(V3r, "V3r", (16 * n_ + 8 * hf_) % NV3, [2048 * n_ + 8 * hf_ + r for r in range(8)], 16)
                    else:
                        qk_ = "Qsp%d" % (unit[1] % 2)
                    b = ctr_b[0] % 3
                    ctr_b[0] += 1
                    for ii, (pd, qd_, (kp, kst), vt) in enumerate(items):
                        op("pe", lambda e, b=b, pd=pd, qd_=qd_, kp=kp, kst=kst, Kh=Kh, ii=ii: e.matmul(
                            PS[b].ap(dims=[(pd[1], pd[2])], off=pd[0]), Kh.ap(dims=[(kst, 128)], off=kp), Qs.ap(dims=[(qd_[1], qd_[2])], off=qd_[0]),
                            start=(ii == 0), stop=False, skip_group_check=True),
                           reads=[kk, qk_], writes=["ps%d" % b], inc=False)
                    op("pe", lambda e, b=b, col0=col0, mi=mi: e.matmul(PS[b][:, col0:512], identb[:, :], masks[:, mi, col0:512], start=False, stop=True, skip_group_check=True),
                       reads=["masks", "identb"], writes=["ps%d" % b], inc=True)
                    pi_ = ctr_pt[0] % NPT
                    ctr_pt[0] += 1
                    P, pk_ = PT[pi_], "PT%d" % pi_
                    op("act", lambda e, b=b, P=P, col0=col0: e.activation(P[:, col0:512], PS[b][:, col0:512], AF.Exp, scale=inv_sqrt), reads=["ps%d" % b], writes=[pk_])
                    pend.append((unit, upar, bi, nbk, P, pk_, col0, items))
                j2 = i - LAG
                if j2 >= 0:
                    (unit, upar, bi, nbk, P, pk_, col0, items) = pend[j2]
                    ob = 4 + 2 * upar
                    for ii, (pd, qd_, (kp, kst), (vv, vt)) in enumerate(items):
                        op("pe", lambda e, P=P, pd=pd, vv=vv, vt=vt, first=(bi == 0 and ii == 0), last=(bi == nbk - 1 and ii == len(items) - 1), ob=ob: e.matmul(
                            PS[ob].ap(dims=[(pd[1], pd[2])], off=pd[0]), vv[0][:, vt % vv[1], :], P.ap(dims=[(pd[1], pd[2])], off=pd[0]), start=first, stop=last, skip_group_check=True),
                           reads=[pk_, "V1r", "V2r", "V3r"], writes=["ps%d" % ob], inc=False)
                    op("pe", lambda e, P=P, col0=col0, bi=bi, nbk=nbk, ob=ob: e.matmul(PS[ob + 1][:, col0:512], onesb[:, :], P[:, col0:512], start=(bi == 0), stop=(bi == nbk - 1), skip_group_check=True),
                       reads=[pk_, "onesb"], writes=["ps%d" % (ob + 1)], inc=True)
                    if bi == nbk - 1:
                        if unit[0] == "g3":
                            g = unit[2]
                            op("dve", lambda e, ob=ob, g=g: e.tensor_copy(O3buf.ap(dims=[(1, 4), (16, 128)], off=4 * g), PS[ob][:, :]), reads=["ps%d" % ob], writes=["O3buf"])
                            op("act", lambda e, ob=ob, g=g: e.activation(D3buf.ap(dims=[(1, 4), (16, 128)], off=4 * g), PS[ob + 1][:, :], AF.Copy), reads=["ps%d" % (ob + 1)], writes=["D3buf"])
                        else:
                            s, sp = unit[1], unit[2]
                            zi = s % NZ
                            Z, zk = Zs[zi], "Z%d" % zi
                            t0 = s * 512
                            par = upar
                            op("dve", lambda e, ob=ob, par=par, sp=sp: e.tensor_tensor(RD[par][:, :], PS[ob + 1][:, :], D3buf[:, 512 * sp:512 * (sp + 1)], ALU.add), reads=["ps%d" % (ob + 1), "D3buf"], writes=["RD%d" % par])
                            op("dve", lambda e, ob=ob, par=par, sp=sp: e.tensor_tensor(OT[par][:, :], PS[ob][:, :], O3buf[:, 512 * sp:512 * (sp + 1)], ALU.add), reads=["ps%d" % ob, "O3buf"], writes=["OT%d" % par])
                            op("act", lambda e, par=par: e.activation(RD[par][:, :], RD[par][:, :], AF.Ln), reads=["RD%d" % par], writes=["RD%d" % par])
                            op("act", lambda e, par=par: e.activation(RD[par][:, :], RD[par][:, :], AF.Exp, scale=-1.0), reads=["RD%d" % par], writes=["RD%d" % par])
                            op("pool", lambda e, par=par: e.tensor_tensor(OT[par][:, :], OT[par][:, :], RD[par][:, :], ALU.mult), reads=["OT%d" % par, "RD%d" % par], writes=["OT%d" % par])
                            op("pool", lambda e, par=par, Z=Z: e.tensor_tensor(AO[par][:, :], OT[par][:, :], Z[:, :], ALU.mult), reads=["OT%d" % par, zk], writes=["AO%d" % par])
                            dma("sp", ds_ao[par], lambda e, par=par, h=h, t0=t0: e.dma_start(out=DA(at_d, h * 128 * S + t0, [[S, 128], [1, 512]]), in_=AO[par][:, :]), reads=["AO%d" % par], writes=["ATd"])
        print('P2 arena hi', A.pos)
        SCH.barrier()

        A.reset(base)
        _wat = A.alloc([128, 8, D], BF16)
        hT3 = [A.alloc([128, 8, 512], BF16) for _ in range(2)]
        assert A.pos == base + 8192
        _wos = A.alloc([128, 8, D], BF16)
        if not EARLY3:
            load_w(WAt, woa_d, D, 0, D, ds_w, "WAt")
            load_w(WOs, wo_d, D, 0, D, ds_w, "WOs")
        AT3 = [A.alloc([128, 8, 512], BF16) for _ in range(2)]
        RT3 = [A.alloc([128, 8, 512], BF16) for _ in range(2)]
        SG = [A.alloc([128, 512], F32) for _ in range(2)]
        M1 = [A.alloc([128, 512], F32) for _ in range(2)]
        M2 = [A.alloc([128, 512], F32) for _ in range(2)]
        MGs = [A.alloc([128, 8, 512], BF16) for _ in range(2)]
        X3 = [A.alloc([128, D], F32) for _ in range(2)]
        Y3 = [A.alloc([128, D], F32) for _ in range(2)]
        ssq3 = A.alloc([128, 2], F32)
        ds_in3 = [newds("in3_%d" % i) for i in range(2)]
        ds_x3 = [newds("x3_%d" % i) for i in range(2)]
        ds_o3 = [newds("o3_%d" % i) for i in range(2)]
        dma("sp", ds_c, lambda e: e.dma_start(out=BC0[:, :], in_=DA(mod_d, 2 * D, [[0, 128], [1, D]])), reads=["MODd"], writes=["BC0"])
        dma("sp", ds_c, lambda e: e.dma_start(out=BC1[:, :], in_=DA(gfin_d, 0, [[0, 128], [1, D]])), writes=["BC1"])

        ctr_p, ctr_w, ctr_x, ctr_m, ctr_sg = [0], [0], [0], [0], [0]
        SGr = [A.alloc([128, 512], F32) for _ in range(2)]

        def p3_ft(s):
            t0 = s * 512
            si = s % 2
            hTs, ATs, RTs = hT3[si], AT3[si], RT3[si]
            ik = "in3_%d" % si
            MG, mgk = MGs[si], "MG%d" % si
            for ft in range(8):
                mi_ = ctr_m[0] % 2
                ctr_m[0] += 1
                for br in range(2):
                    pr = ctr_p[0] % 3
                    ctr_p[0] += 1
                    bg, by = 2 * pr, 2 * pr + 1
                    Wy, Ys = (WRs, RTs) if br == 0 else (WAt, ATs)
                    wyk = "WRs" if br == 0 else "WAt"
                    for kc in range(8):
                        op("pe", lambda e, kc=kc, bg=bg, br=br, ft=ft: e.matmul(PS[bg][:, :], WGs[:, kc, br * 1024 + ft * 128:br * 1024 + (ft + 1) * 128], hTs[:, kc, :], start=(kc == 0), stop=(kc == 7)),
                           reads=[ik, "WGs"], writes=["ps%d" % bg], inc=(kc == 7))
                    for kc in range(8):
                        op("pe", lambda e, kc=kc, by=by, Wy=Wy, Ys=Ys, ft=ft: e.matmul(PS[by][:, :], Wy[:, kc, ft * 128:(ft + 1) * 128], Ys[:, kc, :], start=(kc == 0), stop=(kc == 7)),
                           reads=[ik, wyk], writes=["ps%d" % by], inc=(kc == 7))
                    sgi = ctr_sg[0] % 4
                    ctr_sg[0] += 1
                    sg, sgk = (SG + SGr)[sgi], "SG%d" % sgi
                    Mx, mk = (M1, "M1_%d" % mi_) if br == 0 else (M2, "M2_%d" % mi_)
                    op("act", lambda e, bg=bg, sg=sg, br=br, ft=ft: e.activation(sg[:, :], PS[bg][:, :], AF.Sigmoid, bias=sv[:, SV_BG + 8 * br + ft:SV_BG + 8 * br + ft + 1]), reads=["ps%d" % bg, "sv"], writes=[sgk])
                    op("dve", lambda e, by=by, sg=sg, Mx=Mx, mi_=mi_: e.tensor_tensor(Mx[mi_][:, :], PS[by][:, :], sg[:, :], ALU.mult), reads=["ps%d" % by, sgk], writes=[mk])
                op("pool", lambda e, ft=ft, mi_=mi_: e.tensor_tensor(MG[:, ft, :], M1[mi_][:, :], M2[mi_][:, :], ALU.add), reads=["M1_%d" % mi_, "M2_%d" % mi_], writes=[mgk])

        pendB = []

        def p3_B(xi, X, Y, row0):
            op("dve", lambda e: e.scalar_tensor_tensor(Y[:, :], X[:, :], ssq3[:, xi:xi + 1], BC1[:, :], ALU.mult, ALU.mult), reads=["X3_%d" % xi, "ssq3_%d" % xi, "BC1", "Y3_%d" % xi], writes=["Y3_%d" % xi])
            dma("sp", ds_o3[xi], lambda e: e.dma_start(out=DA(out_d, row0 * D, [[D, 128], [1, D]]), in_=Y[:, :]), reads=["Y3_%d" % xi], writes=["OUTd"])

        def p3_wo(s):
            t0 = s * 512
            si = s % 2
            MG, mgk = MGs[si], "MG%d" % si
            for tt in range(4):
                xi = ctr_x[0] % 2
                ctr_x[0] += 1
                X, Y = X3[xi], Y3[xi]
                dma("sp", ds_x3[xi], lambda e, X=X, tt=tt: e.dma_start(out=X[:, :], in_=DA(x_d, (t0 + 128 * tt) * D, [[D, 128], [1, D]])), writes=["X3_%d" % xi])
                for hf in range(2):
                    b = 6 + ctr_w[0] % 2
                    ctr_w[0] += 1
                    for kc in range(8):
                        op("pe", lambda e, b=b, kc=kc, tt=tt, hf=hf: e.matmul(PS[b][:, :], MG[:, kc, tt * 128:(tt + 1) * 128], WOs[:, kc, hf * 512:(hf + 1) * 512], start=(kc == 0), stop=(kc == 7)),
                           reads=[mgk, "WOs"], writes=["ps%d" % b], inc=(kc == 7))
                    op("dve", lambda e, b=b, hf=hf, Y=Y: e.tensor_tensor(Y[:, hf * 512:(hf + 1) * 512], PS[b][:, :], BC0[:, hf * 512:(hf + 1) * 512], ALU.mult), reads=["ps%d" % b, "BC0"], writes=["Y3_%d" % xi])
                op("pool", lambda e, X=X, Y=Y: e.tensor_tensor(X[:, :], X[:, :], Y[:, :], ALU.add), reads=["X3_%d" % xi, "Y3_%d" % xi], writes=["X3_%d" % xi])
                op("act", lambda e, X=X, Y=Y, xi=xi: e.activation(Y[:, :], X[:, :], AF.Square, accum_out=ssq3[:, xi:xi + 1]), reads=["X3_%d" % xi, "Y3_%d" % xi], writes=["Y3_%d" % xi, "ssq3_%d" % xi])
                op("pool", lambda e, xi=xi: e.tensor_scalar(ssq3[:, xi:xi + 1], ssq3[:, xi:xi + 1], 1.0 / D, 1e-6, ALU.mult, ALU.add), reads=["ssq3_%d" % xi], writes=["ssq3_%d" % xi])
                op("pool", lambda e, xi=xi: e.tensor_tensor(ssq3[:, xi:xi + 1], ssq3[:, xi:xi + 1], mhalf[:, 0:1], ALU.pow), reads=["ssq3_%d" % xi, "mhalf"], writes=["ssq3_%d" % xi])
                if pendB:
                    p3_B(*pendB.pop(0))
                pendB.append((xi, X, Y, t0 + 128 * tt))

        def p3_loads(s):
            t0 = s * 512
            si = s % 2
            ik = "in3_%d" % si
            for (dst_, src_, dk) in ((hT3[si], ht_d, "HTd"), (AT3[si], at_d, "ATd"), (RT3[si], rt_d, "RTd")):
                dma("sp", ds_in3[si], lambda e, dst_=dst_, src_=src_: e.dma_start(out=dst_[:, :, :], in_=DA(src_, t0, [[S, 128], [128 * S, 8], [1, 512]])), reads=[dk], writes=[ik])

        p3_loads(0)
        if NSC > 1:
            p3_loads(1)
        p3_ft(0)
        for s in range(NSC):
            if s + 1 < NSC:
                p3_ft(s + 1)
            if s + 2 < NSC:
                p3_loads(s + 2)
            p3_wo(s)
        while pendB:
            p3_B(*pendB.pop(0))
        print('P3 arena hi', A.pos, 'of', ARENA_WORDS)
        assert A.pos <= W3TOP
        SCH.final_wait("sp")
        with nc.Block() as block:
            SCH.emit(block)
    return nc


def _layouts(w_in, conv_w, conv_b, w_a, b_a, w_x, b_x, lam, b_gate, b_mod, g_norm):
    wi = w_in
    x_rnn, z_rnn, q, k, v, z_attn, g_r, g_a = [wi[:, i * 1024:(i + 1) * 1024] for i in range(8)]
    WA = np.ascontiguousarray(np.concatenate([x_rnn, z_rnn], axis=1))

    def qk_cols(w):
        hd = w.reshape(1024, 8, 128)
        rope = hd[:, :, 0:32].reshape(1024, 256)
        partner = np.concatenate([hd[:, :, 16:32], hd[:, :, 0:16]], axis=2).reshape(1024, 256)
        pas = hd[:, :, 32:128].reshape(1024, 768)
        return [rope, partner, pas]
    WB = np.ascontiguousarray(np.concatenate(qk_cols(q) + qk_cols(k) + [z_attn, v], axis=1))
    WG = np.ascontiguousarray(np.concatenate([g_r, g_a], axis=1))
    sv = np.zeros((128, NSV), np.float32)
    sv[:, SV_BMOD:SV_BMOD + 24] = b_mod.reshape(24, 128).T
    sv[:, SV_GN:SV_GN + 8] = g_norm.reshape(8, 128).T
    sv[:, SV_CW:SV_CW + 32] = conv_w.reshape(4, 8, 128).transpose(2, 1, 0).reshape(128, 32)
    sv[:, SV_CB:SV_CB + 8] = conv_b.reshape(8, 128).T
    sv[:, SV_BA:SV_BA + 8] = b_a.T
    sv[:, SV_BX:SV_BX + 8] = b_x.T
    sv[:, SV_LAM:SV_LAM + 8] = lam.reshape(8, 128).T
    sv[:, SV_BG:SV_BG + 16] = b_gate.reshape(16, 128).T
    wax = np.ascontiguousarray(np.concatenate([w_a, w_x], axis=0).transpose(1, 0, 2).reshape(128, 16 * 128))
    return WA, WB, WG, sv, wax


def _consts():
    p = np.arange(128)
    freq = (500000.0 ** (-(np.arange(0, 32, 2, dtype=np.float32)) / 32.0)).astype(np.float32)
    fcol = freq[p % 16]
    sign = np.where((p % 32) < 16, -1.0, 1.0).astype(np.float32)
    k = np.arange(128)[:, None]
    col = np.arange(512)[None, :]
    m = np.zeros((128, 12, 512), np.float32)
    m[:, 0] = (col % 128) >= k
    m[:, 1] = (col % 128) <= k
    m[:, 2] = (col // 4) >= k
    m[:, 3] = (col // 4) <= k
    for sp in range(4):
        m[:, 4 + sp] = (32 * sp + col // 16) >= k
        m[:, 8 + sp] = (32 * sp + col // 16) <= k
    m = (m - 1.0) * 30000.0
    return fcol, sign, np.ascontiguousarray(m.reshape(128, 12 * 512)), np.eye(128, dtype=np.float32)


def make_in_maps(S, nb, x, c, positions, g_norm, w_mod, b_mod, w_in, b_gate, conv_w, conv_b, w_a, b_a, w_x, b_x, lam,
                 w_out_rnn, w_out_attn, w_o, g_final):
    f = lambda a: np.ascontiguousarray(np.asarray(a), dtype=np.float32)
    WA, WB, WG, sv, wax = _layouts(f(w_in[0]), f(conv_w[0]), f(conv_b[0]), f(w_a[0]), f(b_a[0]), f(w_x[0]), f(b_x[0]), f(lam[0]),
                                   f(b_gate[0]), f(b_mod[0]), f(g_norm[0]))
    fcol, sign, masks, ident = _consts()
    sv[:, SV_FREQ] = fcol
    sv[:, SV_SIGN] = sign
    shared = {"w_mod": f(w_mod[0]), "sv": sv, "WA": WA, "WB": WB, "WG": WG, "wor": f(w_out_rnn[0]), "woa": f(w_out_attn[0]),
              "wo": f(w_o[0]), "wax": wax, "gfin": f(g_final).reshape(1, D), "ident": ident, "masks": masks}
    x = np.asarray(x)
    c = np.asarray(c)
    positions = np.asarray(positions)
    maps = []
    for b in range(nb):
        m = dict(shared)
        m["x"] = np.ascontiguousarray(x[b], dtype=np.float32)
        m["pos"] = np.ascontiguousarray(positions[b].reshape(1, S), dtype=np.int32)
        m["cT"] = np.ascontiguousarray(c[b].astype(np.float32).reshape(8, 128).T)
        maps.append(m)
    return maps


def kernel(**inputs):
    x = np.asarray(inputs["x"])
    B, S, _ = x.shape
    nc = build(S)
    maps = make_in_maps(S, B, **inputs)
    res = run_bass_kernel_spmd(nc, maps, core_ids=list(range(B)))
    return np.stack([np.asarray(r["out"]).reshape(S, D) for r in res.results], axis=0).astype(np.float32)
```

```python
import numpy as np
import concourse.bass as bass
import concourse.mybir as mybir
from concourse.bass_utils import run_bass_kernel_spmd

F32 = mybir.dt.float32
BF16 = mybir.dt.bfloat16
I32 = mybir.dt.int32
AF = mybir.ActivationFunctionType
ALU = mybir.AluOpType

D = 1024
SEQ = 8192
NB = 8
TWO_PI = 6.283185307179586
C1 = 6.28125
C2 = TWO_PI - C1
ENGS = ("pe", "act", "dve", "pool", "sp")


class DSem:
    def __init__(self, sem, name):
        self.sem = sem
        self.count = 0
        self.name = name


class Sched:
    def __init__(self, nc, esems):
        self.nc = nc
        self.ops = {e: [] for e in ENGS}
        self.cnt = {e: 0 for e in ENGS}
        self.pending = {e: False for e in ENGS}
        self.esem = esems
        self.known = {e: {} for e in ENGS}
        self.lw = {}
        self.rd = {}
        self.dsems = []

    def dsem(self, sem, name):
        d = DSem(sem, name)
        self.dsems.append(d)
        return d

    def _tok_val(self, tok):
        if tok[0] == "eng":
            return ("e", tok[1]), self.esem[tok[1]], tok[2]
        d = tok[1]
        return ("d", d.name), d.sem, d.count

    def _deps(self, e, reads, writes):
        toks = []
        for k in reads:
            t = self.lw.get(k)
            if t is not None:
                toks.append(t)
        for k in writes:
            t = self.lw.get(k)
            if t is not None:
                toks.append(t)
            toks.extend(self.rd.get(k, ()))
        need = {}
        for t in toks:
            if t[0] == "eng" and t[1] == e and e == "pe":
                continue
            key, sem, val = self._tok_val(t)
            if val <= 0:
                continue
            if need.get(key, (None, 0))[1] < val:
                need[key] = (sem, val)
        for key, (sem, val) in need.items():
            if self.known[e].get(key, 0) >= val:
                continue
            self.known[e][key] = val
            if key == ("e", e) and val > self.cnt[e]:
                raise RuntimeError("same-engine dep on un-incremented instruction (%s)" % e)
            self.ops[e].append(lambda eng, sem=sem, val=val: eng.wait_ge(sem, val))

    def _commit(self, tok, reads, writes):
        for k in writes:
            self.lw[k] = tok
            self.rd[k] = []
        for k in reads:
            self.rd.setdefault(k, []).append(tok)

    def op(self, e, fn, reads=(), writes=(), inc=True):
        self._deps(e, reads, writes)
        sem = self.esem[e]
        if inc:
            self.cnt[e] += 1
            self.pending[e] = False
            self.ops[e].append(lambda eng, fn=fn, sem=sem: fn(eng).then_inc(sem, 1))
            tok = ("eng", e, self.cnt[e])
        else:
            self.pending[e] = True
            self.ops[e].append(lambda eng, fn=fn: fn(eng))
            tok = ("eng", e, self.cnt[e] + 1)
        self._commit(tok, reads, writes)

    def dma(self, q, ds, fn, reads=(), writes=()):
        self._deps(q, reads, writes)
        ds.count += 16
        self.ops[q].append(lambda eng, fn=fn, sem=ds.sem: fn(eng).then_inc(sem, 16))
        self._commit(("dma", ds), reads, writes)

    def barrier(self):
        for e in ENGS:
            for e2 in ENGS:
                if e2 == e:
                    continue
                assert not self.pending[e2]
                val = self.cnt[e2]
                key = ("e", e2)
                if val > self.known[e].get(key, 0):
                    self.known[e][key] = val
                    self.ops[e].append(lambda eng, sem=self.esem[e2], val=val: eng.wait_ge(sem, val))
            for d in self.dsems:
                key = ("d", d.name)
                if d.count > self.known[e].get(key, 0):
                    self.known[e][key] = d.count
                    self.ops[e].append(lambda eng, sem=d.sem, val=d.count: eng.wait_ge(sem, val))

    def final_wait(self, e="sp"):
        for d in self.dsems:
            key = ("d", d.name)
            if d.count > self.known[e].get(key, 0):
                self.known[e][key] = d.count
                self.ops[e].append(lambda eng, sem=d.sem, val=d.count: eng.wait_ge(sem, val))

    def emit(self, block):
        ops = self.ops

        @block.tensor
        def _(eng):
            for f in ops["pe"]:
                f(eng)

        @block.scalar
        def _(eng):
            for f in ops["act"]:
                f(eng)

        @block.vector
        def _(eng):
            for f in ops["dve"]:
                f(eng)

        @block.gpsimd
        def _(eng):
            for f in ops["pool"]:
                f(eng)

        @block.sync
        def _(eng):
            for f in ops["sp"]:
                f(eng)


class View:
    def __init__(self, tensor, row, off, shape):
        self.t = tensor
        self.row = row
        self.off = off
        self.shape = tuple(shape)
        st = []
        s = 1
        for d in reversed(self.shape[1:]):
            st.append(s)
            s *= d
        self.strides = tuple(reversed(st))
        self.size = s

    def ap(self, p0=0, pn=None, dims=None, off=0):
        if pn is None:
            pn = self.shape[0] - p0
        if dims is None:
            dims = [(1, self.size)]
        pat = [[self.row, pn]] + [[s, n] for (s, n) in dims]
        return bass.AP(self.t, p0 * self.row + self.off + off, pat)

    def __getitem__(self, key):
        if not isinstance(key, tuple):
            key = (key,)
        key = key + (slice(None),) * (len(self.shape) - len(key))
        ps = key[0]
        if isinstance(ps, int):
            p0, pn = ps, 1
        else:
            p0 = ps.start or 0
            pn = (ps.stop if ps.stop is not None else self.shape[0]) - p0
        off = 0
        dims = []
        for k, d, st in zip(key[1:], self.shape[1:], self.strides):
            if isinstance(k, int):
                off += k * st
            else:
                a = k.start or 0
                b = k.stop if k.stop is not None else d
                step = k.step or 1
                n = (b - a + step - 1) // step
                off += a * st
                dims.append((st * step, n))
        merged = []
        for s, n in dims:
            if merged and merged[-1][0] == s * n:
                merged[-1] = (s, merged[-1][1] * n)
            else:
                merged.append((s, n))
        if not merged:
            merged = [(1, 1)]
        return self.ap(p0, pn, dims=merged, off=off)


class Arena:
    def __init__(self, t32, words):
        self.t32 = t32
        self.words = words
        self.views = {}
        self.pos = 0
        self.hi = 0

    def view(self, dtype):
        if dtype not in self.views:
            if dtype == F32:
                self.views[dtype] = (self.t32, self.words, 1)
            else:
                r = 4 // mybir.dt.size(dtype)
                self.views[dtype] = (self.t32.bitcast(dtype), self.words * r, r)
        return self.views[dtype]

    def reset(self, pos):
        self.pos = pos

    def alloc(self, shape, dtype):
        t, row, r = self.view(dtype)
        n = 1
        for d in shape[1:]:
            n *= d
        words = (n + r - 1) // r
        words = (words + 7) // 8 * 8
        off = self.pos
        self.pos += words
        self.hi = max(self.hi, self.pos)
        assert self.pos <= self.words, "arena overflow %d > %d" % (self.pos, self.words)
        return View(t, row, off * r, shape)


def DA(t, off, pat):
    return bass.AP(t, off, [list(p) for p in pat])


SV_BMOD, SV_GN, SV_CW, SV_CB, SV_BA, SV_BX, SV_LAM, SV_BG, SV_FREQ, SV_SIGN = 0, 24, 32, 64, 72, 80, 88, 96, 112, 113
NSV = 120
NWB = 4608
ARENA_WORDS = 52224


def build(S, debug=False):
    NSC = S // 512
    nc = bass.Bass("TRN2", target_bir_lowering=False)
    kin = "ExternalInput"
    x_d = nc.dram_tensor("x", [S, D], F32, kind=kin)
    pos_d = nc.dram_tensor("pos", [1, S], I32, kind=kin)
    cT_d = nc.dram_tensor("cT", [128, 8], F32, kind=kin)
    wmod_d = nc.dram_tensor("w_mod", [D, 3 * D], F32, kind=kin)
    sv_d = nc.dram_tensor("sv", [128, NSV], F32, kind=kin)
    wa_d = nc.dram_tensor("WA", [D, 2048], F32, kind=kin)
    wb_d = nc.dram_tensor("WB", [D, NWB], F32, kind=kin)
    wg_d = nc.dram_tensor("WG", [D, 2048], F32, kind=kin)
    wor_d = nc.dram_tensor("wor", [D, D], F32, kind=kin)
    woa_d = nc.dram_tensor("woa", [D, D], F32, kind=kin)
    wo_d = nc.dram_tensor("wo", [D, D], F32, kind=kin)
    wax_d = nc.dram_tensor("wax", [128, 16 * 128], F32, kind=kin)
    gfin_d = nc.dram_tensor("gfin", [1, D], F32, kind=kin)
    ident_d = nc.dram_tensor("ident", [128, 128], F32, kind=kin)
    masks_d = nc.dram_tensor("masks", [128, 12 * 512], F32, kind=kin)
    skind = "ExternalOutput" if debug else "Internal"
    mod_d = nc.dram_tensor("MODS", [24, 128], F32, kind=skind)
    ht_d = nc.dram_tensor("HT", [D, S], BF16, kind=skind)
    rt_d = nc.dram_tensor("RT", [D, S], BF16, kind=skind)
    qt_d = nc.dram_tensor("QT", [D, S], BF16, kind=skind)
    kt_d = nc.dram_tensor("KT", [D, S], BF16, kind=skind)
    vt_d = nc.dram_tensor("VT", [D, S], BF16, kind=skind)
    zd_d = nc.dram_tensor("ZD", [D, S], F32, kind=skind)
    at_d = nc.dram_tensor("AT", [D, S], BF16, kind=skind)
    out_d = nc.dram_tensor("out", [S, D], F32, kind="ExternalOutput")

    import contextlib
    with contextlib.ExitStack() as es:
        ar_t = es.enter_context(nc.sbuf_tensor("arena", [128, ARENA_WORDS], F32))
        psb = [es.enter_context(nc.psum_tensor("ps%d" % i, [128, 512], F32)) for i in range(8)]
        esems = {e: es.enter_context(nc.semaphore("s_" + e)) for e in ENGS}
        SCH = Sched(nc, esems)
        _dn = [0]

        def newds(name):
            _dn[0] += 1
            return SCH.dsem(es.enter_context(nc.semaphore("d%d_%s" % (_dn[0], name))), "%d_%s" % (_dn[0], name))

        A = Arena(ar_t, ARENA_WORDS)
        PS = [View(p, 512, 0, [128, 512]) for p in psb]
        PSB = [View(p.bitcast(BF16), 1024, 0, [128, 1024]) for p in psb]
        op, dma = SCH.op, SCH.dma

        sv = A.alloc([128, NSV], F32)
        identf = A.alloc([128, 128], F32)
        identb = A.alloc([128, 128], BF16)
        onesb = A.alloc([128, 128], BF16)
        masks = A.alloc([128, 12, 512], BF16)
        MT = A.alloc([128, 24], F32)
        G3 = A.alloc([128, 24], F32)
        hc = A.alloc([128, 8], F32)
        hba = A.alloc([128, 8], F32)
        hbx = A.alloc([128, 8], F32)
        mhalf = A.alloc([128, 8], F32)
        BC0 = A.alloc([128, D], F32)
        BC1 = A.alloc([128, D], F32)
        base = A.pos
        WAs = A.alloc([128, 8, 2048], BF16)
        WX = A.alloc([128, 16, 128], BF16)
        base1a = A.pos

        ds_c = newds("const")
        ds_c2 = newds("const2")
        ds_w = newds("w")
        ds_w2 = newds("w2")
        dma("sp", ds_c, lambda e: e.dma_start(out=sv[:, :], in_=sv_d.ap()), writes=["sv"])
        dma("sp", ds_c, lambda e: e.dma_start(out=identf[:, :], in_=ident_d.ap()), writes=["identf"])
        dma("pool", ds_c2, lambda e: e.dma_start(out=identb[:, :], in_=ident_d.ap()), writes=["identb"])
        for i in range(12):
            dma("pool", ds_c2, lambda e, i=i: e.dma_start(out=masks[:, i, :], in_=DA(masks_d, i * 512, [[12 * 512, 128], [1, 512]])), writes=["masks"])
        op("pool", lambda e: e.memset(onesb[:, :], 1.0), writes=["onesb"])
        op("pool", lambda e: e.memset(mhalf[:, :], -0.5), writes=["mhalf"])
        def load_w(dst, src_d, W, col0, ncols, ds, key, chunk=2048, keyf=None, dsf=None, extra=()):
            c = 0
            while c < ncols:
                n = min(chunk, ncols - c)
                k_ = keyf(c) if keyf else key
                d_ = dsf(c) if dsf else ds
                dma("pool", d_, lambda e, c=c, n=n: e.dma_start(
                    out=dst[:, :, c:c + n], in_=DA(src_d, col0 + c, [[W, 128], [128 * W, 8], [1, n]])), writes=[k_] + list(extra))
                c += n

        load_w(WAs, wa_d, 2048, 0, 2048, ds_w2, "WAs", chunk=1024)
        dma("pool", ds_w2, lambda e: e.dma_start(out=WX[:, :, :], in_=wax_d.ap()), writes=["WX"])

        A.reset(base1a)
        cT = A.alloc([128, 8], F32)
        cact = A.alloc([128, 8], F32)
        wm = A.alloc([128, 8, 3 * D], F32)
        tmp24 = A.alloc([128, 128], F32)
        dma("sp", ds_c, lambda e: e.dma_start(out=cT[:, :], in_=cT_d.ap()), writes=["cT"])
        for kc in range(8):
            dma("sp", ds_w, lambda e, kc=kc: e.dma_start(out=wm[:, kc, :], in_=DA(wmod_d, kc * 128 * 3 * D, [[3 * D, 128], [1, 3 * D]])), writes=["wm"])
        op("act", lambda e: e.activation(cact[:, :], cT[:, :], AF.Silu), reads=["cT"], writes=["cact"])
        for ft in range(24):
            for kc in range(8):
                op("pe", lambda e, ft=ft, kc=kc: e.matmul(PS[0][:, ft:ft + 1], wm[:, kc, ft * 128:(ft + 1) * 128], cact[:, kc:kc + 1],
                                                      start=(kc == 0), stop=(kc == 7)),
                   reads=["wm", "cact"], writes=["ps0"], inc=(ft == 23 and kc == 7))
        op("dve", lambda e: e.tensor_tensor(MT[:, :], PS[0][:, 0:24], sv[:, SV_BMOD:SV_BMOD + 24], ALU.add), reads=["ps0", "sv"], writes=["MT"])
        op("dve", lambda e: e.scalar_tensor_tensor(G3[:, 0:8], MT[:, 8:16], 1.0, sv[:, SV_GN:SV_GN + 8], ALU.add, ALU.mult), reads=["MT", "sv"], writes=["G3a"])
        op("dve", lambda e: e.tensor_copy(G3[:, 8:16], MT[:, 0:8]), reads=["MT"], writes=["G3b"])
        op("dve", lambda e: e.tensor_copy(G3[:, 16:24], MT[:, 16:24]), reads=["MT"], writes=["G3c"])
        op("pe", lambda e: e.transpose(PS[1][0:24, 0:128], G3[:, 0:24], identf[:, :]), reads=["G3a", "G3b", "G3c", "identf"], writes=["ps1"])
        op("dve", lambda e: e.tensor_copy(tmp24[0:24, :], PS[1][0:24, 0:128]), reads=["ps1"], writes=["tmp24"])
        dma("sp", ds_c, lambda e: e.dma_start(out=mod_d.ap(), in_=tmp24[0:24, :]), reads=["tmp24"], writes=["MODd"])
        dma("sp", ds_c, lambda e: e.dma_start(out=BC0[:, :], in_=DA(mod_d, 0, [[0, 128], [1, D]])), reads=["MODd"], writes=["BC0"])
        dma("sp", ds_c, lambda e: e.dma_start(out=BC1[:, :], in_=DA(mod_d, D, [[0, 128], [1, D]])), reads=["MODd"], writes=["BC1"])
        op("act", lambda e: e.activation(hc[:, :], sv[:, SV_LAM:SV_LAM + 8], AF.Exp, scale=-1.0), reads=["sv"], writes=["hc"])
        op("act", lambda e: e.activation(hc[:, :], hc[:, :], AF.Ln, bias=1.0), reads=["hc"], writes=["hc"])
        op("dve", lambda e: e.tensor_scalar(hc[:, :], hc[:, :], -4.0, None, ALU.mult), reads=["hc"], writes=["hc"])
        op("dve", lambda e: e.tensor_scalar(hba[:, :], sv[:, SV_BA:SV_BA + 8], 0.5, None, ALU.mult), reads=["sv"], writes=["hba"])
        op("dve", lambda e: e.tensor_scalar(hbx[:, :], sv[:, SV_BX:SV_BX + 8], 0.5, None, ALU.mult), reads=["sv"], writes=["hbx"])
        SCH.barrier()

        A.reset(base1a)
        xs = [A.alloc([128, D], F32) for _ in range(4)]
        sqs = A.alloc([128, D], BF16)
        ssq = [A.alloc([128, 4], F32) for _ in range(2)]
        rstd = [A.alloc([128, 4], F32) for _ in range(2)]
        tn = [A.alloc([128, D], F32) for _ in range(2)]
        hb = A.alloc([128, 4, D], BF16)
        hT = [A.alloc([128, 8, 512], BF16) for _ in range(2)]
        XR = A.alloc([128, 8, 515], F32)
        posi = A.alloc([128, 512], I32)
        nm0 = [A.alloc([128, 512], F32) for _ in range(3)]
        hst = A.alloc([128, 8], F32)
        NZU, NXC, NXB, NTR, NHH = 7, 6, 3, 5, 3
        ZU = [A.alloc([128, 512], F32) for _ in range(NZU)]
        XC = [A.alloc([128, 512], F32) for _ in range(NXC)]
        XCb = [A.alloc([128, 512], BF16) for _ in range(NXB)]
        TR = [A.alloc([128, 512], F32) for _ in range(NTR)]
        TI = [A.alloc([128, 512], F32) for _ in range(NTR)]
        NA2 = 4
        A2 = [A.alloc([128, 512], F32) for _ in range(NA2)]
        HH = [A.alloc([128, 512], F32) for _ in range(NHH)]
        RO = [A.alloc([128, 4, 512], BF16) for _ in range(2)]
        ds_x = [newds("x%d" % i) for i in range(4)]
        ds_ht = [newds("ht%d" % i) for i in range(2)]
        ds_ro = [newds("ro%d" % i) for i in range(2)]
        ds_pos = newds("pos")
        op("dve", lambda e: e.memset(XR[:, :, 0:3], 0.0), writes=["XRc%d" % cb for cb in range(8)])
        op("dve", lambda e: e.memset(hst[:, :], 0.0), writes=["hst%d" % cb for cb in range(8)])

        def pbank(lo, n, ctr):
            b = lo + ctr[0] % n
            ctr[0] += 1
            return b
        ctr_p, ctr_t, ctr_g = [0], [0], [0]

        def norm_piece(s, piece):
            t0 = s * 512
            hTs, hkey = hT[s % 2], "hT%d" % (s % 2)
            nm, nmk = nm0[s % 3], "nm0_%d" % (s % 3)
            sq_, rs_, sk = ssq[s % 2], rstd[s % 2], "%d" % (s % 2)
            if piece == 0:
                dma("sp", ds_pos, lambda e: e.dma_start(out=posi[:, :], in_=DA(pos_d, t0, [[0, 128], [1, 512]])), writes=["posi"])
                for tt in range(4):
                    j = s * 4 + tt
                    xsl, xk = xs[j % 4], "x%d" % (j % 4)
                    dma("sp", ds_x[j % 4], lambda e, xsl=xsl, tt=tt: e.dma_start(out=xsl[:, :], in_=DA(x_d, (t0 + 128 * tt) * D, [[D, 128], [1, D]])), writes=[xk])
            elif piece == 1:
                op("dve", lambda e: e.tensor_scalar(nm[:, :], posi[:, :], 0.0, None, ALU.not_equal), reads=["posi"], writes=[nmk])
                for tt in range(4):
                    j = s * 4 + tt
                    xsl, xk = xs[j % 4], "x%d" % (j % 4)
                    op("act", lambda e, xsl=xsl, tt=tt: e.activation(sqs[:, :], xsl[:, :], AF.Square, accum_out=sq_[:, tt:tt + 1]), reads=[xk], writes=["sqs", "ssq" + sk + "_%d" % tt])
            elif piece == 2:
                op("dve", lambda e: e.tensor_scalar(rs_[:, :], sq_[:, :], 1.0 / D, 1e-6, ALU.mult, ALU.add), reads=["ssq" + sk + "_%d" % i for i in range(4)], writes=["rstd" + sk])
                op("pool", lambda e: e.tensor_tensor(rs_[:, :], rs_[:, :], mhalf[:, 0:4], ALU.pow), reads=["rstd" + sk, "mhalf"], writes=["rstd" + sk])
            elif piece == 3:
                for tt in range(4):
                    j = s * 4 + tt
                    xsl, xk = xs[j % 4], "x%d" % (j % 4)
                    tns, tk = tn[tt % 2], "tn%d" % (tt % 2)
                    op("dve", lambda e, xsl=xsl, tns=tns, tt=tt: e.scalar_tensor_tensor(tns[:, :], xsl[:, :], rs_[:, tt:tt + 1], BC0[:, :], ALU.mult, ALU.mult),
                       reads=[xk, "rstd" + sk, "BC0"], writes=[tk])
                    op("pool", lambda e, tns=tns, tt=tt: e.tensor_tensor(hb[:, tt, :], tns[:, :], BC1[:, :], ALU.add), reads=[tk, "BC1"], writes=["hb%d" % tt])
            elif piece in (4, 5, 6):
                if piece in (5, 6):
                    for kc in range(4 * (piece - 5), 4 * (piece - 5) + 4):
                        b = kc % 4 // 2
                        o = (kc % 2) * 512
                        if kc % 4 != 3:
                            op("act", lambda e, b=b, o=o, kc=kc: e.activation(hTs[:, kc, :], PSB[b][:, o:o + 512], AF.Copy), reads=["ps%d" % b], writes=[hkey])
                        else:
                            op("dve", lambda e, b=b, o=o, kc=kc: e.tensor_copy(hTs[:, kc, :], PSB[b][:, o:o + 512]), reads=["ps%d" % b], writes=[hkey])
                if piece in (4, 5):
                    for kc in range(4 * (piece - 4), 4 * (piece - 4) + 4):
                        b = kc % 4 // 2
                        o = (kc % 2) * 512
                        for tt in range(4):
                            op("pe", lambda e, b=b, o=o, kc=kc, tt=tt: e.transpose(PSB[b][:, o + tt * 128:o + (tt + 1) * 128], hb[:, tt, kc * 128:(kc + 1) * 128], identb[:, :]),
                               reads=["hb%d" % tt, "identb"], writes=["ps%d" % b], inc=(tt == 3))
                if piece == 6:
                    dma("sp", ds_ht[s % 2], lambda e: e.dma_start(out=DA(ht_d, t0, [[S, 128], [128 * S, 8], [1, 512]]), in_=hTs[:, :, :]), reads=[hkey], writes=["HTd"])

        NIT = NSC * 8
        bx_of, bz_of, bgr_of, bgi_of = {}, {}, {}, {}

        def names(n):
            s, cb = n // 8, n % 8
            return dict(s=s, cb=cb, zu=ZU[n % NZU], zuk="ZU%d" % (n % NZU), xc=XC[n % NXC], xck="XC%d" % (n % NXC), xb=XCb[n % NXB], xbk="XCb%d" % (n % NXB),
                        tr=TR[n % NTR], trk="TR%d" % (n % NTR), ti=TI[n % NTR], tik="TI%d" % (n % NTR), a2=A2[n % NA2], a2k="A2%d" % (n % NA2),
                        hh=HH[n % NHH], hhk="HH%d" % (n % NHH))

        def pe_proj(n):
            s, cb = n // 8, n % 8
            hTs, hkey = hT[s % 2], "hT%d" % (s % 2)
            for which, store in ((0, bx_of), (1, bz_of)):
                b = pbank(2, 3, ctr_p)
                store[n] = b
                for kc in range(8):
                    op("pe", lambda e, b=b, kc=kc, which=which: e.matmul(PS[b][:, :], WAs[:, kc, which * 1024 + cb * 128:which * 1024 + (cb + 1) * 128], hTs[:, kc, :], start=(kc == 0), stop=(kc == 7)),
                       reads=[hkey, "WAs"], writes=["ps%d" % b], inc=(kc == 7))

        def pe_gates(n):
            d = names(n)
            cb = d["cb"]
            for which, store in ((0, bgr_of), (1, bgi_of)):
                b = pbank(5, 3, ctr_g)
                store[n] = b
                op("pe", lambda e, b=b, which=which: e.matmul(PS[b][:, :], WX[:, which * 8 + cb, :], d["xb"][:, :], start=True, stop=True), reads=[d["xbk"], "WX"], writes=["ps%d" % b])

        NIT = NSC * 8
        for pc in range(8):
            norm_piece(0, pc)
        for t in range(NIT + 7):
            if t < NIT and (t // 8 + 1) < NSC:
                norm_piece(t // 8 + 1, t % 8)
            if 0 <= t - 2 < NIT:
                pe_gates(t - 2)
            if t < NIT:
                pe_proj(t)
            if 0 <= t - 1 < NIT:
                d = names(t - 1); cb = d["cb"]; cw = SV_CW + cb * 4
                op("act", lambda e, d=d, cb=cb, cw=cw: e.activation(d["xc"][:, :], XR[:, cb, 3:515], AF.Identity, scale=sv[:, cw + 3:cw + 4], bias=sv[:, SV_CB + cb:SV_CB + cb + 1]),
                   reads=["XR%d" % cb, "sv"], writes=[d["xck"]])
                for k in range(3):
                    op("dve", lambda e, d=d, cb=cb, cw=cw, k=k: e.scalar_tensor_tensor(d["xc"][:, :], XR[:, cb, k:k + 512], sv[:, cw + k:cw + k + 1], d["xc"][:, :], ALU.mult, ALU.add),
                       reads=["XR%d" % cb, "XRc%d" % cb, d["xck"], "sv"], writes=[d["xck"]])
                op("dve", lambda e, d=d: e.tensor_copy(d["xb"][:, :], d["xc"][:, :]), reads=[d["xck"]], writes=[d["xbk"]])
            if 0 <= t - 3 < NIT:
                d = names(t - 3); cb = d["cb"]
                nm, nmk = nm0[d["s"] % 3], "nm0_%d" % (d["s"] % 3)
                op("act", lambda e, d=d, cb=cb: e.activation(d["tr"][:, :], d["tr"][:, :], AF.Exp, scale=hc[:, cb:cb + 1], bias=hc[:, cb:cb + 1]), reads=[d["trk"], "hc"], writes=[d["trk"]])
                op("pool", lambda e, d=d, nm=nm: e.tensor_tensor(d["tr"][:, :], d["tr"][:, :], nm[:, :], ALU.mult), reads=[d["trk"], nmk], writes=[d["trk"]])
            if 0 <= t - 2 < NIT:
                d = names(t - 2); cb = d["cb"]
                br, bi_ = bgr_of.pop(t - 2), bgi_of.pop(t - 2)
                op("act", lambda e, d=d, br=br, cb=cb: e.activation(d["tr"][:, :], PS[br][:, :], AF.Tanh, scale=0.5, bias=hba[:, cb:cb + 1]), reads=["ps%d" % br, "hba"], writes=[d["trk"]])
                op("act", lambda e, d=d, bi_=bi_, cb=cb: e.activation(d["ti"][:, :], PS[bi_][:, :], AF.Tanh, scale=0.5, bias=hbx[:, cb:cb + 1]), reads=["ps%d" % bi_, "hbx"], writes=[d["tik"]])
            if t < NIT:
                d = names(t); cb = d["cb"]
                bx, bz = bx_of.pop(t), bz_of.pop(t)
                op("act", lambda e, bx=bx, cb=cb: e.activation(XR[:, cb, 3:515], PS[bx][:, :], AF.Copy), reads=["ps%d" % bx], writes=["XR%d" % cb])
                op("act", lambda e, d=d, bz=bz: e.activation(d["zu"][:, :], PS[bz][:, :], AF.Tanh, scale=0.5), reads=["ps%d" % bz], writes=[d["zuk"]])
                op("dve", lambda e, d=d, bz=bz: e.scalar_tensor_tensor(d["zu"][:, :], d["zu"][:, :], 1.0, PS[bz][:, :], ALU.add, ALU.mult), reads=[d["zuk"], "ps%d" % bz], writes=[d["zuk"]])
            if 0 <= t - 1 < NIT:
                d = names(t - 1); cb = d["cb"]
                op("pool", lambda e, cb=cb: e.tensor_copy(XR[:, cb, 0:3], XR[:, cb, 512:515]), reads=["XR%d" % cb, "XRc%d" % cb], writes=["XRc%d" % cb])
            m = t - 3
            if 0 <= m < NIT and m % 4 == 3:
                for q_ in range(m - 3, m + 1):
                    d = names(q_)
                    op("act", lambda e, d=d: e.activation(d["a2"][:, :], d["tr"][:, :], AF.Square, scale=0.25), reads=[d["trk"]], writes=[d["a2k"]])
                for q_ in range(m - 3, m + 1):
                    d = names(q_)
                    op("act", lambda e, d=d: e.activation(d["a2"][:, :], d["a2"][:, :], AF.Sqrt, scale=-1.0, bias=0.0625), reads=[d["a2k"]], writes=[d["a2k"]])
            for q_ in ([t - 6] if 0 <= t - 6 < NIT else []):
                if True:
                    d = names(q_); cb = d["cb"]; s_ = d["s"]
                    half, g = cb // 4, cb % 4
                    ro, rok = RO[half], "RO%d" % half
                    op("dve", lambda e, d=d: e.scalar_tensor_tensor(d["ti"][:, :], d["ti"][:, :], 1.0, d["xc"][:, :], ALU.add, ALU.mult), reads=[d["tik"], d["xck"]], writes=[d["tik"]])
                    op("dve", lambda e, d=d: e.tensor_tensor(d["ti"][:, :], d["ti"][:, :], d["a2"][:, :], ALU.mult), reads=[d["tik"], d["a2k"]], writes=[d["tik"]])
                    op("dve", lambda e, d=d, cb=cb: e.tensor_tensor_scan(d["hh"][:, :], d["tr"][:, :], d["ti"][:, :], hst[:, cb:cb + 1], ALU.mult, ALU.add),
                       reads=[d["trk"], d["tik"], "hst%d" % cb], writes=[d["hhk"]])
                    op("dve", lambda e, d=d, cb=cb: e.tensor_copy(hst[:, cb:cb + 1], d["hh"][:, 511:512]), reads=[d["hhk"]], writes=["hst%d" % cb])
                    op("pool", lambda e, d=d, ro=ro, g=g: e.tensor_tensor(ro[:, g, :], d["hh"][:, :], d["zu"][:, :], ALU.mult), reads=[d["hhk"], d["zuk"]], writes=[rok])
                    if g == 3:
                        dma("sp", ds_ro[half], lambda e, ro=ro, half=half, s_=s_: e.dma_start(out=DA(rt_d, half * 4 * 128 * S + s_ * 512, [[S, 128], [128 * S, 4], [1, 512]]), in_=ro[:, :, :]),
                            reads=[rok], writes=["RTd"])
        print('P1A arena hi', A.pos)
        SCH.barrier()

        A.reset(base)
        WBs = A.alloc([128, 8, NWB], BF16)
        hTb = [A.alloc([128, 8, 512], BF16) for _ in range(2)]
        posf = A.alloc([128, 512], F32)
        ang = A.alloc([128, 512], F32)
        kki = A.alloc([128, 512], I32)
        kkf = A.alloc([128, 512], F32)
        rr = A.alloc([128, 512], F32)
        COS = A.alloc([128, 512], F32)
        SIN = A.alloc([128, 512], F32)
        ropef = [A.alloc([128, 512], F32) for _ in range(2)]
        partf = [A.alloc([128, 512], F32) for _ in range(2)]
        QO = [A.alloc([128, 8, 512], BF16) for _ in range(2)]
        ZO = [A.alloc([128, 8, 512], F32) for _ in range(2)]
        VO = [A.alloc([128, 8, 512], BF16) for _ in range(2)]
        ds_h = [newds("hb%d" % i) for i in range(2)]
        ds_q = [newds("qo%d" % i) for i in range(2)]
        ds_z = [newds("zo%d" % i) for i in range(2)]
        ds_v = [newds("vo%d" % i) for i in range(2)]
        WCH = 1536
        ds_wb = [newds("wb%d" % i) for i in range(NWB // WCH)]
        load_w(WBs, wb_d, NWB, 0, NWB, None, None, chunk=WCH, keyf=lambda c: "WBs%d" % (c // WCH), dsf=lambda c: ds_wb[c // WCH])
        ctr_p, ctr_z, ctr_v = [0], [0], [0]

        def proj(b, col0, hTs, hkey, M=128):
            for kc in range(8):
                op("pe", lambda e, b=b, kc=kc, col0=col0, hTs=hTs: e.matmul(PS[b][:, :], WBs[:, kc, col0:col0 + 128], hTs[:, kc, :], start=(kc == 0), stop=(kc == 7)),
                   reads=[hkey, "WBs%d" % (col0 // WCH)], writes=["ps%d" % b], inc=(kc == 7))

        posi2 = [A.alloc([128, 512], I32), A.alloc([128, 512], I32)]
        ds_pos2 = [ds_pos, newds("pos2")]

        def p1b_loads(s):
            hTs_, hkey_ = hTb[s % 2], "hTb%d" % (s % 2)
            dma("sp", ds_h[s % 2], lambda e: e.dma_start(out=hTs_[:, :, :], in_=DA(ht_d, s * 512, [[S, 128], [128 * S, 8], [1, 512]])), reads=["HTd"], writes=[hkey_])
            dma("sp", ds_pos2[s % 2], lambda e: e.dma_start(out=posi2[s % 2][:, :], in_=DA(pos_d, s * 512, [[0, 128], [1, 512]])), writes=["posi%d" % (s % 2)])

        p1b_loads(0)
        for s in range(NSC):
            t0 = s * 512
            hTs, hkey = hTb[s % 2], "hTb%d" % (s % 2)
            if s + 1 < NSC:
                p1b_loads(s + 1)
            op("dve", lambda e, s=s: e.tensor_copy(posf[:, :], posi2[s % 2][:, :]), reads=["posi%d" % (s % 2)], writes=["posf"])
            op("dve", lambda e: e.tensor_scalar(ang[:, :], posf[:, :], sv[:, SV_FREQ:SV_FREQ + 1], None, ALU.mult), reads=["posf", "sv"], writes=["ang"])
            for which in range(2):
                tab, tk = (SIN, "SIN") if which == 0 else (COS, "COS")
                op("dve", lambda e, which=which: e.tensor_scalar(kki[:, :], ang[:, :], 1.0 / TWO_PI, 0.25 * which, ALU.mult, ALU.add), reads=["ang"], writes=["kki"])
                op("dve", lambda e: e.tensor_copy(kkf[:, :], kki[:, :]), reads=["kki"], writes=["kkf"])
                op("dve", lambda e: e.scalar_tensor_tensor(rr[:, :], kkf[:, :], -C1, ang[:, :], ALU.mult, ALU.add), reads=["kkf", "ang"], writes=["rr"])
                op("dve", lambda e: e.scalar_tensor_tensor(rr[:, :], kkf[:, :], -C2, rr[:, :], ALU.mult, ALU.add), reads=["kkf", "rr"], writes=["rr"])
                op("dve", lambda e, which=which: e.tensor_scalar(rr[:, :], rr[:, :], 0.5 * np.pi * which, np.pi, ALU.add, ALU.min), reads=["rr"], writes=["rr"])
                op("dve", lambda e: e.tensor_scalar(rr[:, :], rr[:, :], -np.pi, None, ALU.max), reads=["rr"], writes=["rr"])
                if which == 0:
                    op("act", lambda e, tab=tab: e.activation(tab[:, :], rr[:, :], AF.Sin, scale=sv[:, SV_SIGN:SV_SIGN + 1]), reads=["rr", "sv"], writes=[tk])
                else:
                    op("act", lambda e, tab=tab: e.activation(tab[:, :], rr[:, :], AF.Sin), reads=["rr"], writes=[tk])
            zi = s % 2
            for h in range(8):
                b = pbank(2, 4, ctr_p)
                proj(b, 2560 + h * 128, hTs, hkey)
                op("act", lambda e, b=b, zi=zi, h=h: e.activation(ZO[zi][:, h, :], PS[b][:, :], AF.Silu), reads=["ps%d" % b], writes=["ZO%d" % zi])
            dma("sp", ds_z[zi], lambda e, zi=zi, t0=t0: e.dma_start(out=DA(zd_d, t0, [[S, 128], [128 * S, 8], [1, 512]]), in_=ZO[zi][:, :, :]), reads=["ZO%d" % zi], writes=["ZDd"])
            vi = s % 2
            for h in range(8):
                b = pbank(2, 4, ctr_p)
                proj(b, 3584 + h * 128, hTs, hkey)
                if h % 2 == 0:
                    op("act", lambda e, b=b, vi=vi, h=h: e.activation(VO[vi][:, h, :], PS[b][:, :], AF.Copy), reads=["ps%d" % b], writes=["VO%d" % vi])
                else:
                    op("dve", lambda e, b=b, vi=vi, h=h: e.tensor_copy(VO[vi][:, h, :], PS[b][:, :]), reads=["ps%d" % b], writes=["VO%d" % vi])
            dma("sp", ds_v[vi], lambda e, vi=vi, t0=t0: e.dma_start(out=DA(vt_d, t0, [[S, 128], [128 * S, 8], [1, 512]]), in_=VO[vi][:, :, :]), reads=["VO%d" % vi], writes=["VTd"])
            for qk in range(2):
                cbase = qk * 1280
                qo, qok = QO[qk], "QO%d" % qk
                for g in range(2):
                    b = pbank(2, 4, ctr_p)
                    proj(b, cbase + g * 128, hTs, hkey)
                    b2 = pbank(2, 4, ctr_p)
                    proj(b2, cbase + 256 + g * 128, hTs, hkey)
                    rf, rk = ropef[g], "ropef%d" % g
                    pf, pk = partf[g], "partf%d" % g
                    op("dve", lambda e, b=b, rf=rf: e.tensor_tensor(rf[:, :], PS[b][:, :], COS[:, :], ALU.mult), reads=["ps%d" % b, "COS"], writes=[rk])
                    op("dve", lambda e, b2=b2, pf=pf: e.tensor_tensor(pf[:, :], PS[b2][:, :], SIN[:, :], ALU.mult), reads=["ps%d" % b2, "SIN"], writes=[pk])
                    op("pool", lambda e, rf=rf, pf=pf, qo=qo, g=g: e.tensor_tensor(qo[:, g, :], rf[:, :], pf[:, :], ALU.add), reads=[rk, pk], writes=[qok])
                for t in range(6):
                    b = pbank(2, 4, ctr_p)
                    proj(b, cbase + 512 + t * 128, hTs, hkey)
                    if t % 2 == 0:
                        op("act", lambda e, b=b, qo=qo, t=t: e.activation(qo[:, 2 + t, :], PS[b][:, :], AF.Copy), reads=["ps%d" % b], writes=[qok])
                    else:
                        op("dve", lambda e, b=b, qo=qo, t=t: e.tensor_copy(qo[:, 2 + t, :], PS[b][:, :]), reads=["ps%d" % b], writes=[qok])
                dst = qt_d if qk == 0 else kt_d
                dkey = "QTd" if qk == 0 else "KTd"
                dma("sp", ds_q[qk], lambda e, qo=qo, dst=dst, t0=t0: e.dma_start(out=DA(dst, t0, [[S, 128], [128 * S, 8], [1, 512]]), in_=qo[:, :, :]), reads=[qok], writes=[dkey])
        print('P1B arena hi', A.pos)
        SCH.barrier()

        A.reset(base)
        KTs = [A.alloc([128, S], BF16) for _ in range(2)]
        VTs = [A.alloc([128, S], BF16) for _ in range(2)]
        NV1, NV2, NV3 = 16, 16, 48
        V1r = A.alloc([128, NV1, 128], BF16)
        V2r = A.alloc([128, NV2, 128], BF16)
        V3r = A.alloc([128, NV3, 128], BF16)
        Qs = A.alloc([128, 8, 512], BF16)
        NZ = 4
        Zs = [A.alloc([128, 512], F32) for _ in range(NZ)]
        NPT = 8
        PT = [A.alloc([128, 512], BF16) for _ in range(NPT)]
        RD = [A.alloc([128, 512], F32) for _ in range(2)]
        OT = [A.alloc([128, 512], F32) for _ in range(2)]
        AO = [A.alloc([128, 512], BF16) for _ in range(2)]
        O3buf = A.alloc([128, 2048], F32)
        D3buf = A.alloc([128, 2048], F32)
        p2_hi = A.pos
        W3TOP = ARENA_WORDS - 12288
        assert p2_hi <= W3TOP, (p2_hi, W3TOP)
        A.reset(W3TOP)
        WGs = A.alloc([128, 8, 2048], BF16)
        WRs = A.alloc([128, 8, D], BF16)
        ds_w3 = newds("w3")
        load_w(WGs, wg_d, 2048, 0, 2048, ds_w3, "WGs")
        load_w(WRs, wor_d, D, 0, D, ds_w3, "WRs")
        A.reset(base)
        WAt = A.alloc([128, 8, D], BF16)
        assert A.pos == base + 4096
        A.reset(base + 8192)
        WOs = A.alloc([128, 8, D], BF16)
        A.reset(p2_hi)
        ds_k = [newds("k%d" % i) for i in range(2)]
        ds_qs = [newds("qsp%d" % i) for i in range(2)]
        ds_zs = [newds("zs%d" % i) for i in range(NZ)]
        ds_ao = [newds("ao%d" % i) for i in range(2)]
        ctr_b, ctr_pt, ctr_u = [0], [0], [0]
        inv_sqrt = 1.0 / np.sqrt(128.0)
        NSP = S // 2048
        LAG = 4

        def load_head(h):
            hs = h % 2
            Kh, kk = KTs[hs], "K%d" % hs
            g, j = h // 4, h % 4
            dma("sp", ds_k[hs], lambda e: e.dma_start(out=Kh[0:32, :], in_=DA(kt_d, (g * 128 + 32 * j) * S, [[S, 32], [1, S]])), reads=["KTd"], writes=[kk])
            dma("sp", ds_k[hs], lambda e: e.dma_start(out=Kh[32:128, :], in_=DA(kt_d, (256 + 96 * h) * S, [[S, 96], [1, S]])), reads=["KTd"], writes=[kk])
            dma("sp", ds_k[hs], lambda e: e.dma_start(out=VTs[hs][:, :], in_=DA(vt_d, h * 128 * S, [[S, 128], [1, S]])), reads=["VTd"], writes=[kk])

        def load_qspan(h, n):
            sl = (4 * n) % 8
            qk_ = "Qsp%d" % (n % 2)
            g, j = h // 4, h % 4
            dma("sp", ds_qs[n % 2], lambda e: e.dma_start(out=Qs.ap(p0=0, pn=32, dims=[(1, 2048)], off=sl * 512), in_=DA(qt_d, (g * 128 + 32 * j) * S + 2048 * n, [[S, 32], [1, 2048]])), reads=["QTd"], writes=[qk_])
            dma("sp", ds_qs[n % 2], lambda e: e.dma_start(out=Qs.ap(p0=32, pn=96, dims=[(1, 2048)], off=sl * 512), in_=DA(qt_d, (256 + 96 * h) * S + 2048 * n, [[S, 96], [1, 2048]])), reads=["QTd"], writes=[qk_])

        def load_z(h, s):
            zi = s % NZ
            dma("sp", ds_zs[zi], lambda e: e.dma_start(out=Zs[zi][:, :], in_=DA(zd_d, h * 128 * S + s * 512, [[S, 128], [1, 512]])), reads=["ZDd"], writes=["Z%d" % zi])

        load_head(0)
        load_qspan(0, 0)
        for h in range(8):
            hs = h % 2
            Kh, kk = KTs[hs], "K%d" % hs
            VTh = VTs[hs]

            def vgen(ring, rk, idx0, kps, kst, VTh=VTh, kk=kk):
                for c, kp in enumerate(kps):
                    op("pe", lambda e, c=c, kp=kp: e.transpose(PSB[3][:, c * 128:(c + 1) * 128], VTh.ap(dims=[(kst, 128)], off=kp), identb[:, :]),
                       reads=[kk, "identb"], writes=["ps3"], inc=(c == len(kps) - 1))
                n_ = len(kps)
                op("dve", lambda e: e.tensor_copy(ring[:, idx0:idx0 + n_, :], PSB[3][:, 0:128 * n_]), reads=["ps3"], writes=[rk])

            def vgen_sub(s, which):
                if which == 0:
                    vgen(V1r, "V1r", (4 * s) % NV1, [512 * s + 128 * c for c in range(4)], 1)
                elif which == 1:
                    vgen(V2r, "V2r", (4 * s) % NV2, [512 * s + r for r in range(4)], 4)
                else:
                    n_, qd = s // 4 + 1, s % 4
                    if n_ < NSP:
                        vgen(V3r, "V3r", (16 * n_ + 4 * qd) % NV3, [2048 * n_ + 4 * qd + r for r in range(4)], 16)

            def vgen12(s, VTh=VTh, kk=kk):
                kps = [(512 * s + 128 * c, 1) for c in range(4)] + [(512 * s + r, 4) for r in range(4)]
                for c, (kp, kst) in enumerate(kps):
                    op("pe", lambda e, c=c, kp=kp, kst=kst: e.transpose(PSB[3][:, c * 128:(c + 1) * 128], VTh.ap(dims=[(kst, 128)], off=kp), identb[:, :]),
                       reads=[kk, "identb"], writes=["ps3"], inc=(c == 7))
                i1, i2 = (4 * s) % NV1, (4 * s) % NV2
                op("dve", lambda e: e.tensor_copy(V1r[:, i1:i1 + 4, :], PSB[3][:, 0:512]), reads=["ps3"], writes=["V1r"])
                op("dve", lambda e: e.tensor_copy(V2r[:, i2:i2 + 4, :], PSB[3][:, 512:1024]), reads=["ps3"], writes=["V2r"])

            v1, v2, v3 = (V1r, NV1), (V2r, NV2), (V3r, NV3)
            if h + 1 < 8:
                load_head(h + 1)
            EARLY3 = (S // 2 >= 4096)
            if h == 7 and EARLY3:
                load_w(WAt, woa_d, D, 0, D, ds_w3, "WAt", extra=["K0"])
                load_w(WOs, wo_d, D, 0, D, ds_w3, "WOs", extra=["K0"])
            vgen12(0)
            for qd in range(4):
                vgen(V3r, "V3r", 4 * qd, [4 * qd + r for r in range(4)], 16)
            jobs = []
            for n in range(NSP):
                qsl = ((4 * n) % 8) * 512
                for g in range(4):
                    banks = [(0, 0, [((128 * c, 1, 128), (qsl + 4 * g + c, 16, 128), (2048 * n + 4 * g + c, 16), (v3, 16 * n + 4 * g + c)) for c in range(4)])]
                    if n > 0:
                        banks.append((1, 0, [((128 * c, 1, 128), (qsl + 4 * g + c, 16, 128), (2048 * (n - 1) + 4 * g + c, 16), (v3, 16 * (n - 1) + 4 * g + c)) for c in range(4)]))
                    for bi, bk in enumerate(banks):
                        jobs.append((("g3", n, g), bi, len(banks), bk))
                for sp in range(4):
                    s = 4 * n + sp
                    t0 = s * 512
                    qb = (s % 8) * 512
                    banks = []
                    banks.append((0, 0, [((128 * c, 1, 128), (qb + 128 * c, 1, 128), (t0 + 128 * c, 1), (v1, 4 * s + c)) for c in range(4)]))
                    c0 = 1 if s == 0 else 0
                    banks.append((1, 128 * c0, [((128 * c, 1, 128), (qb + 128 * c, 1, 128), (t0 + 128 * (c - 1), 1), (v1, 4 * s + c - 1)) for c in range(c0, 4)]))
                    banks.append((2, 0, [((r, 4, 128), (qb + r, 4, 128), (t0 + r, 4), (v2, 4 * s + r)) for r in range(4)]))
                    if s > 0:
                        banks.append((3, 0, [((r, 4, 128), (qb + r, 4, 128), (t0 - 512 + r, 4), (v2, 4 * (s - 1) + r)) for r in range(4)]))
                    for bi, bk in enumerate(banks):
                        jobs.append((("sc", s, sp), bi, len(banks), bk))
            load_z(h, 0)
            if NSC > 1:
                load_z(h, 1)
            pend = []
            for i in range(len(jobs) + LAG):
                if i < len(jobs):
                    (unit, bi, nbk, (mi, col0, items)) = jobs[i]
                    if bi == 0:
                        ctr_u[0] += 1
                    upar = ctr_u[0] % 2
                    if unit[0] == "sc":
                        s = unit[1]
                        qk_ = "Qsp%d" % ((s // 4) % 2)
                        if bi == 0 and s + 2 < NSC:
                            load_z(h, s + 2)
                        if bi == 0 and unit[2] == 0:
                            if s // 4 + 1 < NSP:
                                load_qspan(h, s // 4 + 1)
                            elif h + 1 < 8:
                                load_qspan(h + 1, 0)
                        if bi == 0 and s + 1 < NSC:
                            vgen12(s + 1)
                        if bi == 2 and unit[2] % 2 == 0 and s // 4 + 1 < NSP:
                            n_, hf_ = s // 4 + 1, unit[2] // 2
                            vgen(V3r, "V3r", (16 * n_ + 8 * hf_) % NV3, [2048 * n_ + 8 * hf_ + r for r in range(8)], 16)
                    else:
                        qk_ = "Qsp%d" % (unit[1] % 2)
                    b = ctr_b[0] % 3
                    ctr_b[0] += 1
                    for ii, (pd, qd_, (kp, kst), vt) in enumerate(items):
                        op("pe", lambda e, b=b, pd=pd, qd_=qd_, kp=kp, kst=kst, Kh=Kh, ii=ii: e.matmul(
                            PS[b].ap(dims=[(pd[1], pd[2])], off=pd[0]), Kh.ap(dims=[(kst, 128)], off=kp), Qs.ap(dims=[(qd_[1], qd_[2])], off=qd_[0]),
                            start=(ii == 0), stop=False, skip_group_check=True),
                           reads=[kk, qk_], writes=["ps%d" % b], inc=False)
                    op("pe", lambda e, b=b, col0=col0, mi=mi: e.matmul(PS[b][:, col0:512], identb[:, :], masks[:, mi, col0:512], start=False, stop=True, skip_group_check=True),
                       reads=["masks", "identb"], writes=["ps%d" % b], inc=True)
                    pi_ = ctr_pt[0] % NPT
                    ctr_pt[0] += 1
                    P, pk_ = PT[pi_], "PT%d" % pi_
                    op("act", lambda e, b=b, P=P, col0=col0: e.activation(P[:, col0:512], PS[b][:, col0:512], AF.Exp, scale=inv_sqrt), reads=["ps%d" % b], writes=[pk_])
                    pend.append((unit, upar, bi, nbk, P, pk_, col0, items))
                j2 = i - LAG
                if j2 >= 0:
                    (unit, upar, bi, nbk, P, pk_, col0, items) = pend[j2]
                    ob = 4 + 2 * upar
                    for ii, (pd, qd_, (kp, kst), (vv, vt)) in enumerate(items):
                        op("pe", lambda e, P=P, pd=pd, vv=vv, vt=vt, first=(bi == 0 and ii == 0), last=(bi == nbk - 1 and ii == len(items) - 1), ob=ob: e.matmul(
                            PS[ob].ap(dims=[(pd[1], pd[2])], off=pd[0]), vv[0][:, vt % vv[1], :], P.ap(dims=[(pd[1], pd[2])], off=pd[0]), start=first, stop=last, skip_group_check=True),
                           reads=[pk_, "V1r", "V2r", "V3r"], writes=["ps%d" % ob], inc=False)
                    op("pe", lambda e, P=P, col0=col0, bi=bi, nbk=nbk, ob=ob: e.matmul(PS[ob + 1][:, col0:512], onesb[:, :], P[:, col0:512], start=(bi == 0), stop=(bi == nbk - 1), skip_group_check=True),
                       reads=[pk_, "onesb"], writes=["ps%d" % (ob + 1)], inc=True)
                    if bi == nbk - 1:
                        if unit[0] == "g3":
                            g = unit[2]
                            op("dve", lambda e, ob=ob, g=g: e.tensor_copy(O3buf.ap(dims=[(1, 4), (16, 128)], off=4 * g), PS[ob][:, :]), reads=["ps%d" % ob], writes=["O3buf"])
                            op("act", lambda e, ob=ob, g=g: e.activation(D3buf.ap(dims=[(1, 4), (16, 128)], off=4 * g), PS[ob + 1][:, :], AF.Copy), reads=["ps%d" % (ob + 1)], writes=["D3buf"])
                        else:
                            s, sp = unit[1], unit[2]
                            zi = s % NZ
                            Z, zk = Zs[zi], "Z%d" % zi
                            t0 = s * 512
                            par = upar
                            op("dve", lambda e, ob=ob, par=par, sp=sp: e.tensor_tensor(RD[par][:, :], PS[ob + 1][:, :], D3buf[:, 512 * sp:512 * (sp + 1)], ALU.add), reads=["ps%d" % (ob + 1), "D3buf"], writes=["RD%d" % par])
                            op("dve", lambda e, ob=ob, par=par, sp=sp: e.tensor_tensor(OT[par][:, :], PS[ob][:, :], O3buf[:, 512 * sp:512 * (sp + 1)], ALU.add), reads=["ps%d" % ob, "O3buf"], writes=["OT%d" % par])
                            op("act", lambda e, par=par: e.activation(RD[par][:, :], RD[par][:, :], AF.Ln), reads=["RD%d" % par], writes=["RD%d" % par])
                            op("act", lambda e, par=par: e.activation(RD[par][:, :], RD[par][:, :], AF.Exp, scale=-1.0), reads=["RD%d" % par], writes=["RD%d" % par])
                            op("pool", lambda e, par=par: e.tensor_tensor(OT[par][:, :], OT[par][:, :], RD[par][:, :], ALU.mult), reads=["OT%d" % par, "RD%d" % par], writes=["OT%d" % par])
                            op("pool", lambda e, par=par, Z=Z: e.tensor_tensor(AO[par][:, :], OT[par][:, :], Z[:, :], ALU.mult), reads=["OT%d" % par, zk], writes=["AO%d" % par])
                            dma("sp", ds_ao[par], lambda e, par=par, h=h, t0=t0: e.dma_start(out=DA(at_d, h * 128 * S + t0, [[S, 128], [1, 512]]), in_=AO[par][:, :]), reads=["AO%d" % par], writes=["ATd"])
        print('P2 arena hi', A.pos)
        SCH.barrier()

        A.reset(base)
        _wat = A.alloc([128, 8, D], BF16)
        hT3 = [A.alloc([128, 8, 512], BF16) for _ in range(2)]
        assert A.pos == base + 8192
        _wos = A.alloc([128, 8, D], BF16)
        if not EARLY3:
            load_w(WAt, woa_d, D, 0, D, ds_w, "WAt")
            load_w(WOs, wo_d, D, 0, D, ds_w, "WOs")
        AT3 = [A.alloc([128, 8, 512], BF16) for _ in range(2)]
        RT3 = [A.alloc([128, 8, 512], BF16) for _ in range(2)]
        SG = [A.alloc([128, 512], F32) for _ in range(2)]
        M1 = [A.alloc([128, 512], F32) for _ in range(2)]
        M2 = [A.alloc([128, 512], F32) for _ in range(2)]
        MGs = [A.alloc([128, 8, 512], BF16) for _ in range(2)]
        X3 = [A.alloc([128, D], F32) for _ in range(2)]
        Y3 = [A.alloc([128, D], F32) for _ in range(2)]
        ssq3 = A.alloc([128, 2], F32)
        ds_in3 = [newds("in3_%d" % i) for i in range(2)]
        ds_x3 = [newds("x3_%d" % i) for i in range(2)]
        ds_o3 = [newds("o3_%d" % i) for i in range(2)]
        dma("sp", ds_c, lambda e: e.dma_start(out=BC0[:, :], in_=DA(mod_d, 2 * D, [[0, 128], [1, D]])), reads=["MODd"], writes=["BC0"])
        dma("sp", ds_c, lambda e: e.dma_start(out=BC1[:, :], in_=DA(gfin_d, 0, [[0, 128], [1, D]])), writes=["BC1"])

        ctr_p, ctr_w, ctr_x, ctr_m, ctr_sg = [0], [0], [0], [0], [0]
        SGr = [A.alloc([128, 512], F32) for _ in range(2)]

        def p3_ft(s):
            t0 = s * 512
            si = s % 2
            hTs, ATs, RTs = hT3[si], AT3[si], RT3[si]
            ik = "in3_%d" % si
            MG, mgk = MGs[si], "MG%d" % si
            for ft in range(8):
                mi_ = ctr_m[0] % 2
                ctr_m[0] += 1
                for br in range(2):
                    pr = ctr_p[0] % 3
                    ctr_p[0] += 1
                    bg, by = 2 * pr, 2 * pr + 1
                    Wy, Ys = (WRs, RTs) if br == 0 else (WAt, ATs)
                    wyk = "WRs" if br == 0 else "WAt"
                    for kc in range(8):
                        op("pe", lambda e, kc=kc, bg=bg, br=br, ft=ft: e.matmul(PS[bg][:, :], WGs[:, kc, br * 1024 + ft * 128:br * 1024 + (ft + 1) * 128], hTs[:, kc, :], start=(kc == 0), stop=(kc == 7)),
                           reads=[ik, "WGs"], writes=["ps%d" % bg], inc=(kc == 7))
                    for kc in range(8):
                        op("pe", lambda e, kc=kc, by=by, Wy=Wy, Ys=Ys, ft=ft: e.matmul(PS[by][:, :], Wy[:, kc, ft * 128:(ft + 1) * 128], Ys[:, kc, :], start=(kc == 0), stop=(kc == 7)),
                           reads=[ik, wyk], writes=["ps%d" % by], inc=(kc == 7))
                    sgi = ctr_sg[0] % 4
                    ctr_sg[0] += 1
                    sg, sgk = (SG + SGr)[sgi], "SG%d" % sgi
                    Mx, mk = (M1, "M1_%d" % mi_) if br == 0 else (M2, "M2_%d" % mi_)
                    op("act", lambda e, bg=bg, sg=sg, br=br, ft=ft: e.activation(sg[:, :], PS[bg][:, :], AF.Sigmoid, bias=sv[:, SV_BG + 8 * br + ft:SV_BG + 8 * br + ft + 1]), reads=["ps%d" % bg, "sv"], writes=[sgk])
                    op("dve", lambda e, by=by, sg=sg, Mx=Mx, mi_=mi_: e.tensor_tensor(Mx[mi_][:, :], PS[by][:, :], sg[:, :], ALU.mult), reads=["ps%d" % by, sgk], writes=[mk])
                op("pool", lambda e, ft=ft, mi_=mi_: e.tensor_tensor(MG[:, ft, :], M1[mi_][:, :], M2[mi_][:, :], ALU.add), reads=["M1_%d" % mi_, "M2_%d" % mi_], writes=[mgk])

        pendB = []

        def p3_B(xi, X, Y, row0):
            op("dve", lambda e: e.scalar_tensor_tensor(Y[:, :], X[:, :], ssq3[:, xi:xi + 1], BC1[:, :], ALU.mult, ALU.mult), reads=["X3_%d" % xi, "ssq3_%d" % xi, "BC1", "Y3_%d" % xi], writes=["Y3_%d" % xi])
            dma("sp", ds_o3[xi], lambda e: e.dma_start(out=DA(out_d, row0 * D, [[D, 128], [1, D]]), in_=Y[:, :]), reads=["Y3_%d" % xi], writes=["OUTd"])

        def p3_wo(s):
            t0 = s * 512
            si = s % 2
            MG, mgk = MGs[si], "MG%d" % si
            for tt in range(4):
                xi = ctr_x[0] % 2
                ctr_x[0] += 1
                X, Y = X3[xi], Y3[xi]
                dma("sp", ds_x3[xi], lambda e, X=X, tt=tt: e.dma_start(out=X[:, :], in_=DA(x_d, (t0 + 128 * tt) * D, [[D, 128], [1, D]])), writes=["X3_%d" % xi])
                for hf in range(2):
                    b = 6 + ctr_w[0] % 2
                    ctr_w[0] += 1
                    for kc in range(8):
                        op("pe", lambda e, b=b, kc=kc, tt=tt, hf=hf: e.matmul(PS[b][:, :], MG[:, kc, tt * 128:(tt + 1) * 128], WOs[:, kc, hf * 512:(hf + 1) * 512], start=(kc == 0), stop=(kc == 7)),
                           reads=[mgk, "WOs"], writes=["ps%d" % b], inc=(kc == 7))
                    op("dve", lambda e, b=b, hf=hf, Y=Y: e.tensor_tensor(Y[:, hf * 512:(hf + 1) * 512], PS[b][:, :], BC0[:, hf * 512:(hf + 1) * 512], ALU.mult), reads=["ps%d" % b, "BC0"], writes=["Y3_%d" % xi])
                op("pool", lambda e, X=X, Y=Y: e.tensor_tensor(X[:, :], X[:, :], Y[:, :], ALU.add), reads=["X3_%d" % xi, "Y3_%d" % xi], writes=["X3_%d" % xi])
                op("act", lambda e, X=X, Y=Y, xi=xi: e.activation(Y[:, :], X[:, :], AF.Square, accum_out=ssq3[:, xi:xi + 1]), reads=["X3_%d" % xi, "Y3_%d" % xi], writes=["Y3_%d" % xi, "ssq3_%d" % xi])
                op("pool", lambda e, xi=xi: e.tensor_scalar(ssq3[:, xi:xi + 1], ssq3[:, xi:xi + 1], 1.0 / D, 1e-6, ALU.mult, ALU.add), reads=["ssq3_%d" % xi], writes=["ssq3_%d" % xi])
                op("pool", lambda e, xi=xi: e.tensor_tensor(ssq3[:, xi:xi + 1], ssq3[:, xi:xi + 1], mhalf[:, 0:1], ALU.pow), reads=["ssq3_%d" % xi, "mhalf"], writes=["ssq3_%d" % xi])
                if pendB:
                    p3_B(*pendB.pop(0))
                pendB.append((xi, X, Y, t0 + 128 * tt))

        def p3_loads(s):
            t0 = s * 512
            si = s % 2
            ik = "in3_%d" % si
            for (dst_, src_, dk) in ((hT3[si], ht_d, "HTd"), (AT3[si], at_d, "ATd"), (RT3[si], rt_d, "RTd")):
                dma("sp", ds_in3[si], lambda e, dst_=dst_, src_=src_: e.dma_start(out=dst_[:, :, :], in_=DA(src_, t0, [[S, 128], [128 * S, 8], [1, 512]])), reads=[dk], writes=[ik])

        p3_loads(0)
        if NSC > 1:
            p3_loads(1)
        p3_ft(0)
        for s in range(NSC):
            if s + 1 < NSC:
                p3_ft(s + 1)
            if s + 2 < NSC:
                p3_loads(s + 2)
            p3_wo(s)
        while pendB:
            p3_B(*pendB.pop(0))
        print('P3 arena hi', A.pos, 'of', ARENA_WORDS)
        assert A.pos <= W3TOP
        SCH.final_wait("sp")
        with nc.Block() as block:
            SCH.emit(block)
    return nc


def _layouts(w_in, conv_w, conv_b, w_a, b_a, w_x, b_x, lam, b_gate, b_mod, g_norm):
    wi = w_in
    x_rnn, z_rnn, q, k, v, z_attn, g_r, g_a = [wi[:, i * 1024:(i + 1) * 1024] for i in range(8)]
    WA = np.ascontiguousarray(np.concatenate([x_rnn, z_rnn], axis=1))

    def qk_cols(w):
        hd = w.reshape(1024, 8, 128)
        rope = hd[:, :, 0:32].reshape(1024, 256)
        partner = np.concatenate([hd[:, :, 16:32], hd[:, :, 0:16]], axis=2).reshape(1024, 256)
        pas = hd[:, :, 32:128].reshape(1024, 768)
        return [rope, partner, pas]
    WB = np.ascontiguousarray(np.concatenate(qk_cols(q) + qk_cols(k) + [z_attn, v], axis=1))
    WG = np.ascontiguousarray(np.concatenate([g_r, g_a], axis=1))
    sv = np.zeros((128, NSV), np.float32)
    sv[:, SV_BMOD:SV_BMOD + 24] = b_mod.reshape(24, 128).T
    sv[:, SV_GN:SV_GN + 8] = g_norm.reshape(8, 128).T
    sv[:, SV_CW:SV_CW + 32] = conv_w.reshape(4, 8, 128).transpose(2, 1, 0).reshape(128, 32)
    sv[:, SV_CB:SV_CB + 8] = conv_b.reshape(8, 128).T
    sv[:, SV_BA:SV_BA + 8] = b_a.T
    sv[:, SV_BX:SV_BX + 8] = b_x.T
    sv[:, SV_LAM:SV_LAM + 8] = lam.reshape(8, 128).T
    sv[:, SV_BG:SV_BG + 16] = b_gate.reshape(16, 128).T
    wax = np.ascontiguousarray(np.concatenate([w_a, w_x], axis=0).transpose(1, 0, 2).reshape(128, 16 * 128))
    return WA, WB, WG, sv, wax


def _consts():
    p = np.arange(128)
    freq = (500000.0 ** (-(np.arange(0, 32, 2, dtype=np.float32)) / 32.0)).astype(np.float32)
    fcol = freq[p % 16]
    sign = np.where((p % 32) < 16, -1.0, 1.0).astype(np.float32)
    k = np.arange(128)[:, None]
    col = np.arange(512)[None, :]
    m = np.zeros((128, 12, 512), np.float32)
    m[:, 0] = (col % 128) >= k
    m[:, 1] = (col % 128) <= k
    m[:, 2] = (col // 4) >= k
    m[:, 3] = (col // 4) <= k
    for sp in range(4):
        m[:, 4 + sp] = (32 * sp + col // 16) >= k
        m[:, 8 + sp] = (32 * sp + col // 16) <= k
    m = (m - 1.0) * 30000.0
    return fcol, sign, np.ascontiguousarray(m.reshape(128, 12 * 512)), np.eye(128, dtype=np.float32)


def make_in_maps(S, nb, x, c, positions, g_norm, w_mod, b_mod, w_in, b_gate, conv_w, conv_b, w_a, b_a, w_x, b_x, lam,
                 w_out_rnn, w_out_attn, w_o, g_final):
    f = lambda a: np.ascontiguousarray(np.asarray(a), dtype=np.float32)
    WA, WB, WG, sv, wax = _layouts(f(w_in[0]), f(conv_w[0]), f(conv_b[0]), f(w_a[0]), f(b_a[0]), f(w_x[0]), f(b_x[0]), f(lam[0]),
                                   f(b_gate[0]), f(b_mod[0]), f(g_norm[0]))
    fcol, sign, masks, ident = _consts()
    sv[:, SV_FREQ] = fcol
    sv[:, SV_SIGN] = sign
    shared = {"w_mod": f(w_mod[0]), "sv": sv, "WA": WA, "WB": WB, "WG": WG, "wor": f(w_out_rnn[0]), "woa": f(w_out_attn[0]),
              "wo": f(w_o[0]), "wax": wax, "gfin": f(g_final).reshape(1, D), "ident": ident, "masks": masks}
    x = np.asarray(x)
    c = np.asarray(c)
    positions = np.asarray(positions)
    maps = []
    for b in range(nb):
        m = dict(shared)
        m["x"] = np.ascontiguousarray(x[b], dtype=np.float32)
        m["pos"] = np.ascontiguousarray(positions[b].reshape(1, S), dtype=np.int32)
        m["cT"] = np.ascontiguousarray(c[b].astype(np.float32).reshape(8, 128).T)
        maps.append(m)
    return maps


def kernel(**inputs):
    x = np.asarray(inputs["x"])
    B, S, _ = x.shape
    nc = build(S)
    maps = make_in_maps(S, B, **inputs)
    res = run_bass_kernel_spmd(nc, maps, core_ids=list(range(B)))
    return np.stack([np.asarray(r["out"]).reshape(S, D) for r in res.results], axis=0).astype(np.float32)
```

```python
import numpy as np
import concourse.bass as bass
import concourse.mybir as mybir
from concourse.bass_utils import run_bass_kernel_spmd

F32 = mybir.dt.float32
BF16 = mybir.dt.bfloat16
I32 = mybir.dt.int32
AF = mybir.ActivationFunctionType
ALU = mybir.AluOpType

D = 1024
SEQ = 8192
NB = 8
TWO_PI = 6.283185307179586
C1 = 6.28125
C2 = TWO_PI - C1
ENGS = ("pe", "act", "dve", "pool", "sp")


class DSem:
    def __init__(self, sem, name):
        self.sem = sem
        self.count = 0
        self.name = name


class Sched:
    def __init__(self, nc, esems):
        self.nc = nc
        self.ops = {e: [] for e in ENGS}
        self.cnt = {e: 0 for e in ENGS}
        self.pending = {e: False for e in ENGS}
        self.esem = esems
        self.known = {e: {} for e in ENGS}
        self.lw = {}
        self.rd = {}
        self.dsems = []

    def dsem(self, sem, name):
        d = DSem(sem, name)
        self.dsems.append(d)
        return d

    def _tok_val(self, tok):
        if tok[0] == "eng":
            return ("e", tok[1]), self.esem[tok[1]], tok[2]
        d = tok[1]
        return ("d", d.name), d.sem, d.count

    def _deps(self, e, reads, writes):
        toks = []
        for k in reads:
            t = self.lw.get(k)
            if t is not None:
                toks.append(t)
        for k in writes:
            t = self.lw.get(k)
            if t is not None:
                toks.append(t)
            toks.extend(self.rd.get(k, ()))
        need = {}
        for t in toks:
            if t[0] == "eng" and t[1] == e and e == "pe":
                continue
            key, sem, val = self._tok_val(t)
            if val <= 0:
                continue
            if need.get(key, (None, 0))[1] < val:
                need[key] = (sem, val)
        for key, (sem, val) in need.items():
            if self.known[e].get(key, 0) >= val:
                continue
            self.known[e][key] = val
            if key == ("e", e) and val > self.cnt[e]:
                raise RuntimeError("same-engine dep on un-incremented instruction (%s)" % e)
            self.ops[e].append(lambda eng, sem=sem, val=val: eng.wait_ge(sem, val))

    def _commit(self, tok, reads, writes):
        for k in writes:
            self.lw[k] = tok
            self.rd[k] = []
        for k in reads:
            self.rd.setdefault(k, []).append(tok)

    def op(self, e, fn, reads=(), writes=(), inc=True):
        self._deps(e, reads, writes)
        sem = self.esem[e]
        if inc:
            self.cnt[e] += 1
            self.pending[e] = False
            self.ops[e].append(lambda eng, fn=fn, sem=sem: fn(eng).then_inc(sem, 1))
            tok = ("eng", e, self.cnt[e])
        else:
            self.pending[e] = True
            self.ops[e].append(lambda eng, fn=fn: fn(eng))
            tok = ("eng", e, self.cnt[e] + 1)
        self._commit(tok, reads, writes)

    def dma(self, q, ds, fn, reads=(), writes=()):
        self._deps(q, reads, writes)
        ds.count += 16
        self.ops[q].append(lambda eng, fn=fn, sem=ds.sem: fn(eng).then_inc(sem, 16))
        self._commit(("dma", ds), reads, writes)

    def barrier(self):
        for e in ENGS:
            for e2 in ENGS:
                if e2 == e:
                    continue
                assert not self.pending[e2]
                val = self.cnt[e2]
                key = ("e", e2)
                if val > self.known[e].get(key, 0):
                    self.known[e][key] = val
                    self.ops[e].append(lambda eng, sem=self.esem[e2], val=val: eng.wait_ge(sem, val))
            for d in self.dsems:
                key = ("d", d.name)
                if d.count > self.known[e].get(key, 0):
                    self.known[e][key] = d.count
                    self.ops[e].append(lambda eng, sem=d.sem, val=d.count: eng.wait_ge(sem, val))

    def final_wait(self, e="sp"):
        for d in self.dsems:
            key = ("d", d.name)
            if d.count > self.known[e].get(key, 0):
                self.known[e][key] = d.count
                self.ops[e].append(lambda eng, sem=d.sem, val=d.count: eng.wait_ge(sem, val))

    def emit(self, block):
        ops = self.ops

        @block.tensor
        def _(eng):
            for f in ops["pe"]:
                f(eng)

        @block.scalar
        def _(eng):
            for f in ops["act"]:
                f(eng)

        @block.vector
        def _(eng):
            for f in ops["dve"]:
                f(eng)

        @block.gpsimd
        def _(eng):
            for f in ops["pool"]:
                f(eng)

        @block.sync
        def _(eng):
            for f in ops["sp"]:
                f(eng)


class View:
    def __init__(self, tensor, row, off, shape):
        self.t = tensor
        self.row = row
        self.off = off
        self.shape = tuple(shape)
        st = []
        s = 1
        for d in reversed(self.shape[1:]):
            st.append(s)
            s *= d
        self.strides = tuple(reversed(st))
        self.size = s

    def ap(self, p0=0, pn=None, dims=None, off=0):
        if pn is None:
            pn = self.shape[0] - p0
        if dims is None:
            dims = [(1, self.size)]
        pat = [[self.row, pn]] + [[s, n] for (s, n) in dims]
        return bass.AP(self.t, p0 * self.row + self.off + off, pat)

    def __getitem__(self, key):
        if not isinstance(key, tuple):
            key = (key,)
        key = key + (slice(None),) * (len(self.shape) - len(key))
        ps = key[0]
        if isinstance(ps, int):
            p0, pn = ps, 1
        else:
            p0 = ps.start or 0
            pn = (ps.stop if ps.stop is not None else self.shape[0]) - p0
        off = 0
        dims = []
        for k, d, st in zip(key[1:], self.shape[1:], self.strides):
            if isinstance(k, int):
                off += k * st
            else:
                a = k.start or 0
                b = k.stop if k.stop is not None else d
                step = k.step or 1
                n = (b - a + step - 1) // step
                off += a * st
                dims.append((st * step, n))
        merged = []
        for s, n in dims:
            if merged and merged[-1][0] == s * n:
                merged[-1] = (s, merged[-1][1] * n)
            else:
                merged.append((s, n))
        if not merged:
            merged = [(1, 1)]
        return self.ap(p0, pn, dims=merged, off=off)


class Arena:
    def __init__(self, t32, words):
        self.t32 = t32
        self.words = words
        self.views = {}
        self.pos = 0
        self.hi = 0

    def view(self, dtype):
        if dtype not in self.views:
            if dtype == F32:
                self.views[dtype] = (self.t32, self.words, 1)
            else:
                r = 4 // mybir.dt.size(dtype)
                self.views[dtype] = (self.t32.bitcast(dtype), self.words * r, r)
        return self.views[dtype]

    def reset(self, pos):
        self.pos = pos

    def alloc(self, shape, dtype):
        t, row, r = self.view(dtype)
        n = 1
        for d in shape[1:]:
            n *= d
        words = (n + r - 1) // r
        words = (words + 7) // 8 * 8
        off = self.pos
        self.pos += words
        self.hi = max(self.hi, self.pos)
        assert self.pos <= self.words, "arena overflow %d > %d" % (self.pos, self.words)
        return View(t, row, off * r, shape)


def DA(t, off, pat):
    return bass.AP(t, off, [list(p) for p in pat])


SV_BMOD, SV_GN, SV_CW, SV_CB, SV_BA, SV_BX, SV_LAM, SV_BG, SV_FREQ, SV_SIGN = 0, 24, 32, 64, 72, 80, 88, 96, 112, 113
NSV = 120
NWB = 4608
ARENA_WORDS = 52224


def build(S, debug=False):
    NSC = S // 512
    nc = bass.Bass("TRN2", target_bir_lowering=False)
    kin = "ExternalInput"
    x_d = nc.dram_tensor("x", [S, D], F32, kind=kin)
    pos_d = nc.dram_tensor("pos", [1, S], I32, kind=kin)
    cT_d = nc.dram_tensor("cT", [128, 8], F32, kind=kin)
    wmod_d = nc.dram_tensor("w_mod", [D, 3 * D], F32, kind=kin)
    sv_d = nc.dram_tensor("sv", [128, NSV], F32, kind=kin)
    wa_d = nc.dram_tensor("WA", [D, 2048], F32, kind=kin)
    wb_d = nc.dram_tensor("WB", [D, NWB], F32, kind=kin)
    wg_d = nc.dram_tensor("WG", [D, 2048], F32, kind=kin)
    wor_d = nc.dram_tensor("wor", [D, D], F32, kind=kin)
    woa_d = nc.dram_tensor("woa", [D, D], F32, kind=kin)
    wo_d = nc.dram_tensor("wo", [D, D], F32, kind=kin)
    wax_d = nc.dram_tensor("wax", [128, 16 * 128], F32, kind=kin)
    gfin_d = nc.dram_tensor("gfin", [1, D], F32, kind=kin)
    ident_d = nc.dram_tensor("ident", [128, 128], F32, kind=kin)
    masks_d = nc.dram_tensor("masks", [128, 12 * 512], F32, kind=kin)
    skind = "ExternalOutput" if debug else "Internal"
    mod_d = nc.dram_tensor("MODS", [24, 128], F32, kind=skind)
    ht_d = nc.dram_tensor("HT", [D, S], BF16, kind=skind)
    rt_d = nc.dram_tensor("RT", [D, S], BF16, kind=skind)
    qt_d = nc.dram_tensor("QT", [D, S], BF16, kind=skind)
    kt_d = nc.dram_tensor("KT", [D, S], BF16, kind=skind)
    vt_d = nc.dram_tensor("VT", [D, S], BF16, kind=skind)
    zd_d = nc.dram_tensor("ZD", [D, S], F32, kind=skind)
    at_d = nc.dram_tensor("AT", [D, S], BF16, kind=skind)
    out_d = nc.dram_tensor("out", [S, D], F32, kind="ExternalOutput")

    import contextlib
    with contextlib.ExitStack() as es:
        ar_t = es.enter_context(nc.sbuf_tensor("arena", [128, ARENA_WORDS], F32))
        psb = [es.enter_context(nc.psum_tensor("ps%d" % i, [128, 512], F32)) for i in range(8)]
        esems = {e: es.enter_context(nc.semaphore("s_" + e)) for e in ENGS}
        SCH = Sched(nc, esems)
        _dn = [0]

        def newds(name):
            _dn[0] += 1
            return SCH.dsem(es.enter_context(nc.semaphore("d%d_%s" % (_dn[0], name))), "%d_%s" % (_dn[0], name))

        A = Arena(ar_t, ARENA_WORDS)
        PS = [View(p, 512, 0, [128, 512]) for p in psb]
        PSB = [View(p.bitcast(BF16), 1024, 0, [128, 1024]) for p in psb]
        op, dma = SCH.op, SCH.dma

        sv = A.alloc([128, NSV], F32)
        identf = A.alloc([128, 128], F32)
        identb = A.alloc([128, 128], BF16)
        onesb = A.alloc([128, 128], BF16)
        masks = A.alloc([128, 12, 512], BF16)
        MT = A.alloc([128, 24], F32)
        G3 = A.alloc([128, 24], F32)
        hc = A.alloc([128, 8], F32)
        hba = A.alloc([128, 8], F32)
        hbx = A.alloc([128, 8], F32)
        mhalf = A.alloc([128, 8], F32)
        BC0 = A.alloc([128, D], F32)
        BC1 = A.alloc([128, D], F32)
        base = A.pos
        WAs = A.alloc([128, 8, 2048], BF16)
        WX = A.alloc([128, 16, 128], BF16)
        base1a = A.pos

        ds_c = newds("const")
        ds_c2 = newds("const2")
        ds_w = newds("w")
        ds_w2 = newds("w2")
        dma("sp", ds_c, lambda e: e.dma_start(out=sv[:, :], in_=sv_d.ap()), writes=["sv"])
        dma("sp", ds_c, lambda e: e.dma_start(out=identf[:, :], in_=ident_d.ap()), writes=["identf"])
        dma("pool", ds_c2, lambda e: e.dma_start(out=identb[:, :], in_=ident_d.ap()), writes=["identb"])
        for i in range(12):
            dma("pool", ds_c2, lambda e, i=i: e.dma_start(out=masks[:, i, :], in_=DA(masks_d, i * 512, [[12 * 512, 128], [1, 512]])), writes=["masks"])
        op("pool", lambda e: e.memset(onesb[:, :], 1.0), writes=["onesb"])
        op("pool", lambda e: e.memset(mhalf[:, :], -0.5), writes=["mhalf"])
        def load_w(dst, src_d, W, col0, ncols, ds, key, chunk=2048, keyf=None, dsf=None, extra=()):
            c = 0
            while c < ncols:
                n = min(chunk, ncols - c)
                k_ = keyf(c) if keyf else key
                d_ = dsf(c) if dsf else ds
                dma("pool", d_, lambda e, c=c, n=n: e.dma_start(
                    out=dst[:, :, c:c + n], in_=DA(src_d, col0 + c, [[W, 128], [128 * W, 8], [1, n]])), writes=[k_] + list(extra))
                c += n

        load_w(WAs, wa_d, 2048, 0, 2048, ds_w2, "WAs", chunk=1024)
        dma("pool", ds_w2, lambda e: e.dma_start(out=WX[:, :, :], in_=wax_d.ap()), writes=["WX"])

        A.reset(base1a)
        cT = A.alloc([128, 8], F32)
        cact = A.alloc([128, 8], F32)
        wm = A.alloc([128, 8, 3 * D], F32)
        tmp24 = A.alloc([128, 128], F32)
        dma("sp", ds_c, lambda e: e.dma_start(out=cT[:, :], in_=cT_d.ap()), writes=["cT"])
        for kc in range(8):
            dma("sp", ds_w, lambda e, kc=kc: e.dma_start(out=wm[:, kc, :], in_=DA(wmod_d, kc * 128 * 3 * D, [[3 * D, 128], [1, 3 * D]])), writes=["wm"])
        op("act", lambda e: e.activation(cact[:, :], cT[:, :], AF.Silu), reads=["cT"], writes=["cact"])
        for ft in range(24):
            for kc in range(8):
                op("pe", lambda e, ft=ft, kc=kc: e.matmul(PS[0][:, ft:ft + 1], wm[:, kc, ft * 128:(ft + 1) * 128], cact[:, kc:kc + 1],
                                                      start=(kc == 0), stop=(kc == 7)),
                   reads=["wm", "cact"], writes=["ps0"], inc=(ft == 23 and kc == 7))
        op("dve", lambda e: e.tensor_tensor(MT[:, :], PS[0][:, 0:24], sv[:, SV_BMOD:SV_BMOD + 24], ALU.add), reads=["ps0", "sv"], writes=["MT"])
        op("dve", lambda e: e.scalar_tensor_tensor(G3[:, 0:8], MT[:, 8:16], 1.0, sv[:, SV_GN:SV_GN + 8], ALU.add, ALU.mult), reads=["MT", "sv"], writes=["G3a"])
        op("dve", lambda e: e.tensor_copy(G3[:, 8:16], MT[:, 0:8]), reads=["MT"], writes=["G3b"])
        op("dve", lambda e: e.tensor_copy(G3[:, 16:24], MT[:, 16:24]), reads=["MT"], writes=["G3c"])
        op("pe", lambda e: e.transpose(PS[1][0:24, 0:128], G3[:, 0:24], identf[:, :]), reads=["G3a", "G3b", "G3c", "identf"], writes=["ps1"])
        op("dve", lambda e: e.tensor_copy(tmp24[0:24, :], PS[1][0:24, 0:128]), reads=["ps1"], writes=["tmp24"])
        dma("sp", ds_c, lambda e: e.dma_start(out=mod_d.ap(), in_=tmp24[0:24, :]), reads=["tmp24"], writes=["MODd"])
        dma("sp", ds_c, lambda e: e.dma_start(out=BC0[:, :], in_=DA(mod_d, 0, [[0, 128], [1, D]])), reads=["MODd"], writes=["BC0"])
        dma("sp", ds_c, lambda e: e.dma_start(out=BC1[:, :], in_=DA(mod_d, D, [[0, 128], [1, D]])), reads=["MODd"], writes=["BC1"])
        op("act", lambda e: e.activation(hc[:, :], sv[:, SV_LAM:SV_LAM + 8], AF.Exp, scale=-1.0), reads=["sv"], writes=["hc"])
        op("act", lambda e: e.activation(hc[:, :], hc[:, :], AF.Ln, bias=1.0), reads=["hc"], writes=["hc"])
        op("dve", lambda e: e.tensor_scalar(hc[:, :], hc[:, :], -4.0, None, ALU.mult), reads=["hc"], writes=["hc"])
        op("dve", lambda e: e.tensor_scalar(hba[:, :], sv[:, SV_BA:SV_BA + 8], 0.5, None, ALU.mult), reads=["sv"], writes=["hba"])
        op("dve", lambda e: e.tensor_scalar(hbx[:, :], sv[:, SV_BX:SV_BX + 8], 0.5, None, ALU.mult), reads=["sv"], writes=["hbx"])
        SCH.barrier()

        A.reset(base1a)
        xs = [A.alloc([128, D], F32) for _ in range(4)]
        sqs = A.alloc([128, D], BF16)
        ssq = [A.alloc([128, 4], F32) for _ in range(2)]
        rstd = [A.alloc([128, 4], F32) for _ in range(2)]
        tn = [A.alloc([128, D], F32) for _ in range(2)]
        hb = A.alloc([128, 4, D], BF16)
        hT = [A.alloc([128, 8, 512], BF16) for _ in range(2)]
        XR = A.alloc([128, 8, 515], F32)
        posi = A.alloc([128, 512], I32)
        nm0 = [A.alloc([128, 512], F32) for _ in range(3)]
        hst = A.alloc([128, 8], F32)
        NZU, NXC, NXB, NTR, NHH = 7, 6, 3, 5, 3
        ZU = [A.alloc([128, 512], F32) for _ in range(NZU)]
        XC = [A.alloc([128, 512], F32) for _ in range(NXC)]
        XCb = [A.alloc([128, 512], BF16) for _ in range(NXB)]
        TR = [A.alloc([128, 512], F32) for _ in range(NTR)]
        TI = [A.alloc([128, 512], F32) for _ in range(NTR)]
        NA2 = 4
        A2 = [A.alloc([128, 512], F32) for _ in range(NA2)]
        HH = [A.alloc([128, 512], F32) for _ in range(NHH)]
        RO = [A.alloc([128, 4, 512], BF16) for _ in range(2)]
        ds_x = [newds("x%d" % i) for i in range(4)]
        ds_ht = [newds("ht%d" % i) for i in range(2)]
        ds_ro = [newds("ro%d" % i) for i in range(2)]
        ds_pos = newds("pos")
        op("dve", lambda e: e.memset(XR[:, :, 0:3], 0.0), writes=["XRc%d" % cb for cb in range(8)])
        op("dve", lambda e: e.memset(hst[:, :], 0.0), writes=["hst%d" % cb for cb in range(8)])

        def pbank(lo, n, ctr):
            b = lo + ctr[0] % n
            ctr[0] += 1
            return b
        ctr_p, ctr_t, ctr_g = [0], [0], [0]

        def norm_piece(s, piece):
            t0 = s * 512
            hTs, hkey = hT[s % 2], "hT%d" % (s % 2)
            nm, nmk = nm0[s % 3], "nm0_%d" % (s % 3)
            sq_, rs_, sk = ssq[s % 2], rstd[s % 2], "%d" % (s % 2)
            if piece == 0:
                dma("sp", ds_pos, lambda e: e.dma_start(out=posi[:, :], in_=DA(pos_d, t0, [[0, 128], [1, 512]])), writes=["posi"])
                for tt in range(4):
                    j = s * 4 + tt
                    xsl, xk = xs[j % 4], "x%d" % (j % 4)
                    dma("sp", ds_x[j % 4], lambda e, xsl=xsl, tt=tt: e.dma_start(out=xsl[:, :], in_=DA(x_d, (t0 + 128 * tt) * D, [[D, 128], [1, D]])), writes=[xk])
            elif piece == 1:
                op("dve", lambda e: e.tensor_scalar(nm[:, :], posi[:, :], 0.0, None, ALU.not_equal), reads=["posi"], writes=[nmk])
                for tt in range(4):
                    j = s * 4 + tt
                    xsl, xk = xs[j % 4], "x%d" % (j % 4)
                    op("act", lambda e, xsl=xsl, tt=tt: e.activation(sqs[:, :], xsl[:, :], AF.Square, accum_out=sq_[:, tt:tt + 1]), reads=[xk], writes=["sqs", "ssq" + sk + "_%d" % tt])
            elif piece == 2:
                op("dve", lambda e: e.tensor_scalar(rs_[:, :], sq_[:, :], 1.0 / D, 1e-6, ALU.mult, ALU.add), reads=["ssq" + sk + "_%d" % i for i in range(4)], writes=["rstd" + sk])
                op("pool", lambda e: e.tensor_tensor(rs_[:, :], rs_[:, :], mhalf[:, 0:4], ALU.pow), reads=["rstd" + sk, "mhalf"], writes=["rstd" + sk])
            elif piece == 3:
                for tt in range(4):
                    j = s * 4 + tt
                    xsl, xk = xs[j % 4], "x%d" % (j % 4)
                    tns, tk = tn[tt % 2], "tn%d" % (tt % 2)
                    op("dve", lambda e, xsl=xsl, tns=tns, tt=tt: e.scalar_tensor_tensor(tns[:, :], xsl[:, :], rs_[:, tt:tt + 1], BC0[:, :], ALU.mult, ALU.mult),
                       reads=[xk, "rstd" + sk, "BC0"], writes=[tk])
                    op("pool", lambda e, tns=tns, tt=tt: e.tensor_tensor(hb[:, tt, :], tns[:, :], BC1[:, :], ALU.add), reads=[tk, "BC1"], writes=["hb%d" % tt])
            elif piece in (4, 5, 6):
                if piece in (5, 6):
                    for kc in range(4 * (piece - 5), 4 * (piece - 5) + 4):
                        b = kc % 4 // 2
                        o = (kc % 2) * 512
                        if True:
                            op("act", lambda e, b=b, o=o, kc=kc: e.activation(hTs[:, kc, :], PSB[b][:, o:o + 512], AF.Copy), reads=["ps%d" % b], writes=[hkey])
                        else:
                            op("dve", lambda e, b=b, o=o, kc=kc: e.tensor_copy(hTs[:, kc, :], PSB[b][:, o:o + 512]), reads=["ps%d" % b], writes=[hkey])
                if piece in (4, 5):
                    for kc in range(4 * (piece - 4), 4 * (piece - 4) + 4):
                        b = kc % 4 // 2
                        o = (kc % 2) * 512
                        for tt in range(4):
                            op("pe", lambda e, b=b, o=o, kc=kc, tt=tt: e.transpose(PSB[b][:, o + tt * 128:o + (tt + 1) * 128], hb[:, tt, kc * 128:(kc + 1) * 128], identb[:, :]),
                               reads=["hb%d" % tt, "identb"], writes=["ps%d" % b], inc=(tt == 3))
                if piece == 6:
                    dma("sp", ds_ht[s % 2], lambda e: e.dma_start(out=DA(ht_d, t0, [[S, 128], [128 * S, 8], [1, 512]]), in_=hTs[:, :, :]), reads=[hkey], writes=["HTd"])

        NIT = NSC * 8
        bx_of, bz_of, bgr_of, bgi_of = {}, {}, {}, {}

        def names(n):
            s, cb = n // 8, n % 8
            return dict(s=s, cb=cb, zu=ZU[n % NZU], zuk="ZU%d" % (n % NZU), xc=XC[n % NXC], xck="XC%d" % (n % NXC), xb=XCb[n % NXB], xbk="XCb%d" % (n % NXB),
                        tr=TR[n % NTR], trk="TR%d" % (n % NTR), ti=TI[n % NTR], tik="TI%d" % (n % NTR), a2=A2[n % NA2], a2k="A2%d" % (n % NA2),
                        hh=HH[n % NHH], hhk="HH%d" % (n % NHH))

        def pe_proj(n):
            s, cb = n // 8, n % 8
            hTs, hkey = hT[s % 2], "hT%d" % (s % 2)
            for which, store in ((0, bx_of), (1, bz_of)):
                b = pbank(2, 3, ctr_p)
                store[n] = b
                for kc in range(8):
                    op("pe", lambda e, b=b, kc=kc, which=which: e.matmul(PS[b][:, :], WAs[:, kc, which * 1024 + cb * 128:which * 1024 + (cb + 1) * 128], hTs[:, kc, :], start=(kc == 0), stop=(kc == 7)),
                       reads=[hkey, "WAs"], writes=["ps%d" % b], inc=(kc == 7))

        def pe_gates(n):
            d = names(n)
            cb = d["cb"]
            for which, store in ((0, bgr_of), (1, bgi_of)):
                b = pbank(5, 3, ctr_g)
                store[n] = b
                op("pe", lambda e, b=b, which=which: e.matmul(PS[b][:, :], WX[:, which * 8 + cb, :], d["xb"][:, :], start=True, stop=True), reads=[d["xbk"], "WX"], writes=["ps%d" % b])

        NIT = NSC * 8
        for pc in range(8):
            norm_piece(0, pc)
        for t in range(NIT + 7):
            if t < NIT and (t // 8 + 1) < NSC:
                norm_piece(t // 8 + 1, t % 8)
            if 0 <= t - 2 < NIT:
                pe_gates(t - 2)
            if t < NIT:
                pe_proj(t)
            if 0 <= t - 1 < NIT:
                d = names(t - 1); cb = d["cb"]; cw = SV_CW + cb * 4
                op("act", lambda e, d=d, cb=cb, cw=cw: e.activation(d["xc"][:, :], XR[:, cb, 3:515], AF.Identity, scale=sv[:, cw + 3:cw + 4], bias=sv[:, SV_CB + cb:SV_CB + cb + 1]),
                   reads=["XR%d" % cb, "sv"], writes=[d["xck"]])
                for k in range(3):
                    op("dve", lambda e, d=d, cb=cb, cw=cw, k=k: e.scalar_tensor_tensor(d["xc"][:, :], XR[:, cb, k:k + 512], sv[:, cw + k:cw + k + 1], d["xc"][:, :], ALU.mult, ALU.add),
                       reads=["XR%d" % cb, "XRc%d" % cb, d["xck"], "sv"], writes=[d["xck"]])
                op("dve", lambda e, d=d: e.tensor_copy(d["xb"][:, :], d["xc"][:, :]), reads=[d["xck"]], writes=[d["xbk"]])
            if 0 <= t - 3 < NIT:
                d = names(t - 3); cb = d["cb"]
                nm, nmk = nm0[d["s"] % 3], "nm0_%d" % (d["s"] % 3)
                op("act", lambda e, d=d, cb=cb: e.activation(d["tr"][:, :], d["tr"][:, :], AF.Exp, scale=hc[:, cb:cb + 1], bias=hc[:, cb:cb + 1]), reads=[d["trk"], "hc"], writes=[d["trk"]])
                op("pool", lambda e, d=d, nm=nm: e.tensor_tensor(d["tr"][:, :], d["tr"][:, :], nm[:, :], ALU.mult), reads=[d["trk"], nmk], writes=[d["trk"]])
            if 0 <= t - 2 < NIT:
                d = names(t - 2); cb = d["cb"]
                br, bi_ = bgr_of.pop(t - 2), bgi_of.pop(t - 2)
                op("act", lambda e, d=d, br=br, cb=cb: e.activation(d["tr"][:, :], PS[br][:, :], AF.Tanh, scale=0.5, bias=hba[:, cb:cb + 1]), reads=["ps%d" % br, "hba"], writes=[d["trk"]])
                op("act", lambda e, d=d, bi_=bi_, cb=cb: e.activation(d["ti"][:, :], PS[bi_][:, :], AF.Tanh, scale=0.5, bias=hbx[:, cb:cb + 1]), reads=["ps%d" % bi_, "hbx"], writes=[d["tik"]])
            if t < NIT:
                d = names(t); cb = d["cb"]
                bx, bz = bx_of.pop(t), bz_of.pop(t)
                op("act", lambda e, bx=bx, cb=cb: e.activation(XR[:, cb, 3:515], PS[bx][:, :], AF.Copy), reads=["ps%d" % bx], writes=["XR%d" % cb])
                op("act", lambda e, d=d, bz=bz: e.activation(d["zu"][:, :], PS[bz][:, :], AF.Tanh, scale=0.5), reads=["ps%d" % bz], writes=[d["zuk"]])
                op("dve", lambda e, d=d, bz=bz: e.scalar_tensor_tensor(d["zu"][:, :], d["zu"][:, :], 1.0, PS[bz][:, :], ALU.add, ALU.mult), reads=[d["zuk"], "ps%d" % bz], writes=[d["zuk"]])
            if 0 <= t - 1 < NIT:
                d = names(t - 1); cb = d["cb"]
                op("pool", lambda e, cb=cb: e.tensor_copy(XR[:, cb, 0:3], XR[:, cb, 512:515]), reads=["XR%d" % cb, "XRc%d" % cb], writes=["XRc%d" % cb])
            m = t - 3
            if 0 <= m < NIT and m % 4 == 3:
                for q_ in range(m - 3, m + 1):
                    d = names(q_)
                    op("act", lambda e, d=d: e.activation(d["a2"][:, :], d["tr"][:, :], AF.Square, scale=0.25), reads=[d["trk"]], writes=[d["a2k"]])
                for q_ in range(m - 3, m + 1):
                    d = names(q_)
                    op("act", lambda e, d=d: e.activation(d["a2"][:, :], d["a2"][:, :], AF.Sqrt, scale=-1.0, bias=0.0625), reads=[d["a2k"]], writes=[d["a2k"]])
            for q_ in ([t - 6] if 0 <= t - 6 < NIT else []):
                if True:
                    d = names(q_); cb = d["cb"]; s_ = d["s"]
                    half, g = cb // 4, cb % 4
                    ro, rok = RO[half], "RO%d" % half
                    op("dve", lambda e, d=d: e.scalar_tensor_tensor(d["ti"][:, :], d["ti"][:, :], 1.0, d["xc"][:, :], ALU.add, ALU.mult), reads=[d["tik"], d["xck"]], writes=[d["tik"]])
                    op("dve", lambda e, d=d: e.tensor_tensor(d["ti"][:, :], d["ti"][:, :], d["a2"][:, :], ALU.mult), reads=[d["tik"], d["a2k"]], writes=[d["tik"]])
                    op("dve", lambda e, d=d, cb=cb: e.tensor_tensor_scan(d["hh"][:, :], d["tr"][:, :], d["ti"][:, :], hst[:, cb:cb + 1], ALU.mult, ALU.add),
                       reads=[d["trk"], d["tik"], "hst%d" % cb], writes=[d["hhk"]])
                    op("dve", lambda e, d=d, cb=cb: e.tensor_copy(hst[:, cb:cb + 1], d["hh"][:, 511:512]), reads=[d["hhk"]], writes=["hst%d" % cb])
                    op("pool", lambda e, d=d, ro=ro, g=g: e.tensor_tensor(ro[:, g, :], d["hh"][:, :], d["zu"][:, :], ALU.mult), reads=[d["hhk"], d["zuk"]], writes=[rok])
                    if g == 3:
                        dma("sp", ds_ro[half], lambda e, ro=ro, half=half, s_=s_: e.dma_start(out=DA(rt_d, half * 4 * 128 * S + s_ * 512, [[S, 128], [128 * S, 4], [1, 512]]), in_=ro[:, :, :]),
                            reads=[rok], writes=["RTd"])
        print('P1A arena hi', A.pos)
        SCH.barrier()

        A.reset(base)
        WBs = A.alloc([128, 8, NWB], BF16)
        hTb = [A.alloc([128, 8, 512], BF16) for _ in range(2)]
        posf = A.alloc([128, 512], F32)
        ang = A.alloc([128, 512], F32)
        kki = A.alloc([128, 512], I32)
        kkf = A.alloc([128, 512], F32)
        rr = A.alloc([128, 512], F32)
        COS = A.alloc([128, 512], F32)
        SIN = A.alloc([128, 512], F32)
        ropef = [A.alloc([128, 512], F32) for _ in range(2)]
        partf = [A.alloc([128, 512], F32) for _ in range(2)]
        QO = [A.alloc([128, 8, 512], BF16) for _ in range(2)]
        ZO = [A.alloc([128, 8, 512], F32) for _ in range(2)]
        VO = [A.alloc([128, 8, 512], BF16) for _ in range(2)]
        ds_h = [newds("hb%d" % i) for i in range(2)]
        ds_q = [newds("qo%d" % i) for i in range(2)]
        ds_z = [newds("zo%d" % i) for i in range(2)]
        ds_v = [newds("vo%d" % i) for i in range(2)]
        WCH = 1536
        ds_wb = [newds("wb%d" % i) for i in range(NWB // WCH)]
        load_w(WBs, wb_d, NWB, 0, NWB, None, None, chunk=WCH, keyf=lambda c: "WBs%d" % (c // WCH), dsf=lambda c: ds_wb[c // WCH])
        ctr_p, ctr_z, ctr_v = [0], [0], [0]

        def proj(b, col0, hTs, hkey, M=128):
            for kc in range(8):
                op("pe", lambda e, b=b, kc=kc, col0=col0, hTs=hTs: e.matmul(PS[b][:, :], WBs[:, kc, col0:col0 + 128], hTs[:, kc, :], start=(kc == 0), stop=(kc == 7)),
                   reads=[hkey, "WBs%d" % (col0 // WCH)], writes=["ps%d" % b], inc=(kc == 7))

        posi2 = [A.alloc([128, 512], I32), A.alloc([128, 512], I32)]
        ds_pos2 = [ds_pos, newds("pos2")]

        def p1b_loads(s):
            hTs_, hkey_ = hTb[s % 2], "hTb%d" % (s % 2)
            dma("sp", ds_h[s % 2], lambda e: e.dma_start(out=hTs_[:, :, :], in_=DA(ht_d, s * 512, [[S, 128], [128 * S, 8], [1, 512]])), reads=["HTd"], writes=[hkey_])
            dma("sp", ds_pos2[s % 2], lambda e: e.dma_start(out=posi2[s % 2][:, :], in_=DA(pos_d, s * 512, [[0, 128], [1, 512]])), writes=["posi%d" % (s % 2)])

        p1b_loads(0)
        for s in range(NSC):
            t0 = s * 512
            hTs, hkey = hTb[s % 2], "hTb%d" % (s % 2)
            if s + 1 < NSC:
                p1b_loads(s + 1)
            op("dve", lambda e, s=s: e.tensor_copy(posf[:, :], posi2[s % 2][:, :]), reads=["posi%d" % (s % 2)], writes=["posf"])
            op("dve", lambda e: e.tensor_scalar(ang[:, :], posf[:, :], sv[:, SV_FREQ:SV_FREQ + 1], None, ALU.mult), reads=["posf", "sv"], writes=["ang"])
            for which in range(2):
                tab, tk = (SIN, "SIN") if which == 0 else (COS, "COS")
                op("dve", lambda e, which=which: e.tensor_scalar(kki[:, :], ang[:, :], 1.0 / TWO_PI, 0.25 * which, ALU.mult, ALU.add), reads=["ang"], writes=["kki"])
                op("dve", lambda e: e.tensor_copy(kkf[:, :], kki[:, :]), reads=["kki"], writes=["kkf"])
                op("dve", lambda e: e.scalar_tensor_tensor(rr[:, :], kkf[:, :], -C1, ang[:, :], ALU.mult, ALU.add), reads=["kkf", "ang"], writes=["rr"])
                op("dve", lambda e: e.scalar_tensor_tensor(rr[:, :], kkf[:, :], -C2, rr[:, :], ALU.mult, ALU.add), reads=["kkf", "rr"], writes=["rr"])
                op("dve", lambda e, which=which: e.tensor_scalar(rr[:, :], rr[:, :], 0.5 * np.pi * which, np.pi, ALU.add, ALU.min), reads=["rr"], writes=["rr"])
                op("dve", lambda e: e.tensor_scalar(rr[:, :], rr[:, :], -np.pi, None, ALU.max), reads=["rr"], writes=["rr"])
                if which == 0:
                    op("act", lambda e, tab=tab: e.activation(tab[:, :], rr[:, :], AF.Sin, scale=sv[:, SV_SIGN:SV_SIGN + 1]), reads=["rr", "sv"], writes=[tk])
                else:
                    op("act", lambda e, tab=tab: e.activation(tab[:, :], rr[:, :], AF.Sin), reads=["rr"], writes=[tk])
            zi = s % 2
            for h in range(8):
                b = pbank(2, 4, ctr_p)
                proj(b, 2560 + h * 128, hTs, hkey)
                op("act", lambda e, b=b, zi=zi, h=h: e.activation(ZO[zi][:, h, :], PS[b][:, :], AF.Silu), reads=["ps%d" % b], writes=["ZO%d" % zi])
            dma("sp", ds_z[zi], lambda e, zi=zi, t0=t0: e.dma_start(out=DA(zd_d, t0, [[S, 128], [128 * S, 8], [1, 512]]), in_=ZO[zi][:, :, :]), reads=["ZO%d" % zi], writes=["ZDd"])
            vi = s % 2
            for h in range(8):
                b = pbank(2, 4, ctr_p)
                proj(b, 3584 + h * 128, hTs, hkey)
                if h % 2 == 0:
                    op("act", lambda e, b=b, vi=vi, h=h: e.activation(VO[vi][:, h, :], PS[b][:, :], AF.Copy), reads=["ps%d" % b], writes=["VO%d" % vi])
                else:
                    op("dve", lambda e, b=b, vi=vi, h=h: e.tensor_copy(VO[vi][:, h, :], PS[b][:, :]), reads=["ps%d" % b], writes=["VO%d" % vi])
            dma("sp", ds_v[vi], lambda e, vi=vi, t0=t0: e.dma_start(out=DA(vt_d, t0, [[S, 128], [128 * S, 8], [1, 512]]), in_=VO[vi][:, :, :]), reads=["VO%d" % vi], writes=["VTd"])
            for qk in range(2):
                cbase = qk * 1280
                qo, qok = QO[qk], "QO%d" % qk
                for g in range(2):
                    b = pbank(2, 4, ctr_p)
                    proj(b, cbase + g * 128, hTs, hkey)
                    b2 = pbank(2, 4, ctr_p)
                    proj(b2, cbase + 256 + g * 128, hTs, hkey)
                    rf, rk = ropef[g], "ropef%d" % g
                    pf, pk = partf[g], "partf%d" % g
                    op("dve", lambda e, b=b, rf=rf: e.tensor_tensor(rf[:, :], PS[b][:, :], COS[:, :], ALU.mult), reads=["ps%d" % b, "COS"], writes=[rk])
                    op("dve", lambda e, b2=b2, pf=pf: e.tensor_tensor(pf[:, :], PS[b2][:, :], SIN[:, :], ALU.mult), reads=["ps%d" % b2, "SIN"], writes=[pk])
                    op("pool", lambda e, rf=rf, pf=pf, qo=qo, g=g: e.tensor_tensor(qo[:, g, :], rf[:, :], pf[:, :], ALU.add), reads=[rk, pk], writes=[qok])
                for t in range(6):
                    b = pbank(2, 4, ctr_p)
                    proj(b, cbase + 512 + t * 128, hTs, hkey)
                    if t % 2 == 0:
                        op("act", lambda e, b=b, qo=qo, t=t: e.activation(qo[:, 2 + t, :], PS[b][:, :], AF.Copy), reads=["ps%d" % b], writes=[qok])
                    else:
                        op("dve", lambda e, b=b, qo=qo, t=t: e.tensor_copy(qo[:, 2 + t, :], PS[b][:, :]), reads=["ps%d" % b], writes=[qok])
                dst = qt_d if qk == 0 else kt_d
                dkey = "QTd" if qk == 0 else "KTd"
                dma("sp", ds_q[qk], lambda e, qo=qo, dst=dst, t0=t0: e.dma_start(out=DA(dst, t0, [[S, 128], [128 * S, 8], [1, 512]]), in_=qo[:, :, :]), reads=[qok], writes=[dkey])
        print('P1B arena hi', A.pos)
        SCH.barrier()

        A.reset(base)
        KTs = [A.alloc([128, S], BF16) for _ in range(2)]
        VTs = [A.alloc([128, S], BF16) for _ in range(2)]
        NV1, NV2, NV3 = 16, 16, 48
        V1r = A.alloc([128, NV1, 128], BF16)
        V2r = A.alloc([128, NV2, 128], BF16)
        V3r = A.alloc([128, NV3, 128], BF16)
        Qs = A.alloc([128, 8, 512], BF16)
        NZ = 4
        Zs = [A.alloc([128, 512], F32) for _ in range(NZ)]
        NPT = 8
        PT = [A.alloc([128, 512], BF16) for _ in range(NPT)]
        RD = [A.alloc([128, 512], F32) for _ in range(2)]
        OT = [A.alloc([128, 512], F32) for _ in range(2)]
        AO = [A.alloc([128, 512], BF16) for _ in range(2)]
        O3buf = A.alloc([128, 2048], F32)
        D3buf = A.alloc([128, 2048], F32)
        p2_hi = A.pos
        W3TOP = ARENA_WORDS - 12288
        assert p2_hi <= W3TOP, (p2_hi, W3TOP)
        A.reset(W3TOP)
        WGs = A.alloc([128, 8, 2048], BF16)
        WRs = A.alloc([128, 8, D], BF16)
        ds_w3 = newds("w3")
        load_w(WGs, wg_d, 2048, 0, 2048, ds_w3, "WGs")
        load_w(WRs, wor_d, D, 0, D, ds_w3, "WRs")
        A.reset(base)
        WAt = A.alloc([128, 8, D], BF16)
        assert A.pos == base + 4096
        A.reset(base + 8192)
        WOs = A.alloc([128, 8, D], BF16)
        A.reset(p2_hi)
        ds_k = [newds("k%d" % i) for i in range(2)]
        ds_qs = [newds("qsp%d" % i) for i in range(2)]
        ds_zs = [newds("zs%d" % i) for i in range(NZ)]
        ds_ao = [newds("ao%d" % i) for i in range(2)]
        ctr_b, ctr_pt, ctr_u = [0], [0], [0]
        inv_sqrt = 1.0 / np.sqrt(128.0)
        NSP = S // 2048
        LAG = 3

        def load_head(h):
            hs = h % 2
            Kh, kk = KTs[hs], "K%d" % hs
            g, j = h // 4, h % 4
            dma("sp", ds_k[hs], lambda e: e.dma_start(out=Kh[0:32, :], in_=DA(kt_d, (g * 128 + 32 * j) * S, [[S, 32], [1, S]])), reads=["KTd"], writes=[kk])
            dma("sp", ds_k[hs], lambda e: e.dma_start(out=Kh[32:128, :], in_=DA(kt_d, (256 + 96 * h) * S, [[S, 96], [1, S]])), reads=["KTd"], writes=[kk])
            dma("sp", ds_k[hs], lambda e: e.dma_start(out=VTs[hs][:, :], in_=DA(vt_d, h * 128 * S, [[S, 128], [1, S]])), reads=["VTd"], writes=[kk])

        def load_qspan(h, n):
            sl = (4 * n) % 8
            qk_ = "Qsp%d" % (n % 2)
            g, j = h // 4, h % 4
            dma("sp", ds_qs[n % 2], lambda e: e.dma_start(out=Qs.ap(p0=0, pn=32, dims=[(1, 2048)], off=sl * 512), in_=DA(qt_d, (g * 128 + 32 * j) * S + 2048 * n, [[S, 32], [1, 2048]])), reads=["QTd"], writes=[qk_])
            dma("sp", ds_qs[n % 2], lambda e: e.dma_start(out=Qs.ap(p0=32, pn=96, dims=[(1, 2048)], off=sl * 512), in_=DA(qt_d, (256 + 96 * h) * S + 2048 * n, [[S, 96], [1, 2048]])), reads=["QTd"], writes=[qk_])

        def load_z(h, s):
            zi = s % NZ
            dma("sp", ds_zs[zi], lambda e: e.dma_start(out=Zs[zi][:, :], in_=DA(zd_d, h * 128 * S + s * 512, [[S, 128], [1, 512]])), reads=["ZDd"], writes=["Z%d" % zi])

        load_head(0)
        load_qspan(0, 0)
        for h in range(8):
            hs = h % 2
            Kh, kk = KTs[hs], "K%d" % hs
            VTh = VTs[hs]

            def vgen(ring, rk, idx0, kps, kst, VTh=VTh, kk=kk):
                for c, kp in enumerate(kps):
                    op("pe", lambda e, c=c, kp=kp: e.transpose(PSB[3][:, c * 128:(c + 1) * 128], VTh.ap(dims=[(kst, 128)], off=kp), identb[:, :]),
                       reads=[kk, "identb"], writes=["ps3"], inc=(c == len(kps) - 1))
                n_ = len(kps)
                op("dve", lambda e: e.tensor_copy(ring[:, idx0:idx0 + n_, :], PSB[3][:, 0:128 * n_]), reads=["ps3"], writes=[rk])

            def vgen_sub(s, which):
                if which == 0:
                    vgen(V1r, "V1r", (4 * s) % NV1, [512 * s + 128 * c for c in range(4)], 1)
                elif which == 1:
                    vgen(V2r, "V2r", (4 * s) % NV2, [512 * s + r for r in range(4)], 4)
                else:
                    n_, qd = s // 4 + 1, s % 4
                    if n_ < NSP:
                        vgen(V3r, "V3r", (16 * n_ + 4 * qd) % NV3, [2048 * n_ + 4 * qd + r for r in range(4)], 16)

            def vgen12(s, VTh=VTh, kk=kk):
                kps = [(512 * s + 128 * c, 1) for c in range(4)] + [(512 * s + r, 4) for r in range(4)]
                for c, (kp, kst) in enumerate(kps):
                    op("pe", lambda e, c=c, kp=kp, kst=kst: e.transpose(PSB[3][:, c * 128:(c + 1) * 128], VTh.ap(dims=[(kst, 128)], off=kp), identb[:, :]),
                       reads=[kk, "identb"], writes=["ps3"], inc=(c == 7))
                i1, i2 = (4 * s) % NV1, (4 * s) % NV2
                op("dve", lambda e: e.tensor_copy(V1r[:, i1:i1 + 4, :], PSB[3][:, 0:512]), reads=["ps3"], writes=["V1r"])
                op("dve", lambda e: e.tensor_copy(V2r[:, i2:i2 + 4, :], PSB[3][:, 512:1024]), reads=["ps3"], writes=["V2r"])

            v1, v2, v3 = (V1r, NV1), (V2r, NV2), (V3r, NV3)
            if h + 1 < 8:
                load_head(h + 1)
            EARLY3 = (S // 2 >= 4096)
            if h == 7 and EARLY3:
                load_w(WAt, woa_d, D, 0, D, ds_w3, "WAt", extra=["K0"])
                load_w(WOs, wo_d, D, 0, D, ds_w3, "WOs", extra=["K0"])
            vgen12(0)
            for qd in range(4):
                vgen(V3r, "V3r", 4 * qd, [4 * qd + r for r in range(4)], 16)
            jobs = []
            for n in range(NSP):
                qsl = ((4 * n) % 8) * 512
                for g in range(4):
                    banks = [(0, 0, [((128 * c, 1, 128), (qsl + 4 * g + c, 16, 128), (2048 * n + 4 * g + c, 16), (v3, 16 * n + 4 * g + c)) for c in range(4)])]
                    if n > 0:
                        banks.append((1, 0, [((128 * c, 1, 128), (qsl + 4 * g + c, 16, 128), (2048 * (n - 1) + 4 * g + c, 16), (v3, 16 * (n - 1) + 4 * g + c)) for c in range(4)]))
                    for bi, bk in enumerate(banks):
                        jobs.append((("g3", n, g), bi, len(banks), bk))
                for sp in range(4):
                    s = 4 * n + sp
                    t0 = s * 512
                    qb = (s % 8) * 512
                    banks = []
                    banks.append((0, 0, [((128 * c, 1, 128), (qb + 128 * c, 1, 128), (t0 + 128 * c, 1), (v1, 4 * s + c)) for c in range(4)]))
                    c0 = 1 if s == 0 else 0
                    banks.append((1, 128 * c0, [((128 * c, 1, 128), (qb + 128 * c, 1, 128), (t0 + 128 * (c - 1), 1), (v1, 4 * s + c - 1)) for c in range(c0, 4)]))
                    banks.append((2, 0, [((r, 4, 128), (qb + r, 4, 128), (t0 + r, 4), (v2, 4 * s + r)) for r in range(4)]))
                    if s > 0:
                        banks.append((3, 0, [((r, 4, 128), (qb + r, 4, 128), (t0 - 512 + r, 4), (v2, 4 * (s - 1) + r)) for r in range(4)]))
                    for bi, bk in enumerate(banks):
                        jobs.append((("sc", s, sp), bi, len(banks), bk))
            load_z(h, 0)
            if NSC > 1:
                load_z(h, 1)
            pend = []
            for i in range(len(jobs) + LAG):
                if i < len(jobs):
                    (unit, bi, nbk, (mi, col0, items)) = jobs[i]
                    if bi == 0:
                        ctr_u[0] += 1
                    upar = ctr_u[0] % 2
                    if unit[0] == "sc":
                        s = unit[1]
                        qk_ = "Qsp%d" % ((s // 4) % 2)
                        if bi == 0 and s + 2 < NSC:
                            load_z(h, s + 2)
                        if bi == 0 and unit[2] == 0:
                            if s // 4 + 1 < NSP:
                                load_qspan(h, s // 4 + 1)
                            elif h + 1 < 8:
                                load_qspan(h + 1, 0)
                        if bi == 0 and s + 1 < NSC:
                            vgen12(s + 1)
                        if bi == 2 and unit[2] % 2 == 0 and s // 4 + 1 < NSP:
                            n_, hf_ = s // 4 + 1, unit[2] // 2
                            vgen(V3r, "V3r", (16 * n_ + 8 * hf_) % NV3, [2048 * n_ + 8 * hf_ + r for r in range(8)], 16)
                    else:
                        qk_ = "Qsp%d" % (unit[1] % 2)
                    b = ctr_b[0] % 3
                    ctr_b[0] += 1
                    for ii, (pd, qd_, (kp, kst), vt) in enumerate(items):
                        op("pe", lambda e, b=b, pd=pd, qd_=qd_, kp=kp, kst=kst, Kh=Kh, ii=ii: e.matmul(
                            PS[b].ap(dims=[(pd[1], pd[2])], off=pd[0]), Kh.ap(dims=[(kst, 128)], off=kp), Qs.ap(dims=[(qd_[1], qd_[2])], off=qd_[0]),
                            start=(ii == 0), stop=False, skip_group_check=True),
                           reads=[kk, qk_], writes=["ps%d" % b], inc=False)
                    op("pe", lambda e, b=b, col0=col0, mi=mi: e.matmul(PS[b][:, col0:512], identb[:, :], masks[:, mi, col0:512], start=False, stop=True, skip_group_check=True),
                       reads=["masks", "identb"], writes=["ps%d" % b], inc=True)
                    pi_ = ctr_pt[0] % NPT
                    ctr_pt[0] += 1
                    P, pk_ = PT[pi_], "PT%d" % pi_
                    op("act", lambda e, b=b, P=P, col0=col0: e.activation(P[:, col0:512], PS[b][:, col0:512], AF.Exp, scale=inv_sqrt), reads=["ps%d" % b], writes=[pk_])
                    pend.append((unit, upar, bi, nbk, P, pk_, col0, items))
                j2 = i - LAG
                if j2 >= 0:
                    (unit, upar, bi, nbk, P, pk_, col0, items) = pend[j2]
                    ob = 4 + 2 * upar
                    for ii, (pd, qd_, (kp, kst), (vv, vt)) in enumerate(items):
                        op("pe", lambda e, P=P, pd=pd, vv=vv, vt=vt, first=(bi == 0 and ii == 0), last=(bi == nbk - 1 and ii == len(items) - 1), ob=ob: e.matmul(
                            PS[ob].ap(dims=[(pd[1], pd[2])], off=pd[0]), vv[0][:, vt % vv[1], :], P.ap(dims=[(pd[1], pd[2])], off=pd[0]), start=first, stop=last, skip_group_check=True),
                           reads=[pk_, "V1r", "V2r", "V3r"], writes=["ps%d" % ob], inc=False)
                    op("pe", lambda e, P=P, col0=col0, bi=bi, nbk=nbk, ob=ob: e.matmul(PS[ob + 1][:, col0:512], onesb[:, :], P[:, col0:512], start=(bi == 0), stop=(bi == nbk - 1), skip_group_check=True),
                       reads=[pk_, "onesb"], writes=["ps%d" % (ob + 1)], inc=True)
                    if bi == nbk - 1:
                        if unit[0] == "g3":
                            g = unit[2]
                            op("dve", lambda e, ob=ob, g=g: e.tensor_copy(O3buf.ap(dims=[(1, 4), (16, 128)], off=4 * g), PS[ob][:, :]), reads=["ps%d" % ob], writes=["O3buf"])
                            op("act", lambda e, ob=ob, g=g: e.activation(D3buf.ap(dims=[(1, 4), (16, 128)], off=4 * g), PS[ob + 1][:, :], AF.Copy), reads=["ps%d" % (ob + 1)], writes=["D3buf"])
                        else:
                            s, sp = unit[1], unit[2]
                            zi = s % NZ
                            Z, zk = Zs[zi], "Z%d" % zi
                            t0 = s * 512
                            par = upar
                            op("dve", lambda e, ob=ob, par=par, sp=sp: e.tensor_tensor(RD[par][:, :], PS[ob + 1][:, :], D3buf[:, 512 * sp:512 * (sp + 1)], ALU.add), reads=["ps%d" % (ob + 1), "D3buf"], writes=["RD%d" % par])
                            op("dve", lambda e, ob=ob, par=par, sp=sp: e.tensor_tensor(OT[par][:, :], PS[ob][:, :], O3buf[:, 512 * sp:512 * (sp + 1)], ALU.add), reads=["ps%d" % ob, "O3buf"], writes=["OT%d" % par])
                            op("act", lambda e, par=par: e.activation(RD[par][:, :], RD[par][:, :], AF.Ln), reads=["RD%d" % par], writes=["RD%d" % par])
                            op("act", lambda e, par=par: e.activation(RD[par][:, :], RD[par][:, :], AF.Exp, scale=-1.0), reads=["RD%d" % par], writes=["RD%d" % par])
                            op("pool", lambda e, par=par: e.tensor_tensor(OT[par][:, :], OT[par][:, :], RD[par][:, :], ALU.mult), reads=["OT%d" % par, "RD%d" % par], writes=["OT%d" % par])
                            op("pool", lambda e, par=par, Z=Z: e.tensor_tensor(AO[par][:, :], OT[par][:, :], Z[:, :], ALU.mult), reads=["OT%d" % par, zk], writes=["AO%d" % par])
                            dma("sp", ds_ao[par], lambda e, par=par, h=h, t0=t0: e.dma_start(out=DA(at_d, h * 128 * S + t0, [[S, 128], [1, 512]]), in_=AO[par][:, :]), reads=["AO%d" % par], writes=["ATd"])
        print('P2 arena hi', A.pos)
        SCH.barrier()

        A.reset(base)
        _wat = A.alloc([128, 8, D], BF16)
        hT3 = [A.alloc([128, 8, 512], BF16) for _ in range(2)]
        assert A.pos == base + 8192
        _wos = A.alloc([128, 8, D], BF16)
        if not EARLY3:
            load_w(WAt, woa_d, D, 0, D, ds_w, "WAt")
            load_w(WOs, wo_d, D, 0, D, ds_w, "WOs")
        AT3 = [A.alloc([128, 8, 512], BF16) for _ in range(2)]
        RT3 = [A.alloc([128, 8, 512], BF16) for _ in range(2)]
        SG = [A.alloc([128, 512], F32) for _ in range(2)]
        M1 = [A.alloc([128, 512], F32) for _ in range(2)]
        M2 = [A.alloc([128, 512], F32) for _ in range(2)]
        MGs = [A.alloc([128, 8, 512], BF16) for _ in range(2)]
        X3 = [A.alloc([128, D], F32) for _ in range(2)]
        Y3 = [A.alloc([128, D], F32) for _ in range(2)]
        ssq3 = A.alloc([128, 2], F32)
        ds_in3 = [newds("in3_%d" % i) for i in range(2)]
        ds_x3 = [newds("x3_%d" % i) for i in range(2)]
        ds_o3 = [newds("o3_%d" % i) for i in range(2)]
        dma("sp", ds_c, lambda e: e.dma_start(out=BC0[:, :], in_=DA(mod_d, 2 * D, [[0, 128], [1, D]])), reads=["MODd"], writes=["BC0"])
        dma("sp", ds_c, lambda e: e.dma_start(out=BC1[:, :], in_=DA(gfin_d, 0, [[0, 128], [1, D]])), writes=["BC1"])

        ctr_p, ctr_w, ctr_x, ctr_m, ctr_sg = [0], [0], [0], [0], [0]
        SGr = [A.alloc([128, 512], F32) for _ in range(2)]

        def p3_ft(s):
            t0 = s * 512
            si = s % 2
            hTs, ATs, RTs = hT3[si], AT3[si], RT3[si]
            ik = "in3_%d" % si
            MG, mgk = MGs[si], "MG%d" % si
            for ft in range(8):
                mi_ = ctr_m[0] % 2
                ctr_m[0] += 1
                for br in range(2):
                    pr = ctr_p[0] % 3
                    ctr_p[0] += 1
                    bg, by = 2 * pr, 2 * pr + 1
                    Wy, Ys = (WRs, RTs) if br == 0 else (WAt, ATs)
                    wyk = "WRs" if br == 0 else "WAt"
                    for kc in range(8):
                        op("pe", lambda e, kc=kc, bg=bg, br=br, ft=ft: e.matmul(PS[bg][:, :], WGs[:, kc, br * 1024 + ft * 128:br * 1024 + (ft + 1) * 128], hTs[:, kc, :], start=(kc == 0), stop=(kc == 7)),
                           reads=[ik, "WGs"], writes=["ps%d" % bg], inc=(kc == 7))
                    for kc in range(8):
                        op("pe", lambda e, kc=kc, by=by, Wy=Wy, Ys=Ys, ft=ft: e.matmul(PS[by][:, :], Wy[:, kc, ft * 128:(ft + 1) * 128], Ys[:, kc, :], start=(kc == 0), stop=(kc == 7)),
                           reads=[ik, wyk], writes=["ps%d" % by], inc=(kc == 7))
                    sgi = ctr_sg[0] % 4
                    ctr_sg[0] += 1
                    sg, sgk = (SG + SGr)[sgi], "SG%d" % sgi
                    Mx, mk = (M1, "M1_%d" % mi_) if br == 0 else (M2, "M2_%d" % mi_)
                    op("act", lambda e, bg=bg, sg=sg, br=br, ft=ft: e.activation(sg[:, :], PS[bg][:, :], AF.Sigmoid, bias=sv[:, SV_BG + 8 * br + ft:SV_BG + 8 * br + ft + 1]), reads=["ps%d" % bg, "sv"], writes=[sgk])
                    op("dve", lambda e, by=by, sg=sg, Mx=Mx, mi_=mi_: e.tensor_tensor(Mx[mi_][:, :], PS[by][:, :], sg[:, :], ALU.mult), reads=["ps%d" % by, sgk], writes=[mk])
                op("pool", lambda e, ft=ft, mi_=mi_: e.tensor_tensor(MG[:, ft, :], M1[mi_][:, :], M2[mi_][:, :], ALU.add), reads=["M1_%d" % mi_, "M2_%d" % mi_], writes=[mgk])

        pendB = []

        def p3_B(xi, X, Y, row0):
            op("dve", lambda e: e.scalar_tensor_tensor(Y[:, :], X[:, :], ssq3[:, xi:xi + 1], BC1[:, :], ALU.mult, ALU.mult), reads=["X3_%d" % xi, "ssq3_%d" % xi, "BC1", "Y3_%d" % xi], writes=["Y3_%d" % xi])
            dma("sp", ds_o3[xi], lambda e: e.dma_start(out=DA(out_d, row0 * D, [[D, 128], [1, D]]), in_=Y[:, :]), reads=["Y3_%d" % xi], writes=["OUTd"])

        def p3_wo(s):
            t0 = s * 512
            si = s % 2
            MG, mgk = MGs[si], "MG%d" % si
            for tt in range(4):
                xi = ctr_x[0] % 2
                ctr_x[0] += 1
                X, Y = X3[xi], Y3[xi]
                dma("sp", ds_x3[xi], lambda e, X=X, tt=tt: e.dma_start(out=X[:, :], in_=DA(x_d, (t0 + 128 * tt) * D, [[D, 128], [1, D]])), writes=["X3_%d" % xi])
                for hf in range(2):
                    b = 6 + ctr_w[0] % 2
                    ctr_w[0] += 1
                    for kc in range(8):
                        op("pe", lambda e, b=b, kc=kc, tt=tt, hf=hf: e.matmul(PS[b][:, :], MG[:, kc, tt * 128:(tt + 1) * 128], WOs[:, kc, hf * 512:(hf + 1) * 512], start=(kc == 0), stop=(kc == 7)),
                           reads=[mgk, "WOs"], writes=["ps%d" % b], inc=(kc == 7))
                    op("dve", lambda e, b=b, hf=hf, Y=Y: e.tensor_tensor(Y[:, hf * 512:(hf + 1) * 512], PS[b][:, :], BC0[:, hf * 512:(hf + 1) * 512], ALU.mult), reads=["ps%d" % b, "BC0"], writes=["Y3_%d" % xi])
                op("pool", lambda e, X=X, Y=Y: e.tensor_tensor(X[:, :], X[:, :], Y[:, :], ALU.add), reads=["X3_%d" % xi, "Y3_%d" % xi], writes=["X3_%d" % xi])
                op("act", lambda e, X=X, Y=Y, xi=xi: e.activation(Y[:, :], X[:, :], AF.Square, accum_out=ssq3[:, xi:xi + 1]), reads=["X3_%d" % xi, "Y3_%d" % xi], writes=["Y3_%d" % xi, "ssq3_%d" % xi])
                op("pool", lambda e, xi=xi: e.tensor_scalar(ssq3[:, xi:xi + 1], ssq3[:, xi:xi + 1], 1.0 / D, 1e-6, ALU.mult, ALU.add), reads=["ssq3_%d" % xi], writes=["ssq3_%d" % xi])
                op("pool", lambda e, xi=xi: e.tensor_tensor(ssq3[:, xi:xi + 1], ssq3[:, xi:xi + 1], mhalf[:, 0:1], ALU.pow), reads=["ssq3_%d" % xi, "mhalf"], writes=["ssq3_%d" % xi])
                if pendB:
                    p3_B(*pendB.pop(0))
                pendB.append((xi, X, Y, t0 + 128 * tt))

        def p3_loads(s):
            t0 = s * 512
            si = s % 2
            ik = "in3_%d" % si
            for (dst_, src_, dk) in ((hT3[si], ht_d, "HTd"), (AT3[si], at_d, "ATd"), (RT3[si], rt_d, "RTd")):
                dma("sp", ds_in3[si], lambda e, dst_=dst_, src_=src_: e.dma_start(out=dst_[:, :, :], in_=DA(src_, t0, [[S, 128], [128 * S, 8], [1, 512]])), reads=[dk], writes=[ik])

        p3_loads(0)
        if NSC > 1:
            p3_loads(1)
        p3_ft(0)
        for s in range(NSC):
            if s + 1 < NSC:
                p3_ft(s + 1)
            if s + 2 < NSC:
                p3_loads(s + 2)
            p3_wo(s)
        while pendB:
            p3_B(*pendB.pop(0))
        print('P3 arena hi', A.pos, 'of', ARENA_WORDS)
        assert A.pos <= W3TOP
        SCH.final_wait("sp")
        with nc.Block() as block:
            SCH.emit(block)
    return nc


def _layouts(w_in, conv_w, conv_b, w_a, b_a, w_x, b_x, lam, b_gate, b_mod, g_norm):
    wi = w_in
    x_rnn, z_rnn, q, k, v, z_attn, g_r, g_a = [wi[:, i * 1024:(i + 1) * 1024] for i in range(8)]
    WA = np.ascontiguousarray(np.concatenate([x_rnn, z_rnn], axis=1))

    def qk_cols(w):
        hd = w.reshape(1024, 8, 128)
        rope = hd[:, :, 0:32].reshape(1024, 256)
        partner = np.concatenate([hd[:, :, 16:32], hd[:, :, 0:16]], axis=2).reshape(1024, 256)
        pas = hd[:, :, 32:128].reshape(1024, 768)
        return [rope, partner, pas]
    WB = np.ascontiguousarray(np.concatenate(qk_cols(q) + qk_cols(k) + [z_attn, v], axis=1))
    WG = np.ascontiguousarray(np.concatenate([g_r, g_a], axis=1))
    sv = np.zeros((128, NSV), np.float32)
    sv[:, SV_BMOD:SV_BMOD + 24] = b_mod.reshape(24, 128).T
    sv[:, SV_GN:SV_GN + 8] = g_norm.reshape(8, 128).T
    sv[:, SV_CW:SV_CW + 32] = conv_w.reshape(4, 8, 128).transpose(2, 1, 0).reshape(128, 32)
    sv[:, SV_CB:SV_CB + 8] = conv_b.reshape(8, 128).T
    sv[:, SV_BA:SV_BA + 8] = b_a.T
    sv[:, SV_BX:SV_BX + 8] = b_x.T
    sv[:, SV_LAM:SV_LAM + 8] = lam.reshape(8, 128).T
    sv[:, SV_BG:SV_BG + 16] = b_gate.reshape(16, 128).T
    wax = np.ascontiguousarray(np.concatenate([w_a, w_x], axis=0).transpose(1, 0, 2).reshape(128, 16 * 128))
    return WA, WB, WG, sv, wax


def _consts():
    p = np.arange(128)
    freq = (500000.0 ** (-(np.arange(0, 32, 2, dtype=np.float32)) / 32.0)).astype(np.float32)
    fcol = freq[p % 16]
    sign = np.where((p % 32) < 16, -1.0, 1.0).astype(np.float32)
    k = np.arange(128)[:, None]
    col = np.arange(512)[None, :]
    m = np.zeros((128, 12, 512), np.float32)
    m[:, 0] = (col % 128) >= k
    m[:, 1] = (col % 128) <= k
    m[:, 2] = (col // 4) >= k
    m[:, 3] = (col // 4) <= k
    for sp in range(4):
        m[:, 4 + sp] = (32 * sp + col // 16) >= k
        m[:, 8 + sp] = (32 * sp + col // 16) <= k
    m = (m - 1.0) * 30000.0
    return fcol, sign, np.ascontiguousarray(m.reshape(128, 12 * 512)), np.eye(128, dtype=np.float32)


def make_in_maps(S, nb, x, c, positions, g_norm, w_mod, b_mod, w_in, b_gate, conv_w, conv_b, w_a, b_a, w_x, b_x, lam,
                 w_out_rnn, w_out_attn, w_o, g_final):
    f = lambda a: np.ascontiguousarray(np.asarray(a), dtype=np.float32)
    WA, WB, WG, sv, wax = _layouts(f(w_in[0]), f(conv_w[0]), f(conv_b[0]), f(w_a[0]), f(b_a[0]), f(w_x[0]), f(b_x[0]), f(lam[0]),
                                   f(b_gate[0]), f(b_mod[0]), f(g_norm[0]))
    fcol, sign, masks, ident = _consts()
    sv[:, SV_FREQ] = fcol
    sv[:, SV_SIGN] = sign
    shared = {"w_mod": f(w_mod[0]), "sv": sv, "WA": WA, "WB": WB, "WG": WG, "wor": f(w_out_rnn[0]), "woa": f(w_out_attn[0]),
              "wo": f(w_o[0]), "wax": wax, "gfin": f(g_final).reshape(1, D), "ident": ident, "masks": masks}
    x = np.asarray(x)
    c = np.asarray(c)
    positions = np.asarray(positions)
    maps = []
    for b in range(nb):
        m = dict(shared)
        m["x"] = np.ascontiguousarray(x[b], dtype=np.float32)
        m["pos"] = np.ascontiguousarray(positions[b].reshape(1, S), dtype=np.int32)
        m["cT"] = np.ascontiguousarray(c[b].astype(np.float32).reshape(8, 128).T)
        maps.append(m)
    return maps


def kernel(**inputs):
    x = np.asarray(inputs["x"])
    B, S, _ = x.shape
    nc = build(S)
    maps = make_in_maps(S, B, **inputs)
    res = run_bass_kernel_spmd(nc, maps, core_ids=list(range(B)))
    return np.stack([np.asarray(r["out"]).reshape(S, D) for r in res.results], axis=0).astype(np.float32)
```

```python
import numpy as np
import concourse.bass as bass
import concourse.mybir as mybir
from concourse.bass_utils import run_bass_kernel_spmd

F32 = mybir.dt.float32
BF16 = mybir.dt.bfloat16
I32 = mybir.dt.int32
AF = mybir.ActivationFunctionType
ALU = mybir.AluOpType

D = 1024
SEQ = 8192
NB = 8
TWO_PI = 6.283185307179586
C1 = 6.28125
C2 = TWO_PI - C1
ENGS = ("pe", "act", "dve", "pool", "sp")


class DSem:
    def __init__(self, sem, name):
        self.sem = sem
        self.count = 0
        self.name = name


class Sched:
    def __init__(self, nc, esems):
        self.nc = nc
        self.ops = {e: [] for e in ENGS}
        self.cnt = {e: 0 for e in ENGS}
        self.pending = {e: False for e in ENGS}
        self.esem = esems
        self.known = {e: {} for e in ENGS}
        self.lw = {}
        self.rd = {}
        self.dsems = []

    def dsem(self, sem, name):
        d = DSem(sem, name)
        self.dsems.append(d)
        return d

    def _tok_val(self, tok):
        if tok[0] == "eng":
            return ("e", tok[1]), self.esem[tok[1]], tok[2]
        d = tok[1]
        return ("d", d.name), d.sem, d.count

    def _deps(self, e, reads, writes):
        toks = []
        for k in reads:
            t = self.lw.get(k)
            if t is not None:
                toks.append(t)
        for k in writes:
            t = self.lw.get(k)
            if t is not None:
                toks.append(t)
            toks.extend(self.rd.get(k, ()))
        need = {}
        for t in toks:
            if t[0] == "eng" and t[1] == e and e == "pe":
                continue
            key, sem, val = self._tok_val(t)
            if val <= 0:
                continue
            if need.get(key, (None, 0))[1] < val:
                need[key] = (sem, val)
        for key, (sem, val) in need.items():
            if self.known[e].get(key, 0) >= val:
                continue
            self.known[e][key] = val
            if key == ("e", e) and val > self.cnt[e]:
                raise RuntimeError("same-engine dep on un-incremented instruction (%s)" % e)
            self.ops[e].append(lambda eng, sem=sem, val=val: eng.wait_ge(sem, val))

    def _commit(self, tok, reads, writes):
        for k in writes:
            self.lw[k] = tok
            self.rd[k] = []
        for k in reads:
            self.rd.setdefault(k, []).append(tok)

    def op(self, e, fn, reads=(), writes=(), inc=True):
        self._deps(e, reads, writes)
        sem = self.esem[e]
        if inc:
            self.cnt[e] += 1
            self.pending[e] = False
            self.ops[e].append(lambda eng, fn=fn, sem=sem: fn(eng).then_inc(sem, 1))
            tok = ("eng", e, self.cnt[e])
        else:
            self.pending[e] = True
            self.ops[e].append(lambda eng, fn=fn: fn(eng))
            tok = ("eng", e, self.cnt[e] + 1)
        self._commit(tok, reads, writes)

    def dma(self, q, ds, fn, reads=(), writes=()):
        self._deps(q, reads, writes)
        ds.count += 16
        self.ops[q].append(lambda eng, fn=fn, sem=ds.sem: fn(eng).then_inc(sem, 16))
        self._commit(("dma", ds), reads, writes)

    def barrier(self):
        for e in ENGS:
            for e2 in ENGS:
                if e2 == e:
                    continue
                assert not self.pending[e2]
                val = self.cnt[e2]
                key = ("e", e2)
                if val > self.known[e].get(key, 0):
                    self.known[e][key] = val
                    self.ops[e].append(lambda eng, sem=self.esem[e2], val=val: eng.wait_ge(sem, val))
            for d in self.dsems:
                key = ("d", d.name)
                if d.count > self.known[e].get(key, 0):
                    self.known[e][key] = d.count
                    self.ops[e].append(lambda eng, sem=d.sem, val=d.count: eng.wait_ge(sem, val))

    def final_wait(self, e="sp"):
        for d in self.dsems:
            key = ("d", d.name)
            if d.count > self.known[e].get(key, 0):
                self.known[e][key] = d.count
                self.ops[e].append(lambda eng, sem=d.sem, val=d.count: eng.wait_ge(sem, val))

    def emit(self, block):
        ops = self.ops

        @block.tensor
        def _(eng):
            for f in ops["pe"]:
                f(eng)

        @block.scalar
        def _(eng):
            for f in ops["act"]:
                f(eng)

        @block.vector
        def _(eng):
            for f in ops["dve"]:
                f(eng)

        @block.gpsimd
        def _(eng):
            for f in ops["pool"]:
                f(eng)

        @block.sync
        def _(eng):
            for f in ops["sp"]:
                f(eng)


class View:
    def __init__(self, tensor, row, off, shape):
        self.t = tensor
        self.row = row
        self.off = off
        self.shape = tuple(shape)
        st = []
        s = 1
        for d in reversed(self.shape[1:]):
            st.append(s)
            s *= d
        self.strides = tuple(reversed(st))
        self.size = s

    def ap(self, p0=0, pn=None, dims=None, off=0):
        if pn is None:
            pn = self.shape[0] - p0
        if dims is None:
            dims = [(1, self.size)]
        pat = [[self.row, pn]] + [[s, n] for (s, n) in dims]
        return bass.AP(self.t, p0 * self.row + self.off + off, pat)

    def __getitem__(self, key):
        if not isinstance(key, tuple):
            key = (key,)
        key = key + (slice(None),) * (len(self.shape) - len(key))
        ps = key[0]
        if isinstance(ps, int):
            p0, pn = ps, 1
        else:
            p0 = ps.start or 0
            pn = (ps.stop if ps.stop is not None else self.shape[0]) - p0
        off = 0
        dims = []
        for k, d, st in zip(key[1:], self.shape[1:], self.strides):
            if isinstance(k, int):
                off += k * st
            else:
                a = k.start or 0
                b = k.stop if k.stop is not None else d
                step = k.step or 1
                n = (b - a + step - 1) // step
                off += a * st
                dims.append((st * step, n))
        merged = []
        for s, n in dims:
            if merged and merged[-1][0] == s * n:
                merged[-1] = (s, merged[-1][1] * n)
            else:
                merged.append((s, n))
        if not merged:
            merged = [(1, 1)]
        return self.ap(p0, pn, dims=merged, off=off)


class Arena:
    def __init__(self, t32, words):
        self.t32 = t32
        self.words = words
        self.views = {}
        self.pos = 0
        self.hi = 0

    def view(self, dtype):
        if dtype not in self.views:
            if dtype == F32:
                self.views[dtype] = (self.t32, self.words, 1)
            else:
                r = 4 // mybir.dt.size(dtype)
                self.views[dtype] = (self.t32.bitcast(dtype), self.words * r, r)
        return self.views[dtype]

    def reset(self, pos):
        self.pos = pos

    def alloc(self, shape, dtype):
        t, row, r = self.view(dtype)
        n = 1
        for d in shape[1:]:
            n *= d
        words = (n + r - 1) // r
        words = (words + 7) // 8 * 8
        off = self.pos
        self.pos += words
        self.hi = max(self.hi, self.pos)
        assert self.pos <= self.words, "arena overflow %d > %d" % (self.pos, self.words)
        return View(t, row, off * r, shape)


def DA(t, off, pat):
    return bass.AP(t, off, [list(p) for p in pat])


SV_BMOD, SV_GN, SV_CW, SV_CB, SV_BA, SV_BX, SV_LAM, SV_BG, SV_FREQ, SV_SIGN = 0, 24, 32, 64, 72, 80, 88, 96, 112, 113
NSV = 120
NWB = 4608
ARENA_WORDS = 52224


def build(S, debug=False):
    NSC = S // 512
    nc = bass.Bass("TRN2", target_bir_lowering=False)
    kin = "ExternalInput"
    x_d = nc.dram_tensor("x", [S, D], F32, kind=kin)
    pos_d = nc.dram_tensor("pos", [1, S], I32, kind=kin)
    cT_d = nc.dram_tensor("cT", [128, 8], F32, kind=kin)
    wmod_d = nc.dram_tensor("w_mod", [D, 3 * D], F32, kind=kin)
    sv_d = nc.dram_tensor("sv", [128, NSV], F32, kind=kin)
    wa_d = nc.dram_tensor("WA", [D, 2048], F32, kind=kin)
    wb_d = nc.dram_tensor("WB", [D, NWB], F32, kind=kin)
    wg_d = nc.dram_tensor("WG", [D, 2048], F32, kind=kin)
    wor_d = nc.dram_tensor("wor", [D, D], F32, kind=kin)
    woa_d = nc.dram_tensor("woa", [D, D], F32, kind=kin)
    wo_d = nc.dram_tensor("wo", [D, D], F32, kind=kin)
    wax_d = nc.dram_tensor("wax", [128, 16 * 128], F32, kind=kin)
    gfin_d = nc.dram_tensor("gfin", [1, D], F32, kind=kin)
    ident_d = nc.dram_tensor("ident", [128, 128], F32, kind=kin)
    masks_d = nc.dram_tensor("masks", [128, 12 * 512], F32, kind=kin)
    skind = "ExternalOutput" if debug else "Internal"
    mod_d = nc.dram_tensor("MODS", [24, 128], F32, kind=skind)
    ht_d = nc.dram_tensor("HT", [D, S], BF16, kind=skind)
    rt_d = nc.dram_tensor("RT", [D, S], BF16, kind=skind)
    qt_d = nc.dram_tensor("QT", [D, S], BF16, kind=skind)
    kt_d = nc.dram_tensor("KT", [D, S], BF16, kind=skind)
    vt_d = nc.dram_tensor("VT", [D, S], BF16, kind=skind)
    zd_d = nc.dram_tensor("ZD", [D, S], F32, kind=skind)
    at_d = nc.dram_tensor("AT", [D, S], BF16, kind=skind)
    out_d = nc.dram_tensor("out", [S, D], F32, kind="ExternalOutput")

    import contextlib
    with contextlib.ExitStack() as es:
        ar_t = es.enter_context(nc.sbuf_tensor("arena", [128, ARENA_WORDS], F32))
        psb = [es.enter_context(nc.psum_tensor("ps%d" % i, [128, 512], F32)) for i in range(8)]
        esems = {e: es.enter_context(nc.semaphore("s_" + e)) for e in ENGS}
        SCH = Sched(nc, esems)
        _dn = [0]

        def newds(name):
            _dn[0] += 1
            return SCH.dsem(es.enter_context(nc.semaphore("d%d_%s" % (_dn[0], name))), "%d_%s" % (_dn[0], name))

        A = Arena(ar_t, ARENA_WORDS)
        PS = [View(p, 512, 0, [128, 512]) for p in psb]
        PSB = [View(p.bitcast(BF16), 1024, 0, [128, 1024]) for p in psb]
        op, dma = SCH.op, SCH.dma

        sv = A.alloc([128, NSV], F32)
        identf = A.alloc([128, 128], F32)
        identb = A.alloc([128, 128], BF16)
        onesb = A.alloc([128, 128], BF16)
        masks = A.alloc([128, 12, 512], BF16)
        MT = A.alloc([128, 24], F32)
        G3 = A.alloc([128, 24], F32)
        hc = A.alloc([128, 8], F32)
        hba = A.alloc([128, 8], F32)
        hbx = A.alloc([128, 8], F32)
        mhalf = A.alloc([128, 8], F32)
        BC0 = A.alloc([128, D], F32)
        BC1 = A.alloc([128, D], F32)
        base = A.pos
        WAs = A.alloc([128, 8, 2048], BF16)
        WX = A.alloc([128, 16, 128], BF16)
        base1a = A.pos

        ds_c = newds("const")
        ds_c2 = newds("const2")
        ds_w = newds("w")
        ds_w2 = newds("w2")
        dma("sp", ds_c, lambda e: e.dma_start(out=sv[:, :], in_=sv_d.ap()), writes=["sv"])
        dma("sp", ds_c, lambda e: e.dma_start(out=identf[:, :], in_=ident_d.ap()), writes=["identf"])
        dma("pool", ds_c2, lambda e: e.dma_start(out=identb[:, :], in_=ident_d.ap()), writes=["identb"])
        for i in range(12):
            dma("pool", ds_c2, lambda e, i=i: e.dma_start(out=masks[:, i, :], in_=DA(masks_d, i * 512, [[12 * 512, 128], [1, 512]])), writes=["masks"])
        op("pool", lambda e: e.memset(onesb[:, :], 1.0), writes=["onesb"])
        op("pool", lambda e: e.memset(mhalf[:, :], -0.5), writes=["mhalf"])
        def load_w(dst, src_d, W, col0, ncols, ds, key, chunk=2048, keyf=None, dsf=None, extra=()):
            c = 0
            while c < ncols:
                n = min(chunk, ncols - c)
                k_ = keyf(c) if keyf else key
                d_ = dsf(c) if dsf else ds
                dma("pool", d_, lambda e, c=c, n=n: e.dma_start(
                    out=dst[:, :, c:c + n], in_=DA(src_d, col0 + c, [[W, 128], [128 * W, 8], [1, n]])), writes=[k_] + list(extra))
                c += n

        load_w(WAs, wa_d, 2048, 0, 2048, ds_w2, "WAs", chunk=1024)
        dma("pool", ds_w2, lambda e: e.dma_start(out=WX[:, :, :], in_=wax_d.ap()), writes=["WX"])

        A.reset(base1a)
        cT = A.alloc([128, 8], F32)
        cact = A.alloc([128, 8], F32)
        wm = A.alloc([128, 8, 3 * D], F32)
        tmp24 = A.alloc([128, 128], F32)
        dma("sp", ds_c, lambda e: e.dma_start(out=cT[:, :], in_=cT_d.ap()), writes=["cT"])
        for kc in range(8):
            dma("sp", ds_w, lambda e, kc=kc: e.dma_start(out=wm[:, kc, :], in_=DA(wmod_d, kc * 128 * 3 * D, [[3 * D, 128], [1, 3 * D]])), writes=["wm"])
        op("act", lambda e: e.activation(cact[:, :], cT[:, :], AF.Silu), reads=["cT"], writes=["cact"])
        for ft in range(24):
            for kc in range(8):
                op("pe", lambda e, ft=ft, kc=kc: e.matmul(PS[0][:, ft:ft + 1], wm[:, kc, ft * 128:(ft + 1) * 128], cact[:, kc:kc + 1],
                                                      start=(kc == 0), stop=(kc == 7)),
                   reads=["wm", "cact"], writes=["ps0"], inc=(ft == 23 and kc == 7))
        op("dve", lambda e: e.tensor_tensor(MT[:, :], PS[0][:, 0:24], sv[:, SV_BMOD:SV_BMOD + 24], ALU.add), reads=["ps0", "sv"], writes=["MT"])
        op("dve", lambda e: e.scalar_tensor_tensor(G3[:, 0:8], MT[:, 8:16], 1.0, sv[:, SV_GN:SV_GN + 8], ALU.add, ALU.mult), reads=["MT", "sv"], writes=["G3a"])
        op("dve", lambda e: e.tensor_copy(G3[:, 8:16], MT[:, 0:8]), reads=["MT"], writes=["G3b"])
        op("dve", lambda e: e.tensor_copy(G3[:, 16:24], MT[:, 16:24]), reads=["MT"], writes=["G3c"])
        op("pe", lambda e: e.transpose(PS[1][0:24, 0:128], G3[:, 0:24], identf[:, :]), reads=["G3a", "G3b", "G3c", "identf"], writes=["ps1"])
        op("dve", lambda e: e.tensor_copy(tmp24[0:24, :], PS[1][0:24, 0:128]), reads=["ps1"], writes=["tmp24"])
        dma("sp", ds_c, lambda e: e.dma_start(out=mod_d.ap(), in_=tmp24[0:24, :]), reads=["tmp24"], writes=["MODd"])
        dma("sp", ds_c, lambda e: e.dma_start(out=BC0[:, :], in_=DA(mod_d, 0, [[0, 128], [1, D]])), reads=["MODd"], writes=["BC0"])
        dma("sp", ds_c, lambda e: e.dma_start(out=BC1[:, :], in_=DA(mod_d, D, [[0, 128], [1, D]])), reads=["MODd"], writes=["BC1"])
        op("act", lambda e: e.activation(hc[:, :], sv[:, SV_LAM:SV_LAM + 8], AF.Exp, scale=-1.0), reads=["sv"], writes=["hc"])
        op("act", lambda e: e.activation(hc[:, :], hc[:, :], AF.Ln, bias=1.0), reads=["hc"], writes=["hc"])
        op("dve", lambda e: e.tensor_scalar(hc[:, :], hc[:, :], -4.0, None, ALU.mult), reads=["hc"], writes=["hc"])
        op("dve", lambda e: e.tensor_scalar(hba[:, :], sv[:, SV_BA:SV_BA + 8], 0.5, None, ALU.mult), reads=["sv"], writes=["hba"])
        op("dve", lambda e: e.tensor_scalar(hbx[:, :], sv[:, SV_BX:SV_BX + 8], 0.5, None, ALU.mult), reads=["sv"], writes=["hbx"])
        SCH.barrier()

        A.reset(base1a)
        xs = [A.alloc([128, D], F32) for _ in range(4)]
        sqs = A.alloc([128, D], BF16)
        ssq = [A.alloc([128, 4], F32) for _ in range(2)]
        rstd = [A.alloc([128, 4], F32) for _ in range(2)]
        tn = [A.alloc([128, D], F32) for _ in range(2)]
        hb = A.alloc([128, 4, D], BF16)
        hT = [A.alloc([128, 8, 512], BF16) for _ in range(2)]
        XR = A.alloc([128, 8, 515], F32)
        posi = A.alloc([128, 512], I32)
        nm0 = [A.alloc([128, 512], F32) for _ in range(3)]
        hst = A.alloc([128, 8], F32)
        NZU, NXC, NXB, NTR, NHH = 7, 6, 3, 5, 3
        ZU = [A.alloc([128, 512], F32) for _ in range(NZU)]
        XC = [A.alloc([128, 512], F32) for _ in range(NXC)]
        XCb = [A.alloc([128, 512], BF16) for _ in range(NXB)]
        TR = [A.alloc([128, 512], F32) for _ in range(NTR)]
        TI = [A.alloc([128, 512], F32) for _ in range(NTR)]
        NA2 = 4
        A2 = [A.alloc([128, 512], F32) for _ in range(NA2)]
        HH = [A.alloc([128, 512], F32) for _ in range(NHH)]
        RO = [A.alloc([128, 4, 512], BF16) for _ in range(2)]
        ds_x = [newds("x%d" % i) for i in range(4)]
        ds_ht = [newds("ht%d" % i) for i in range(2)]
        ds_ro = [newds("ro%d" % i) for i in range(2)]
        ds_pos = newds("pos")
        op("dve", lambda e: e.memset(XR[:, :, 0:3], 0.0), writes=["XRc%d" % cb for cb in range(8)])
        op("dve", lambda e: e.memset(hst[:, :], 0.0), writes=["hst%d" % cb for cb in range(8)])

        def pbank(lo, n, ctr):
            b = lo + ctr[0] % n
            ctr[0] += 1
            return b
        ctr_p, ctr_t, ctr_g = [0], [0], [0]

        def norm_piece(s, piece):
            t0 = s * 512
            hTs, hkey = hT[s % 2], "hT%d" % (s % 2)
            nm, nmk = nm0[s % 3], "nm0_%d" % (s % 3)
            sq_, rs_, sk = ssq[s % 2], rstd[s % 2], "%d" % (s % 2)
            if piece == 0:
                dma("sp", ds_pos, lambda e: e.dma_start(out=posi[:, :], in_=DA(pos_d, t0, [[0, 128], [1, 512]])), writes=["posi"])
                for tt in range(4):
                    j = s * 4 + tt
                    xsl, xk = xs[j % 4], "x%d" % (j % 4)
                    dma("sp", ds_x[j % 4], lambda e, xsl=xsl, tt=tt: e.dma_start(out=xsl[:, :], in_=DA(x_d, (t0 + 128 * tt) * D, [[D, 128], [1, D]])), writes=[xk])
            elif piece == 1:
                op("dve", lambda e: e.tensor_scalar(nm[:, :], posi[:, :], 0.0, None, ALU.not_equal), reads=["posi"], writes=[nmk])
                for tt in range(4):
                    j = s * 4 + tt
                    xsl, xk = xs[j % 4], "x%d" % (j % 4)
                    op("act", lambda e, xsl=xsl, tt=tt: e.activation(sqs[:, :], xsl[:, :], AF.Square, accum_out=sq_[:, tt:tt + 1]), reads=[xk], writes=["sqs", "ssq" + sk + "_%d" % tt])
            elif piece == 2:
                op("dve", lambda e: e.tensor_scalar(rs_[:, :], sq_[:, :], 1.0 / D, 1e-6, ALU.mult, ALU.add), reads=["ssq" + sk + "_%d" % i for i in range(4)], writes=["rstd" + sk])
                op("pool", lambda e: e.tensor_tensor(rs_[:, :], rs_[:, :], mhalf[:, 0:4], ALU.pow), reads=["rstd" + sk, "mhalf"], writes=["rstd" + sk])
            elif piece == 3:
                for tt in range(4):
                    j = s * 4 + tt
                    xsl, xk = xs[j % 4], "x%d" % (j % 4)
                    tns, tk = tn[tt % 2], "tn%d" % (tt % 2)
                    op("dve", lambda e, xsl=xsl, tns=tns, tt=tt: e.scalar_tensor_tensor(tns[:, :], xsl[:, :], rs_[:, tt:tt + 1], BC0[:, :], ALU.mult, ALU.mult),
                       reads=[xk, "rstd" + sk, "BC0"], writes=[tk])
                    op("pool", lambda e, tns=tns, tt=tt: e.tensor_tensor(hb[:, tt, :], tns[:, :], BC1[:, :], ALU.add), reads=[tk, "BC1"], writes=["hb%d" % tt])
            elif piece in (4, 5, 6):
                if piece in (5, 6):
                    for kc in range(4 * (piece - 5), 4 * (piece - 5) + 4):
                        b = kc % 4 // 2
                        o = (kc % 2) * 512
                        if True:
                            op("act", lambda e, b=b, o=o, kc=kc: e.activation(hTs[:, kc, :], PSB[b][:, o:o + 512], AF.Copy), reads=["ps%d" % b], writes=[hkey])
                        else:
                            op("dve", lambda e, b=b, o=o, kc=kc: e.tensor_copy(hTs[:, kc, :], PSB[b][:, o:o + 512]), reads=["ps%d" % b], writes=[hkey])
                if piece in (4, 5):
                    for kc in range(4 * (piece - 4), 4 * (piece - 4) + 4):
                        b = kc % 4 // 2
                        o = (kc % 2) * 512
                        for tt in range(4):
                            op("pe", lambda e, b=b, o=o, kc=kc, tt=tt: e.transpose(PSB[b][:, o + tt * 128:o + (tt + 1) * 128], hb[:, tt, kc * 128:(kc + 1) * 128], identb[:, :]),
                               reads=["hb%d" % tt, "identb"], writes=["ps%d" % b], inc=(tt == 3))
                if piece == 6:
                    dma("sp", ds_ht[s % 2], lambda e: e.dma_start(out=DA(ht_d, t0, [[S, 128], [128 * S, 8], [1, 512]]), in_=hTs[:, :, :]), reads=[hkey], writes=["HTd"])

        NIT = NSC * 8
        bx_of, bz_of, bgr_of, bgi_of = {}, {}, {}, {}

        def names(n):
            s, cb = n // 8, n % 8
            return dict(s=s, cb=cb, zu=ZU[n % NZU], zuk="ZU%d" % (n % NZU), xc=XC[n % NXC], xck="XC%d" % (n % NXC), xb=XCb[n % NXB], xbk="XCb%d" % (n % NXB),
                        tr=TR[n % NTR], trk="TR%d" % (n % NTR), ti=TI[n % NTR], tik="TI%d" % (n % NTR), a2=A2[n % NA2], a2k="A2%d" % (n % NA2),
                        hh=HH[n % NHH], hhk="HH%d" % (n % NHH))

        def pe_proj(n):
            s, cb = n // 8, n % 8
            hTs, hkey = hT[s % 2], "hT%d" % (s % 2)
            for which, store in ((0, bx_of), (1, bz_of)):
                b = pbank(2, 3, ctr_p)
                store[n] = b
                for kc in range(8):
                    op("pe", lambda e, b=b, kc=kc, which=which: e.matmul(PS[b][:, :], WAs[:, kc, which * 1024 + cb * 128:which * 1024 + (cb + 1) * 128], hTs[:, kc, :], start=(kc == 0), stop=(kc == 7)),
                       reads=[hkey, "WAs"], writes=["ps%d" % b], inc=(kc == 7))

        def pe_gates(n):
            d = names(n)
            cb = d["cb"]
            for which, store in ((0, bgr_of), (1, bgi_of)):
                b = pbank(5, 3, ctr_g)
                store[n] = b
                op("pe", lambda e, b=b, which=which: e.matmul(PS[b][:, :], WX[:, which * 8 + cb, :], d["xb"][:, :], start=True, stop=True), reads=[d["xbk"], "WX"], writes=["ps%d" % b])

        NIT = NSC * 8
        for pc in range(8):
            norm_piece(0, pc)
        for t in range(NIT + 7):
            if t < NIT and (t // 8 + 1) < NSC:
                norm_piece(t // 8 + 1, t % 8)
            if 0 <= t - 2 < NIT:
                pe_gates(t - 2)
            if t < NIT:
                pe_proj(t)
            if 0 <= t - 1 < NIT:
                d = names(t - 1); cb = d["cb"]; cw = SV_CW + cb * 4
                op("act", lambda e, d=d, cb=cb, cw=cw: e.activation(d["xc"][:, :], XR[:, cb, 3:515], AF.Identity, scale=sv[:, cw + 3:cw + 4], bias=sv[:, SV_CB + cb:SV_CB + cb + 1]),
                   reads=["XR%d" % cb, "sv"], writes=[d["xck"]])
                for k in range(3):
                    op("dve", lambda e, d=d, cb=cb, cw=cw, k=k: e.scalar_tensor_tensor(d["xc"][:, :], XR[:, cb, k:k + 512], sv[:, cw + k:cw + k + 1], d["xc"][:, :], ALU.mult, ALU.add),
                       reads=["XR%d" % cb, "XRc%d" % cb, d["xck"], "sv"], writes=[d["xck"]])
                op("dve", lambda e, d=d: e.tensor_copy(d["xb"][:, :], d["xc"][:, :]), reads=[d["xck"]], writes=[d["xbk"]])
            if 0 <= t - 3 < NIT:
                d = names(t - 3); cb = d["cb"]
                nm, nmk = nm0[d["s"] % 3], "nm0_%d" % (d["s"] % 3)
                op("act", lambda e, d=d, cb=cb: e.activation(d["tr"][:, :], d["tr"][:, :], AF.Exp, scale=hc[:, cb:cb + 1], bias=hc[:, cb:cb + 1]), reads=[d["trk"], "hc"], writes=[d["trk"]])
                op("pool", lambda e, d=d, nm=nm: e.tensor_tensor(d["tr"][:, :], d["tr"][:, :], nm[:, :], ALU.mult), reads=[d["trk"], nmk], writes=[d["trk"]])
            if 0 <= t - 2 < NIT:
                d = names(t - 2); cb = d["cb"]
                br, bi_ = bgr_of.pop(t - 2), bgi_of.pop(t - 2)
                op("act", lambda e, d=d, br=br, cb=cb: e.activation(d["tr"][:, :], PS[br][:, :], AF.Tanh, scale=0.5, bias=hba[:, cb:cb + 1]), reads=["ps%d" % br, "hba"], writes=[d["trk"]])
                op("act", lambda e, d=d, bi_=bi_, cb=cb: e.activation(d["ti"][:, :], PS[bi_][:, :], AF.Tanh, scale=0.5, bias=hbx[:, cb:cb + 1]), reads=["ps%d" % bi_, "hbx"], writes=[d["tik"]])
            if t < NIT:
                d = names(t); cb = d["cb"]
                bx, bz = bx_of.pop(t), bz_of.pop(t)
                op("act", lambda e, bx=bx, cb=cb: e.activation(XR[:, cb, 3:515], PS[bx][:, :], AF.Copy), reads=["ps%d" % bx], writes=["XR%d" % cb])
                op("act", lambda e, d=d, bz=bz: e.activation(d["zu"][:, :], PS[bz][:, :], AF.Tanh, scale=0.5), reads=["ps%d" % bz], writes=[d["zuk"]])
                op("dve", lambda e, d=d, bz=bz: e.scalar_tensor_tensor(d["zu"][:, :], d["zu"][:, :], 1.0, PS[bz][:, :], ALU.add, ALU.mult), reads=[d["zuk"], "ps%d" % bz], writes=[d["zuk"]])
            if 0 <= t - 1 < NIT:
                d = names(t - 1); cb = d["cb"]
                op("pool", lambda e, cb=cb: e.tensor_copy(XR[:, cb, 0:3], XR[:, cb, 512:515]), reads=["XR%d" % cb, "XRc%d" % cb], writes=["XRc%d" % cb])
            m = t - 3
            if 0 <= m < NIT and m % 4 == 3:
                for q_ in range(m - 3, m + 1):
                    d = names(q_)
                    op("act", lambda e, d=d: e.activation(d["a2"][:, :], d["tr"][:, :], AF.Square, scale=0.25), reads=[d["trk"]], writes=[d["a2k"]])
                for q_ in range(m - 3, m + 1):
                    d = names(q_)
                    op("act", lambda e, d=d: e.activation(d["a2"][:, :], d["a2"][:, :], AF.Sqrt, scale=-1.0, bias=0.0625), reads=[d["a2k"]], writes=[d["a2k"]])
            for q_ in ([t - 6] if 0 <= t - 6 < NIT else []):
                if True:
                    d = names(q_); cb = d["cb"]; s_ = d["s"]
                    half, g = cb // 4, cb % 4
                    ro, rok = RO[half], "RO%d" % half
                    op("dve", lambda e, d=d: e.scalar_tensor_tensor(d["ti"][:, :], d["ti"][:, :], 1.0, d["xc"][:, :], ALU.add, ALU.mult), reads=[d["tik"], d["xck"]], writes=[d["tik"]])
                    op("dve", lambda e, d=d: e.tensor_tensor(d["ti"][:, :], d["ti"][:, :], d["a2"][:, :], ALU.mult), reads=[d["tik"], d["a2k"]], writes=[d["tik"]])
                    op("dve", lambda e, d=d, cb=cb: e.tensor_tensor_scan(d["hh"][:, :], d["tr"][:, :], d["ti"][:, :], hst[:, cb:cb + 1], ALU.mult, ALU.add),
                       reads=[d["trk"], d["tik"], "hst%d" % cb], writes=[d["hhk"]])
                    op("pool", lambda e, d=d, cb=cb: e.tensor_copy(hst[:, cb:cb + 1], d["hh"][:, 511:512]), reads=[d["hhk"]], writes=["hst%d" % cb])
                    op("pool", lambda e, d=d, ro=ro, g=g: e.tensor_tensor(ro[:, g, :], d["hh"][:, :], d["zu"][:, :], ALU.mult), reads=[d["hhk"], d["zuk"]], writes=[rok])
                    if g == 3:
                        dma("sp", ds_ro[half], lambda e, ro=ro, half=half, s_=s_: e.dma_start(out=DA(rt_d, half * 4 * 128 * S + s_ * 512, [[S, 128], [128 * S, 4], [1, 512]]), in_=ro[:, :, :]),
                            reads=[rok], writes=["RTd"])
        print('P1A arena hi', A.pos)
        SCH.barrier()

        A.reset(base)
        WBs = A.alloc([128, 8, NWB], BF16)
        hTb = [A.alloc([128, 8, 512], BF16) for _ in range(2)]
        posf = A.alloc([128, 512], F32)
        ang = A.alloc([128, 512], F32)
        kki = A.alloc([128, 512], I32)
        kkf = A.alloc([128, 512], F32)
        rr = A.alloc([128, 512], F32)
        COS = A.alloc([128, 512], F32)
        SIN = A.alloc([128, 512], F32)
        ropef = [A.alloc([128, 512], F32) for _ in range(2)]
        partf = [A.alloc([128, 512], F32) for _ in range(2)]
        QO = [A.alloc([128, 8, 512], BF16) for _ in range(2)]
        ZO = [A.alloc([128, 8, 512], F32) for _ in range(2)]
        VO = [A.alloc([128, 8, 512], BF16) for _ in range(2)]
        ds_h = [newds("hb%d" % i) for i in range(2)]
        ds_q = [newds("qo%d" % i) for i in range(2)]
        ds_z = [newds("zo%d" % i) for i in range(2)]
        ds_v = [newds("vo%d" % i) for i in range(2)]
        WCH = 1536
        ds_wb = [newds("wb%d" % i) for i in range(NWB // WCH)]
        load_w(WBs, wb_d, NWB, 0, NWB, None, None, chunk=WCH, keyf=lambda c: "WBs%d" % (c // WCH), dsf=lambda c: ds_wb[c // WCH])
        ctr_p, ctr_z, ctr_v = [0], [0], [0]

        def proj(b, col0, hTs, hkey, M=128):
            for kc in range(8):
                op("pe", lambda e, b=b, kc=kc, col0=col0, hTs=hTs: e.matmul(PS[b][:, :], WBs[:, kc, col0:col0 + 128], hTs[:, kc, :], start=(kc == 0), stop=(kc == 7)),
                   reads=[hkey, "WBs%d" % (col0 // WCH)], writes=["ps%d" % b], inc=(kc == 7))

        posi2 = [A.alloc([128, 512], I32), A.alloc([128, 512], I32)]
        ds_pos2 = [ds_pos, newds("pos2")]

        def p1b_loads(s):
            hTs_, hkey_ = hTb[s % 2], "hTb%d" % (s % 2)
            dma("sp", ds_h[s % 2], lambda e: e.dma_start(out=hTs_[:, :, :], in_=DA(ht_d, s * 512, [[S, 128], [128 * S, 8], [1, 512]])), reads=["HTd"], writes=[hkey_])
            dma("sp", ds_pos2[s % 2], lambda e: e.dma_start(out=posi2[s % 2][:, :], in_=DA(pos_d, s * 512, [[0, 128], [1, 512]])), writes=["posi%d" % (s % 2)])

        p1b_loads(0)
        for s in range(NSC):
            t0 = s * 512
            hTs, hkey = hTb[s % 2], "hTb%d" % (s % 2)
            if s + 1 < NSC:
                p1b_loads(s + 1)
            op("dve", lambda e, s=s: e.tensor_copy(posf[:, :], posi2[s % 2][:, :]), reads=["posi%d" % (s % 2)], writes=["posf"])
            op("dve", lambda e: e.tensor_scalar(ang[:, :], posf[:, :], sv[:, SV_FREQ:SV_FREQ + 1], None, ALU.mult), reads=["posf", "sv"], writes=["ang"])
            for which in range(2):
                tab, tk = (SIN, "SIN") if which == 0 else (COS, "COS")
                op("dve", lambda e, which=which: e.tensor_scalar(kki[:, :], ang[:, :], 1.0 / TWO_PI, 0.25 * which, ALU.mult, ALU.add), reads=["ang"], writes=["kki"])
                op("dve", lambda e: e.tensor_copy(kkf[:, :], kki[:, :]), reads=["kki"], writes=["kkf"])
                op("dve", lambda e: e.scalar_tensor_tensor(rr[:, :], kkf[:, :], -C1, ang[:, :], ALU.mult, ALU.add), reads=["kkf", "ang"], writes=["rr"])
                op("dve", lambda e: e.scalar_tensor_tensor(rr[:, :], kkf[:, :], -C2, rr[:, :], ALU.mult, ALU.add), reads=["kkf", "rr"], writes=["rr"])
                op("dve", lambda e, which=which: e.tensor_scalar(rr[:, :], rr[:, :], 0.5 * np.pi * which, np.pi, ALU.add, ALU.min), reads=["rr"], writes=["rr"])
                op("dve", lambda e: e.tensor_scalar(rr[:, :], rr[:, :], -np.pi, None, ALU.max), reads=["rr"], writes=["rr"])
                if which == 0:
                    op("act", lambda e, tab=tab: e.activation(tab[:, :], rr[:, :], AF.Sin, scale=sv[:, SV_SIGN:SV_SIGN + 1]), reads=["rr", "sv"], writes=[tk])
                else:
                    op("act", lambda e, tab=tab: e.activation(tab[:, :], rr[:, :], AF.Sin), reads=["rr"], writes=[tk])
            zi = s % 2
            for h in range(8):
                b = pbank(2, 4, ctr_p)
                proj(b, 2560 + h * 128, hTs, hkey)
                op("act", lambda e, b=b, zi=zi, h=h: e.activation(ZO[zi][:, h, :], PS[b][:, :], AF.Silu), reads=["ps%d" % b], writes=["ZO%d" % zi])
            dma("sp", ds_z[zi], lambda e, zi=zi, t0=t0: e.dma_start(out=DA(zd_d, t0, [[S, 128], [128 * S, 8], [1, 512]]), in_=ZO[zi][:, :, :]), reads=["ZO%d" % zi], writes=["ZDd"])
            vi = s % 2
            for h in range(8):
                b = pbank(2, 4, ctr_p)
                proj(b, 3584 + h * 128, hTs, hkey)
                if h % 2 == 0:
                    op("act", lambda e, b=b, vi=vi, h=h: e.activation(VO[vi][:, h, :], PS[b][:, :], AF.Copy), reads=["ps%d" % b], writes=["VO%d" % vi])
                else:
                    op("dve", lambda e, b=b, vi=vi, h=h: e.tensor_copy(VO[vi][:, h, :], PS[b][:, :]), reads=["ps%d" % b], writes=["VO%d" % vi])
            dma("sp", ds_v[vi], lambda e, vi=vi, t0=t0: e.dma_start(out=DA(vt_d, t0, [[S, 128], [128 * S, 8], [1, 512]]), in_=VO[vi][:, :, :]), reads=["VO%d" % vi], writes=["VTd"])
            for qk in range(2):
                cbase = qk * 1280
                qo, qok = QO[qk], "QO%d" % qk
                for g in range(2):
                    b = pbank(2, 4, ctr_p)
                    proj(b, cbase + g * 128, hTs, hkey)
                    b2 = pbank(2, 4, ctr_p)
                    proj(b2, cbase + 256 + g * 128, hTs, hkey)
                    rf, rk = ropef[g], "ropef%d" % g
                    pf, pk = partf[g], "partf%d" % g
                    op("dve", lambda e, b=b, rf=rf: e.tensor_tensor(rf[:, :], PS[b][:, :], COS[:, :], ALU.mult), reads=["ps%d" % b, "COS"], writes=[rk])
                    op("dve", lambda e, b2=b2, pf=pf: e.tensor_tensor(pf[:, :], PS[b2][:, :], SIN[:, :], ALU.mult), reads=["ps%d" % b2, "SIN"], writes=[pk])
                    op("pool", lambda e, rf=rf, pf=pf, qo=qo, g=g: e.tensor_tensor(qo[:, g, :], rf[:, :], pf[:, :], ALU.add), reads=[rk, pk], writes=[qok])
                for t in range(6):
                    b = pbank(2, 4, ctr_p)
                    proj(b, cbase + 512 + t * 128, hTs, hkey)
                    if t % 2 == 0:
                        op("act", lambda e, b=b, qo=qo, t=t: e.activation(qo[:, 2 + t, :], PS[b][:, :], AF.Copy), reads=["ps%d" % b], writes=[qok])
                    else:
                        op("dve", lambda e, b=b, qo=qo, t=t: e.tensor_copy(qo[:, 2 + t, :], PS[b][:, :]), reads=["ps%d" % b], writes=[qok])
                dst = qt_d if qk == 0 else kt_d
                dkey = "QTd" if qk == 0 else "KTd"
                dma("sp", ds_q[qk], lambda e, qo=qo, dst=dst, t0=t0: e.dma_start(out=DA(dst, t0, [[S, 128], [128 * S, 8], [1, 512]]), in_=qo[:, :, :]), reads=[qok], writes=[dkey])
        print('P1B arena hi', A.pos)
        SCH.barrier()

        A.reset(base)
        KTs = [A.alloc([128, S], BF16) for _ in range(2)]
        VTs = [A.alloc([128, S], BF16) for _ in range(2)]
        NV1, NV2, NV3 = 16, 16, 48
        V1r = A.alloc([128, NV1, 128], BF16)
        V2r = A.alloc([128, NV2, 128], BF16)
        V3r = A.alloc([128, NV3, 128], BF16)
        Qs = A.alloc([128, 8, 512], BF16)
        NZ = 4
        Zs = [A.alloc([128, 512], F32) for _ in range(NZ)]
        NPT = 8
        PT = [A.alloc([128, 512], BF16) for _ in range(NPT)]
        RD = [A.alloc([128, 512], F32) for _ in range(2)]
        OT = [A.alloc([128, 512], F32) for _ in range(2)]
        AO = [A.alloc([128, 512], BF16) for _ in range(2)]
        O3buf = A.alloc([128, 2048], F32)
        D3buf = A.alloc([128, 2048], F32)
        p2_hi = A.pos
        W3TOP = ARENA_WORDS - 12288
        assert p2_hi <= W3TOP, (p2_hi, W3TOP)
        A.reset(W3TOP)
        WGs = A.alloc([128, 8, 2048], BF16)
        WRs = A.alloc([128, 8, D], BF16)
        ds_w3 = newds("w3")
        load_w(WGs, wg_d, 2048, 0, 2048, ds_w3, "WGs")
        load_w(WRs, wor_d, D, 0, D, ds_w3, "WRs")
        A.reset(base)
        WAt = A.alloc([128, 8, D], BF16)
        assert A.pos == base + 4096
        A.reset(base + 8192)
        WOs = A.alloc([128, 8, D], BF16)
        A.reset(p2_hi)
        ds_k = [newds("k%d" % i) for i in range(2)]
        ds_qs = [newds("qsp%d" % i) for i in range(2)]
        ds_zs = [newds("zs%d" % i) for i in range(NZ)]
        ds_ao = [newds("ao%d" % i) for i in range(2)]
        ctr_b, ctr_pt, ctr_u = [0], [0], [0]
        inv_sqrt = 1.0 / np.sqrt(128.0)
        NSP = S // 2048
        LAG = 3

        def load_head(h):
            hs = h % 2
            Kh, kk = KTs[hs], "K%d" % hs
            g, j = h // 4, h % 4
            dma("sp", ds_k[hs], lambda e: e.dma_start(out=Kh[0:32, :], in_=DA(kt_d, (g * 128 + 32 * j) * S, [[S, 32], [1, S]])), reads=["KTd"], writes=[kk])
            dma("sp", ds_k[hs], lambda e: e.dma_start(out=Kh[32:128, :], in_=DA(kt_d, (256 + 96 * h) * S, [[S, 96], [1, S]])), reads=["KTd"], writes=[kk])
            dma("sp", ds_k[hs], lambda e: e.dma_start(out=VTs[hs][:, :], in_=DA(vt_d, h * 128 * S, [[S, 128], [1, S]])), reads=["VTd"], writes=[kk])

        def load_qspan(h, n):
            sl = (4 * n) % 8
            qk_ = "Qsp%d" % (n % 2)
            g, j = h // 4, h % 4
            dma("sp", ds_qs[n % 2], lambda e: e.dma_start(out=Qs.ap(p0=0, pn=32, dims=[(1, 2048)], off=sl * 512), in_=DA(qt_d, (g * 128 + 32 * j) * S + 2048 * n, [[S, 32], [1, 2048]])), reads=["QTd"], writes=[qk_])
            dma("sp", ds_qs[n % 2], lambda e: e.dma_start(out=Qs.ap(p0=32, pn=96, dims=[(1, 2048)], off=sl * 512), in_=DA(qt_d, (256 + 96 * h) * S + 2048 * n, [[S, 96], [1, 2048]])), reads=["QTd"], writes=[qk_])

        def load_z(h, s):
            zi = s % NZ
            dma("sp", ds_zs[zi], lambda e: e.dma_start(out=Zs[zi][:, :], in_=DA(zd_d, h * 128 * S + s * 512, [[S, 128], [1, 512]])), reads=["ZDd"], writes=["Z%d" % zi])

        load_head(0)
        load_qspan(0, 0)
        for h in range(8):
            hs = h % 2
            Kh, kk = KTs[hs], "K%d" % hs
            VTh = VTs[hs]

            def vgen(ring, rk, idx0, kps, kst, VTh=VTh, kk=kk):
                for c, kp in enumerate(kps):
                    op("pe", lambda e, c=c, kp=kp: e.transpose(PSB[3][:, c * 128:(c + 1) * 128], VTh.ap(dims=[(kst, 128)], off=kp), identb[:, :]),
                       reads=[kk, "identb"], writes=["ps3"], inc=(c == len(kps) - 1))
                n_ = len(kps)
                op("dve", lambda e: e.tensor_copy(ring[:, idx0:idx0 + n_, :], PSB[3][:, 0:128 * n_]), reads=["ps3"], writes=[rk])

            def vgen_sub(s, which):
                if which == 0:
                    vgen(V1r, "V1r", (4 * s) % NV1, [512 * s + 128 * c for c in range(4)], 1)
                elif which == 1:
                    vgen(V2r, "V2r", (4 * s) % NV2, [512 * s + r for r in range(4)], 4)
                else:
                    n_, qd = s // 4 + 1, s % 4
                    if n_ < NSP:
                        vgen(V3r, "V3r", (16 * n_ + 4 * qd) % NV3, [2048 * n_ + 4 * qd + r for r in range(4)], 16)

            def vgen12(s, VTh=VTh, kk=kk):
                kps = [(512 * s + 128 * c, 1) for c in range(4)] + [(512 * s + r, 4) for r in range(4)]
                for c, (kp, kst) in enumerate(kps):
                    op("pe", lambda e, c=c, kp=kp, kst=kst: e.transpose(PSB[3][:, c * 128:(c + 1) * 128], VTh.ap(dims=[(kst, 128)], off=kp), identb[:, :]),
                       reads=[kk, "identb"], writes=["ps3"], inc=(c == 7))
                i1, i2 = (4 * s) % NV1, (4 * s) % NV2
                op("dve", lambda e: e.tensor_copy(V1r[:, i1:i1 + 4, :], PSB[3][:, 0:512]), reads=["ps3"], writes=["V1r"])
                op("dve", lambda e: e.tensor_copy(V2r[:, i2:i2 + 4, :], PSB[3][:, 512:1024]), reads=["ps3"], writes=["V2r"])

            v1, v2, v3 = (V1r, NV1), (V2r, NV2), (V3r, NV3)
            if h + 1 < 8:
                load_head(h + 1)
            EARLY3 = (S // 2 >= 4096)
            if h == 7 and EARLY3:
                load_w(WAt, woa_d, D, 0, D, ds_w3, "WAt", extra=["K0"])
                load_w(WOs, wo_d, D, 0, D, ds_w3, "WOs", extra=["K0"])
            vgen12(0)
            for qd in range(4):
                vgen(V3r, "V3r", 4 * qd, [4 * qd + r for r in range(4)], 16)
            jobs = []
            for n in range(NSP):
                qsl = ((4 * n) % 8) * 512
                for g in range(4):
                    banks = [(0, 0, [((128 * c, 1, 128), (qsl + 4 * g + c, 16, 128), (2048 * n + 4 * g + c, 16), (v3, 16 * n + 4 * g + c)) for c in range(4)])]
                    if n > 0:
                        banks.append((1, 0, [((128 * c, 1, 128), (qsl + 4 * g + c, 16, 128), (2048 * (n - 1) + 4 * g + c, 16), (v3, 16 * (n - 1) + 4 * g + c)) for c in range(4)]))
                    for bi, bk in enumerate(banks):
                        jobs.append((("g3", n, g), bi, len(banks), bk))
                for sp in range(4):
                    s = 4 * n + sp
                    t0 = s * 512
                    qb = (s % 8) * 512
                    banks = []
                    banks.append((0, 0, [((128 * c, 1, 128), (qb + 128 * c, 1, 128), (t0 + 128 * c, 1), (v1, 4 * s + c)) for c in range(4)]))
                    c0 = 1 if s == 0 else 0
                    banks.append((1, 128 * c0, [((128 * c, 1, 128), (qb + 128 * c, 1, 128), (t0 + 128 * (c - 1), 1), (v1, 4 * s + c - 1)) for c in range(c0, 4)]))
                    banks.append((2, 0, [((r, 4, 128), (qb + r, 4, 128), (t0 + r, 4), (v2, 4 * s + r)) for r in range(4)]))
                    if s > 0:
                        banks.append((3, 0, [((r, 4, 128), (qb + r, 4, 128), (t0 - 512 + r, 4), (v2, 4 * (s - 1) + r)) for r in range(4)]))
                    for bi, bk in enumerate(banks):
                        jobs.append((("sc", s, sp), bi, len(banks), bk))
            load_z(h, 0)
            if NSC > 1:
                load_z(h, 1)
            pend = []
            for i in range(len(jobs) + LAG):
                if i < len(jobs):
                    (unit, bi, nbk, (mi, col0, items)) = jobs[i]
                    if bi == 0:
                        ctr_u[0] += 1
                    upar = ctr_u[0] % 2
                    if unit[0] == "sc":
                        s = unit[1]
                        qk_ = "Qsp%d" % ((s // 4) % 2)
                        if bi == 0 and s + 2 < NSC:
                            load_z(h, s + 2)
                        if bi == 0 and unit[2] == 0:
                            if s // 4 + 1 < NSP:
                                load_qspan(h, s // 4 + 1)
                            elif h + 1 < 8:
                                load_qspan(h + 1, 0)
                        if bi == 0 and s + 1 < NSC:
                            vgen12(s + 1)
                        if bi == 2 and unit[2] % 2 == 0 and s // 4 + 1 < NSP:
                            n_, hf_ = s // 4 + 1, unit[2] // 2
                            vgen(V3r, "V3r", (16 * n_ + 8 * hf_) % NV3, [2048 * n_ + 8 * hf_ + r for r in range(8)], 16)
                    else:
                        qk_ = "Qsp%d" % (unit[1] % 2)
                    b = ctr_b[0] % 3
                    ctr_b[0] += 1
                    for ii, (pd, qd_, (kp, kst), vt) in enumerate(items):
                        op("pe", lambda e, b=b, pd=pd, qd_=qd_, kp=kp, kst=kst, Kh=Kh, ii=ii: e.matmul(
                            PS[b].ap(dims=[(pd[1], pd[2])], off=pd[0]), Kh.ap(dims=[(kst, 128)], off=kp), Qs.ap(dims=[(qd_[1], qd_[2])], off=qd_[0]),
                            start=(ii == 0), stop=False, skip_group_check=True),
                           reads=[kk, qk_], writes=["ps%d" % b], inc=False)
                    op("pe", lambda e, b=b, col0=col0, mi=mi: e.matmul(PS[b][:, col0:512], identb[:, :], masks[:, mi, col0:512], start=False, stop=True, skip_group_check=True),
                       reads=["masks", "identb"], writes=["ps%d" % b], inc=True)
                    pi_ = ctr_pt[0] % NPT
                    ctr_pt[0] += 1
                    P, pk_ = PT[pi_], "PT%d" % pi_
                    op("act", lambda e, b=b, P=P, col0=col0: e.activation(P[:, col0:512], PS[b][:, col0:512], AF.Exp, scale=inv_sqrt), reads=["ps%d" % b], writes=[pk_])
                    pend.append((unit, upar, bi, nbk, P, pk_, col0, items))
                j2 = i - LAG
                if j2 >= 0:
                    (unit, upar, bi, nbk, P, pk_, col0, items) = pend[j2]
                    ob = 4 + 2 * upar
                    for ii, (pd, qd_, (kp, kst), (vv, vt)) in enumerate(items):
                        op("pe", lambda e, P=P, pd=pd, vv=vv, vt=vt, first=(bi == 0 and ii == 0), last=(bi == nbk - 1 and ii == len(items) - 1), ob=ob: e.matmul(
                            PS[ob].ap(dims=[(pd[1], pd[2])], off=pd[0]), vv[0][:, vt % vv[1], :], P.ap(dims=[(pd[1], pd[2])], off=pd[0]), start=first, stop=last, skip_group_check=True),
                           reads=[pk_, "V1r", "V2r", "V3r"], writes=["ps%d" % ob], inc=False)
                    op("pe", lambda e, P=P, col0=col0, bi=bi, nbk=nbk, ob=ob: e.matmul(PS[ob + 1][:, col0:512], onesb[:, :], P[:, col0:512], start=(bi == 0), stop=(bi == nbk - 1), skip_group_check=True),
                       reads=[pk_, "onesb"], writes=["ps%d" % (ob + 1)], inc=True)
                    if bi == nbk - 1:
                        if unit[0] == "g3":
                            g = unit[2]
                            op("dve", lambda e, ob=ob, g=g: e.tensor_copy(O3buf.ap(dims=[(1, 4), (16, 128)], off=4 * g), PS[ob][:, :]), reads=["ps%d" % ob], writes=["O3buf"])
                            op("act", lambda e, ob=ob, g=g: e.activation(D3buf.ap(dims=[(1, 4), (16, 128)], off=4 * g), PS[ob + 1][:, :], AF.Copy), reads=["ps%d" % (ob + 1)], writes=["D3buf"])
                        else:
                            s, sp = unit[1], unit[2]
                            zi = s % NZ
                            Z, zk = Zs[zi], "Z%d" % zi
                            t0 = s * 512
                            par = upar
                            op("dve", lambda e, ob=ob, par=par, sp=sp: e.tensor_tensor(RD[par][:, :], PS[ob + 1][:, :], D3buf[:, 512 * sp:512 * (sp + 1)], ALU.add), reads=["ps%d" % (ob + 1), "D3buf"], writes=["RD%d" % par])
                            op("dve", lambda e, ob=ob, par=par, sp=sp: e.tensor_tensor(OT[par][:, :], PS[ob][:, :], O3buf[:, 512 * sp:512 * (sp + 1)], ALU.add), reads=["ps%d" % ob, "O3buf"], writes=["OT%d" % par])
                            op("act", lambda e, par=par: e.activation(RD[par][:, :], RD[par][:, :], AF.Ln), reads=["RD%d" % par], writes=["RD%d" % par])
                            op("act", lambda e, par=par: e.activation(RD[par][:, :], RD[par][:, :], AF.Exp, scale=-1.0), reads=["RD%d" % par], writes=["RD%d" % par])
                            op("pool", lambda e, par=par: e.tensor_tensor(OT[par][:, :], OT[par][:, :], RD[par][:, :], ALU.mult), reads=["OT%d" % par, "RD%d" % par], writes=["OT%d" % par])
                            op("pool", lambda e, par=par, Z=Z: e.tensor_tensor(AO[par][:, :], OT[par][:, :], Z[:, :], ALU.mult), reads=["OT%d" % par, zk], writes=["AO%d" % par])
                            dma("sp", ds_ao[par], lambda e, par=par, h=h, t0=t0: e.dma_start(out=DA(at_d, h * 128 * S + t0, [[S, 128], [1, 512]]), in_=AO[par][:, :]), reads=["AO%d" % par], writes=["ATd"])
        print('P2 arena hi', A.pos)
        SCH.barrier()

        A.reset(base)
        _wat = A.alloc([128, 8, D], BF16)
        hT3 = [A.alloc([128, 8, 512], BF16) for _ in range(2)]
        assert A.pos == base + 8192
        _wos = A.alloc([128, 8, D], BF16)
        if not EARLY3:
            load_w(WAt, woa_d, D, 0, D, ds_w, "WAt")
            load_w(WOs, wo_d, D, 0, D, ds_w, "WOs")
        AT3 = [A.alloc([128, 8, 512], BF16) for _ in range(2)]
        RT3 = [A.alloc([128, 8, 512], BF16) for _ in range(2)]
        SG = [A.alloc([128, 512], F32) for _ in range(2)]
        M1 = [A.alloc([128, 512], F32) for _ in range(2)]
        M2 = [A.alloc([128, 512], F32) for _ in range(2)]
        MGs = [A.alloc([128, 8, 512], BF16) for _ in range(2)]
        X3 = [A.alloc([128, D], F32) for _ in range(2)]
        Y3 = [A.alloc([128, D], F32) for _ in range(2)]
        ssq3 = A.alloc([128, 2], F32)
        ds_in3 = [newds("in3_%d" % i) for i in range(2)]
        ds_x3 = [newds("x3_%d" % i) for i in range(2)]
        ds_o3 = [newds("o3_%d" % i) for i in range(2)]
        dma("sp", ds_c, lambda e: e.dma_start(out=BC0[:, :], in_=DA(mod_d, 2 * D, [[0, 128], [1, D]])), reads=["MODd"], writes=["BC0"])
        dma("sp", ds_c, lambda e: e.dma_start(out=BC1[:, :], in_=DA(gfin_d, 0, [[0, 128], [1, D]])), writes=["BC1"])

        ctr_p, ctr_w, ctr_x, ctr_m, ctr_sg = [0], [0], [0], [0], [0]
        SGr = [A.alloc([128, 512], F32) for _ in range(2)]

        def p3_ft(s):
            t0 = s * 512
            si = s % 2
            hTs, ATs, RTs = hT3[si], AT3[si], RT3[si]
            ik = "in3_%d" % si
            MG, mgk = MGs[si], "MG%d" % si
            for ft in range(8):
                mi_ = ctr_m[0] % 2
                ctr_m[0] += 1
                for br in range(2):
                    pr = ctr_p[0] % 3
                    ctr_p[0] += 1
                    bg, by = 2 * pr, 2 * pr + 1
                    Wy, Ys = (WRs, RTs) if br == 0 else (WAt, ATs)
                    wyk = "WRs" if br == 0 else "WAt"
                    for kc in range(8):
                        op("pe", lambda e, kc=kc, bg=bg, br=br, ft=ft: e.matmul(PS[bg][:, :], WGs[:, kc, br * 1024 + ft * 128:br * 1024 + (ft + 1) * 128], hTs[:, kc, :], start=(kc == 0), stop=(kc == 7)),
                           reads=[ik, "WGs"], writes=["ps%d" % bg], inc=(kc == 7))
                    for kc in range(8):
                        op("pe", lambda e, kc=kc, by=by, Wy=Wy, Ys=Ys, ft=ft: e.matmul(PS[by][:, :], Wy[:, kc, ft * 128:(ft + 1) * 128], Ys[:, kc, :], start=(kc == 0), stop=(kc == 7)),
                           reads=[ik, wyk], writes=["ps%d" % by], inc=(kc == 7))
                    sgi = ctr_sg[0] % 4
                    ctr_sg[0] += 1
                    sg, sgk = (SG + SGr)[sgi], "SG%d" % sgi
                    Mx, mk = (M1, "M1_%d" % mi_) if br == 0 else (M2, "M2_%d" % mi_)
                    op("act", lambda e, bg=bg, sg=sg, br=br, ft=ft: e.activation(sg[:, :], PS[bg][:, :], AF.Sigmoid, bias=sv[:, SV_BG + 8 * br + ft:SV_BG + 8 * br + ft + 1]), reads=["ps%d" % bg, "sv"], writes=[sgk])
                    op("dve", lambda e, by=by, sg=sg, Mx=Mx, mi_=mi_: e.tensor_tensor(Mx[mi_][:, :], PS[by][:, :], sg[:, :], ALU.mult), reads=["ps%d" % by, sgk], writes=[mk])
                op("pool", lambda e, ft=ft, mi_=mi_: e.tensor_tensor(MG[:, ft, :], M1[mi_][:, :], M2[mi_][:, :], ALU.add), reads=["M1_%d" % mi_, "M2_%d" % mi_], writes=[mgk])

        pendB = []

        def p3_B(xi, X, Y, row0):
            op("dve", lambda e: e.scalar_tensor_tensor(Y[:, :], X[:, :], ssq3[:, xi:xi + 1], BC1[:, :], ALU.mult, ALU.mult), reads=["X3_%d" % xi, "ssq3_%d" % xi, "BC1", "Y3_%d" % xi], writes=["Y3_%d" % xi])
            dma("sp", ds_o3[xi], lambda e: e.dma_start(out=DA(out_d, row0 * D, [[D, 128], [1, D]]), in_=Y[:, :]), reads=["Y3_%d" % xi], writes=["OUTd"])

        def p3_wo(s):
            t0 = s * 512
            si = s % 2
            MG, mgk = MGs[si], "MG%d" % si
            for tt in range(4):
                xi = ctr_x[0] % 2
                ctr_x[0] += 1
                X, Y = X3[xi], Y3[xi]
                dma("sp", ds_x3[xi], lambda e, X=X, tt=tt: e.dma_start(out=X[:, :], in_=DA(x_d, (t0 + 128 * tt) * D, [[D, 128], [1, D]])), writes=["X3_%d" % xi])
                for hf in range(2):
                    b = 6 + ctr_w[0] % 2
                    ctr_w[0] += 1
                    for kc in range(8):
                        op("pe", lambda e, b=b, kc=kc, tt=tt, hf=hf: e.matmul(PS[b][:, :], MG[:, kc, tt * 128:(tt + 1) * 128], WOs[:, kc, hf * 512:(hf + 1) * 512], start=(kc == 0), stop=(kc == 7)),
                           reads=[mgk, "WOs"], writes=["ps%d" % b], inc=(kc == 7))
                    op("dve", lambda e, b=b, hf=hf, Y=Y: e.tensor_tensor(Y[:, hf * 512:(hf + 1) * 512], PS[b][:, :], BC0[:, hf * 512:(hf + 1) * 512], ALU.mult), reads=["ps%d" % b, "BC0"], writes=["Y3_%d" % xi])
                op("pool", lambda e, X=X, Y=Y: e.tensor_tensor(X[:, :], X[:, :], Y[:, :], ALU.add), reads=["X3_%d" % xi, "Y3_%d" % xi], writes=["X3_%d" % xi])
                op("act", lambda e, X=X, Y=Y, xi=xi: e.activation(Y[:, :], X[:, :], AF.Square, accum_out=ssq3[:, xi:xi + 1]), reads=["X3_%d" % xi, "Y3_%d" % xi], writes=["Y3_%d" % xi, "ssq3_%d" % xi])
                op("pool", lambda e, xi=xi: e.tensor_scalar(ssq3[:, xi:xi + 1], ssq3[:, xi:xi + 1], 1.0 / D, 1e-6, ALU.mult, ALU.add), reads=["ssq3_%d" % xi], writes=["ssq3_%d" % xi])
                op("pool", lambda e, xi=xi: e.tensor_tensor(ssq3[:, xi:xi + 1], ssq3[:, xi:xi + 1], mhalf[:, 0:1], ALU.pow), reads=["ssq3_%d" % xi, "mhalf"], writes=["ssq3_%d" % xi])
                if pendB:
                    p3_B(*pendB.pop(0))
                pendB.append((xi, X, Y, t0 + 128 * tt))

        def p3_loads(s):
            t0 = s * 512
            si = s % 2
            ik = "in3_%d" % si
            for (dst_, src_, dk) in ((hT3[si], ht_d, "HTd"), (AT3[si], at_d, "ATd"), (RT3[si], rt_d, "RTd")):
                dma("sp", ds_in3[si], lambda e, dst_=dst_, src_=src_: e.dma_start(out=dst_[:, :, :], in_=DA(src_, t0, [[S, 128], [128 * S, 8], [1, 512]])), reads=[dk], writes=[ik])

        p3_loads(0)
        if NSC > 1:
            p3_loads(1)
        p3_ft(0)
        for s in range(NSC):
            if s + 1 < NSC:
                p3_ft(s + 1)
            if s + 2 < NSC:
                p3_loads(s + 2)
            p3_wo(s)
        while pendB:
            p3_B(*pendB.pop(0))
        print('P3 arena hi', A.pos, 'of', ARENA_WORDS)
        assert A.pos <= W3TOP
        SCH.final_wait("sp")
        with nc.Block() as block:
            SCH.emit(block)
    return nc


def _layouts(w_in, conv_w, conv_b, w_a, b_a, w_x, b_x, lam, b_gate, b_mod, g_norm):
    wi = w_in
    x_rnn, z_rnn, q, k, v, z_attn, g_r, g_a = [wi[:, i * 1024:(i + 1) * 1024] for i in range(8)]
    WA = np.ascontiguousarray(np.concatenate([x_rnn, z_rnn], axis=1))

    def qk_cols(w):
        hd = w.reshape(1024, 8, 128)
        rope = hd[:, :, 0:32].reshape(1024, 256)
        partner = np.concatenate([hd[:, :, 16:32], hd[:, :, 0:16]], axis=2).reshape(1024, 256)
        pas = hd[:, :, 32:128].reshape(1024, 768)
        return [rope, partner, pas]
    WB = np.ascontiguousarray(np.concatenate(qk_cols(q) + qk_cols(k) + [z_attn, v], axis=1))
    WG = np.ascontiguousarray(np.concatenate([g_r, g_a], axis=1))
    sv = np.zeros((128, NSV), np.float32)
    sv[:, SV_BMOD:SV_BMOD + 24] = b_mod.reshape(24, 128).T
    sv[:, SV_GN:SV_GN + 8] = g_norm.reshape(8, 128).T
    sv[:, SV_CW:SV_CW + 32] = conv_w.reshape(4, 8, 128).transpose(2, 1, 0).reshape(128, 32)
    sv[:, SV_CB:SV_CB + 8] = conv_b.reshape(8, 128).T
    sv[:, SV_BA:SV_BA + 8] = b_a.T
    sv[:, SV_BX:SV_BX + 8] = b_x.T
    sv[:, SV_LAM:SV_LAM + 8] = lam.reshape(8, 128).T
    sv[:, SV_BG:SV_BG + 16] = b_gate.reshape(16, 128).T
    wax = np.ascontiguousarray(np.concatenate([w_a, w_x], axis=0).transpose(1, 0, 2).reshape(128, 16 * 128))
    return WA, WB, WG, sv, wax


def _consts():
    p = np.arange(128)
    freq = (500000.0 ** (-(np.arange(0, 32, 2, dtype=np.float32)) / 32.0)).astype(np.float32)
    fcol = freq[p % 16]
    sign = np.where((p % 32) < 16, -1.0, 1.0).astype(np.float32)
    k = np.arange(128)[:, None]
    col = np.arange(512)[None, :]
    m = np.zeros((128, 12, 512), np.float32)
    m[:, 0] = (col % 128) >= k
    m[:, 1] = (col % 128) <= k
    m[:, 2] = (col // 4) >= k
    m[:, 3] = (col // 4) <= k
    for sp in range(4):
        m[:, 4 + sp] = (32 * sp + col // 16) >= k
        m[:, 8 + sp] = (32 * sp + col // 16) <= k
    m = (m - 1.0) * 30000.0
    return fcol, sign, np.ascontiguousarray(m.reshape(128, 12 * 512)), np.eye(128, dtype=np.float32)


def make_in_maps(S, nb, x, c, positions, g_norm, w_mod, b_mod, w_in, b_gate, conv_w, conv_b, w_a, b_a, w_x, b_x, lam,
                 w_out_rnn, w_out_attn, w_o, g_final):
    f = lambda a: np.ascontiguousarray(np.asarray(a), dtype=np.float32)
    WA, WB, WG, sv, wax = _layouts(f(w_in[0]), f(conv_w[0]), f(conv_b[0]), f(w_a[0]), f(b_a[0]), f(w_x[0]), f(b_x[0]), f(lam[0]),
                                   f(b_gate[0]), f(b_mod[0]), f(g_norm[0]))
    fcol, sign, masks, ident = _consts()
    sv[:, SV_FREQ] = fcol
    sv[:, SV_SIGN] = sign
    shared = {"w_mod": f(w_mod[0]), "sv": sv, "WA": WA, "WB": WB, "WG": WG, "wor": f(w_out_rnn[0]), "woa": f(w_out_attn[0]),
              "wo": f(w_o[0]), "wax": wax, "gfin": f(g_final).reshape(1, D), "ident": ident, "masks": masks}
    x = np.asarray(x)
    c = np.asarray(c)
    positions = np.asarray(positions)
    maps = []
    for b in range(nb):
        m = dict(shared)
        m["x"] = np.ascontiguousarray(x[b], dtype=np.float32)
        m["pos"] = np.ascontiguousarray(positions[b].reshape(1, S), dtype=np.int32)
        m["cT"] = np.ascontiguousarray(c[b].astype(np.float32).reshape(8, 128).T)
        maps.append(m)
    return maps


def kernel(**inputs):
    x = np.asarray(inputs["x"])
    B, S, _ = x.shape
    nc = build(S)
    maps = make_in_maps(S, B, **inputs)
    res = run_bass_kernel_spmd(nc, maps, core_ids=list(range(B)))
    return np.stack([np.asarray(r["out"]).reshape(S, D) for r in res.results], axis=0).astype(np.float32)
```
